# Optimizing a Trainium2 kernel written in Bass

```python
import math
import jax
import jax.numpy as jnp
from jax import lax
import numpy as np

D_MODEL = 1024
BATCH = 4
SEQ = 8192
DEPTH = 2

N_A_LAYERS = DEPTH // 2
N_B_LAYERS = DEPTH - N_A_LAYERS
PLE_DIM = 256
D_FF = 2816
EPS = 1e-6
N_SUBNORMS = 8

MLSTM_HEADS = 4
MLSTM_DV = D_MODEL // MLSTM_HEADS
MLSTM_DK = MLSTM_DV // 2
MLSTM_CHUNK = 128
MLSTM_QK_W = MLSTM_HEADS * MLSTM_DK
MLSTM_IN = 2 * MLSTM_QK_W + 2 * D_MODEL + 2 * MLSTM_HEADS

DIFF_HEAD_DIM = 64
DIFF_HEADS = D_MODEL // (2 * DIFF_HEAD_DIM)
Q_BLOCK = 128

kernel_name = "yoco_mlstm_diffattn_macaron_block"


def rms_norm(x, g):
    xf = x.astype(jnp.float32)
    y = xf * lax.rsqrt(jnp.mean(xf * xf, axis=-1, keepdims=True) + EPS)
    return (y * g.astype(jnp.float32)).astype(x.dtype)


def swiglu(x, w_in, w_out):
    gate, up = jnp.split(x @ w_in, 2, axis=-1)
    return (jax.nn.silu(gate) * up) @ w_out


def mlstm_chunkwise(q, k, v, log_i, log_f):
    B, H, S, DK = q.shape
    DV = v.shape[-1]
    L = MLSTM_CHUNK
    NC = S // L
    q, k, v = (t.astype(jnp.float32) for t in (q, k, v))

    def chunks(t):
        return jnp.moveaxis(t.reshape(B, H, NC, L, *t.shape[3:]), 2, 0)

    xs = tuple(chunks(t) for t in (q, k, v, log_i, log_f))
    causal = jnp.tril(jnp.ones((L, L), dtype=bool))

    def step(carry, inp):
        C, n, m = carry
        qj, kj, vj, li, lf = inp
        b = jnp.cumsum(lf, axis=-1)
        d = b[..., :, None] - b[..., None, :] + li[..., None, :]
        d = jnp.where(causal, d, -jnp.inf)
        inter = b + m[..., None]
        m_t = jnp.maximum(inter, jnp.max(d, axis=-1))
        w = jnp.exp(d - m_t[..., None])
        s_inter = jnp.exp(inter - m_t)
        qk = jnp.einsum('bhtd,bhsd->bhts', qj, kj) * w
        num = (jnp.einsum('bhts,bhsv->bhtv', qk, vj)
               + s_inter[..., None] * jnp.einsum('bhtd,bhdv->bhtv', qj, C))
        den = jnp.sum(qk, axis=-1) + s_inter * jnp.einsum('bhtd,bhd->bht', qj, n)
        h = num / jnp.maximum(jnp.abs(den), jnp.exp(-m_t))[..., None]
        bL = b[..., -1]
        g = bL[..., None] - b + li
        m_new = jnp.maximum(bL + m, jnp.max(g, axis=-1))
        sc = jnp.exp(g - m_new[..., None])
        decay = jnp.exp(bL + m - m_new)
        C_new = decay[..., None, None] * C + jnp.einsum('bhs,bhsd,bhsv->bhdv', sc, kj, vj)
        n_new = decay[..., None] * n + jnp.einsum('bhs,bhsd->bhd', sc, kj)
        return (C_new, n_new, m_new), h

    init = (jnp.zeros((B, H, DK, DV), jnp.float32),
            jnp.zeros((B, H, DK), jnp.float32),
            jnp.full((B, H), -jnp.inf, jnp.float32))
    _, hs = lax.scan(step, init, xs)
    return jnp.moveaxis(hs, 0, 2).reshape(B, H, S, DV)


def mlstm_mixer(x, w_in, b_gates, head_norm, w_out):
    B, S, _ = x.shape
    proj = x @ w_in
    q, k, v, o, gates = jnp.split(
        proj, [MLSTM_QK_W, 2 * MLSTM_QK_W, 2 * MLSTM_QK_W + D_MODEL,
               2 * MLSTM_QK_W + 2 * D_MODEL], axis=-1)

    def heads(t, d):
        return t.reshape(B, S, MLSTM_HEADS, d).transpose(0, 2, 1, 3)

    q = heads(q, MLSTM_DK)
    k = heads(k, MLSTM_DK) * (MLSTM_DK ** -0.5)
    v = heads(v, MLSTM_DV)
    gates = gates.astype(jnp.float32) + b_gates.astype(jnp.float32)
    log_i = gates[..., :MLSTM_HEADS].transpose(0, 2, 1)
    log_f = jax.nn.log_sigmoid(gates[..., MLSTM_HEADS:]).transpose(0, 2, 1)
    h = mlstm_chunkwise(q, k, v, log_i, log_f)
    h = rms_norm(h, head_norm[:, None, :])
    h = h.transpose(0, 2, 1, 3).reshape(B, S, D_MODEL).astype(x.dtype)
    return (jax.nn.sigmoid(o) * h) @ w_out


def shared_kv(x, kv_norm, w_kv):
    B, S, _ = x.shape
    kv = rms_norm(x, kv_norm) @ w_kv
    k, v = jnp.split(kv, 2, axis=-1)
    k = k.reshape(B, S, 2 * DIFF_HEADS, DIFF_HEAD_DIM).transpose(0, 2, 1, 3)
    v = v.reshape(B, S, DIFF_HEADS, 2 * DIFF_HEAD_DIM).transpose(0, 2, 1, 3)
    return k, v


def diff_attention(x, k_sh, v_sh, w_q, lam_vecs, subln, w_out, lam_init):
    B, S, _ = x.shape
    H, DH = DIFF_HEADS, DIFF_HEAD_DIM
    q = (x @ w_q).reshape(B, S, 2 * H, DH).transpose(0, 2, 1, 3) * (DH ** -0.5)
    lv = lam_vecs.astype(jnp.float32)
    lam = jnp.exp(jnp.sum(lv[0] * lv[1])) - jnp.exp(jnp.sum(lv[2] * lv[3])) + lam_init
    nb = S // Q_BLOCK
    qb = jnp.moveaxis(q.reshape(B, 2 * H, nb, Q_BLOCK, DH), 2, 0)
    key_pos = jnp.arange(S)

    def block(args):
        qi, start = args
        s = jnp.einsum('bhqd,bhkd->bhqk', qi, k_sh).astype(jnp.float32)
        qpos = start + jnp.arange(Q_BLOCK)
        s = jnp.where(key_pos[None, :] <= qpos[:, None], s, -jnp.inf)
        a = jax.nn.softmax(s, axis=-1).reshape(B, H, 2, Q_BLOCK, S)
        diff = a[:, :, 0] - lam * a[:, :, 1]
        return jnp.einsum('bhqk,bhkv->bhqv', diff.astype(v_sh.dtype), v_sh)

    o = lax.map(block, (qb, jnp.arange(nb) * Q_BLOCK))
    o = o.transpose(1, 2, 0, 3, 4).reshape(B, H, S, 2 * DH)
    o = rms_norm(o, subln) * (1.0 - lam_init)
    o = o.transpose(0, 2, 1, 3).reshape(B, S, D_MODEL).astype(x.dtype)
    return o @ w_out


def setup_inputs(seed: int = 0) -> dict:
    key = jax.random.key(seed)
    ks = jax.random.split(key, 20)
    f32 = jnp.float32

    def nrm(k, shape, fan_in):
        return jax.random.normal(k, shape, f32) * (fan_in ** -0.5)

    def gain(k, shape):
        return 1.0 + 0.05 * jax.random.normal(k, shape, f32)

    f_bias = jnp.linspace(3.0, 6.0, MLSTM_HEADS, dtype=f32)
    b_gates = jnp.concatenate(
        [0.1 * jax.random.normal(ks[8], (N_A_LAYERS, MLSTM_HEADS), f32),
         f_bias[None, :] + 0.1 * jax.random.normal(ks[9], (N_A_LAYERS, MLSTM_HEADS), f32)], axis=-1)
    return {
        "x": jax.random.normal(ks[0], (BATCH, SEQ, D_MODEL), f32),
        "p": jax.random.normal(ks[1], (DEPTH, BATCH, SEQ, PLE_DIM), f32),
        "norm_g": gain(ks[2], (DEPTH, N_SUBNORMS, D_MODEL)),
        "w_ffn_in": nrm(ks[3], (DEPTH, 2, D_MODEL, 2 * D_FF), D_MODEL),
        "w_ffn_out": nrm(ks[4], (DEPTH, 2, D_FF, D_MODEL), D_FF),
        "w_ple_proj": nrm(ks[5], (DEPTH, PLE_DIM, D_MODEL), PLE_DIM),
        "w_ple_gate": nrm(ks[6], (DEPTH, D_MODEL, D_MODEL), D_MODEL),
        "mlstm_w_in": nrm(ks[7], (N_A_LAYERS, D_MODEL, MLSTM_IN), D_MODEL),
        "mlstm_b_gates": b_gates,
        "mlstm_head_norm": gain(ks[10], (N_A_LAYERS, MLSTM_HEADS, MLSTM_DV)),
        "mlstm_w_out": nrm(ks[11], (N_A_LAYERS, D_MODEL, D_MODEL), D_MODEL),
        "kv_norm": gain(ks[12], (D_MODEL,)),
        "w_kv": nrm(ks[13], (D_MODEL, 2 * D_MODEL), D_MODEL),
        "diff_w_q": nrm(ks[14], (N_B_LAYERS, D_MODEL, D_MODEL), D_MODEL),
        "diff_lambda": 0.1 * jax.random.normal(ks[15], (N_B_LAYERS, 4, DIFF_HEAD_DIM), f32),
        "diff_subln": gain(ks[16], (N_B_LAYERS, 2 * DIFF_HEAD_DIM)),
        "diff_w_out": nrm(ks[17], (N_B_LAYERS, D_MODEL, D_MODEL), D_MODEL),
    }


def reference(x, p, norm_g, w_ffn_in, w_ffn_out, w_ple_proj, w_ple_gate,
              mlstm_w_in, mlstm_b_gates, mlstm_head_norm, mlstm_w_out,
              kv_norm, w_kv, diff_w_q, diff_lambda, diff_subln, diff_w_out):
    k_sh = None
    v_sh = None
    for layer in range(DEPTH):
        g = norm_g[layer]
        h = swiglu(rms_norm(x, g[0]), w_ffn_in[layer, 0], w_ffn_out[layer, 0])
        x = x + 0.5 * rms_norm(h, g[1])
        h = rms_norm(x, g[2])
        if layer < N_A_LAYERS:
            h = mlstm_mixer(h, mlstm_w_in[layer], mlstm_b_gates[layer],
                            mlstm_head_norm[layer], mlstm_w_out[layer])
        else:
            j = layer - N_A_LAYERS
            lam_init = 0.8 - 0.6 * math.exp(-0.3 * layer)
            h = diff_attention(h, k_sh, v_sh, diff_w_q[j], diff_lambda[j],
                               diff_subln[j], diff_w_out[j], lam_init)
        x = x + rms_norm(h, g[3])
        h = swiglu(rms_norm(x, g[4]), w_ffn_in[layer, 1], w_ffn_out[layer, 1])
        x = x + 0.5 * rms_norm(h, g[5])
        gate = jax.nn.sigmoid(rms_norm(x, g[6]) @ w_ple_gate[layer])
        e = p[layer].astype(x.dtype) @ w_ple_proj[layer]
        x = x + rms_norm(e * gate, g[7])
        if layer == N_A_LAYERS - 1:
            k_sh, v_sh = shared_kv(x, kv_norm, w_kv)
    return x
```

```python
import numpy as np
from contextlib import ExitStack
import concourse.bass as bass
import concourse.mybir as mybir
from concourse.bass_utils import run_bass_kernel_spmd

F32, BF16 = mybir.dt.float32, mybir.dt.bfloat16
AF = mybir.ActivationFunctionType
ALU = mybir.AluOpType
AX = mybir.AxisListType

NCORES = 8
D = 1024
KC = 8
DFF = 2816
FC = 22
TOK = 4096
TT = 512
EPS = 1e-6

ENGS = ("pe", "act", "dve", "pool", "sp")
SEM_LIMIT = 30000


class Res:
    __slots__ = ("name", "w", "rs")

    def __init__(self, name):
        self.name = name
        self.w = None
        self.rs = []


class Op:
    __slots__ = ("eng", "fn", "deps", "sig", "sem", "val", "dma", "ndma", "name", "inc")


class Prog:
    def __init__(self, nc):
        self.nc = nc
        self.q = {e: [] for e in ENGS}
        self.dma_cnt = {}
        self.stack = ExitStack()
        self.sems = {}
        self.nsem = 0
        self.last_dma = {}

    def sbuf(self, name, shape, dt):
        return self.stack.enter_context(self.nc.sbuf_tensor(name, list(shape), dt))

    def psum(self, name, shape=(128, 512), dt=F32):
        return self.stack.enter_context(self.nc.psum_tensor(name, list(shape), dt))

    def sem(self, key):
        if key not in self.sems:
            self.sems[key] = self.stack.enter_context(self.nc.semaphore("s_%d" % self.nsem))
            self.nsem += 1
        return self.sems[key]

    def add(self, eng, fn, r=(), w=(), dma=None, ndma=1, name="", inc=16):
        op = Op()
        op.eng, op.fn, op.dma, op.ndma, op.name = eng, fn, dma, ndma, name
        op.inc = inc
        op.sig = False
        op.sem = None
        op.val = 0
        raw, oth = set(), set()
        for x in r:
            if x.w is not None:
                raw.add(x.w)
        for x in w:
            if x.w is not None:
                oth.add(x.w)
            oth.update(x.rs)
        deps = []
        for d in raw | oth:
            if d is op:
                continue
            if d.dma is None and dma is None and d.eng == eng:
                if eng == "pe":
                    continue
            deps.append(d)
        if dma is not None:
            prev = self.last_dma.get(dma)
            if prev is not None and prev not in deps:
                deps.append(prev)
            self.last_dma[dma] = op
        op.deps = deps
        for d in deps:
            d.sig = True
        for x in r:
            x.rs.append(op)
        for x in w:
            x.w = op
            x.rs = []
        self.q[eng].append(op)
        return op

    def barrier(self):
        lasts = []
        for e in ENGS:
            for op in reversed(self.q[e]):
                if op.fn is not None:
                    lasts.append(op)
                    break
        for d in self.last_dma.values():
            if d not in lasts:
                lasts.append(d)
        for e in ENGS:
            op = Op()
            op.eng, op.fn, op.dma, op.ndma, op.name = e, None, None, 1, "barrier"
            op.inc = 16
            op.sig, op.sem, op.val = False, None, 0
            op.deps = list(lasts)
            for d in op.deps:
                d.sig = True
            self.q[e].append(op)

    def finalize(self):
        for e in ENGS:
            cnt = 0
            epoch = 0
            for op in self.q[e]:
                if op.dma is not None:
                    c = self.dma_cnt.get(op.dma, 0) + op.ndma
                    self.dma_cnt[op.dma] = c
                    op.sem = self.sem(("dma", op.dma))
                    op.val = op.inc * c
                elif op.sig:
                    if cnt >= SEM_LIMIT:
                        epoch += 1
                        cnt = 0
                    cnt += 1
                    op.sem = self.sem((e, epoch))
                    op.val = cnt

    def emit(self):
        self.finalize()
        nc = self.nc
        prog = self

        def run(ename, eng):
            waited = {}
            for op in prog.q[ename]:
                for d in op.deps:
                    k = id(d.sem)
                    if waited.get(k, 0) >= d.val:
                        continue
                    eng.wait_ge(d.sem, d.val)
                    waited[k] = d.val
                if op.fn is None:
                    continue
                ins = op.fn(eng)
                if op.dma is not None:
                    if not isinstance(ins, (list, tuple)):
                        ins = [ins]
                    assert len(ins) == op.ndma, (op.name, len(ins), op.ndma)
                    for i in ins:
                        i.then_inc(op.sem, op.inc)
                elif op.sem is not None:
                    ins.then_inc(op.sem, 1)

        with nc.Block() as block:
            @block.tensor
            def _(e):
                run("pe", e)

            @block.scalar
            def _(e):
                run("act", e)

            @block.vector
            def _(e):
                run("dve", e)

            @block.gpsimd
            def _(e):
                run("pool", e)

            @block.sync
            def _(e):
                run("sp", e)

    def close(self):
        self.stack.close()


ROW = 2048


class WPack:
    def __init__(self):
        self.parts = []
        self.off = {}
        self.n = 0

    def put(self, key, arr):
        if WKEYS is not None and key not in WKEYS:
            return
        a = np.ascontiguousarray(arr, dtype=np.float32).reshape(-1)
        pad = (-a.size) % ROW
        self.off[key] = (self.n, a.size)
        self.parts.append(a)
        if pad:
            self.parts.append(np.zeros(pad, np.float32))
        self.n += a.size + pad

    def flat(self):
        return np.concatenate(self.parts)


LVL = 99
WKEYS = None


def w_layout_sizes():
    sizes = []
    for l in range(2):
        for f in range(2):
            sizes.append((("w1", l, f), FC * 128 * KC * 256))
            sizes.append((("w2", l, f), KC * 128 * FC * 128))
        sizes.append((("pg", l), KC * 128 * KC * 128))
        sizes.append((("pp", l), KC * 128 * 2 * 128))
        if l == 0:
            sizes.append((("mq",), 4 * 128 * KC * 128))
            sizes.append((("mk",), 4 * 128 * KC * 128))
            sizes.append((("mt",), 5 * 128 * KC * 512))
            sizes.append((("mg",), 128 * KC * 8))
            sizes.append((("mo",), KC * 128 * KC * 128))
            sizes.append((("kk",), KC * 128 * KC * 128))
            sizes.append((("kv",), 2 * 128 * KC * 512))
        else:
            sizes.append((("dq",), KC * 128 * KC * 128))
            sizes.append((("do",), KC * 128 * KC * 128))
    out = {}
    off = 0
    order = []
    if WKEYS is not None:
        sizes = [(k, n) for k, n in sizes if k in WKEYS]
    for k, n in sizes:
        n_pad = n + ((-n) % ROW)
        out[k] = (off, n)
        order.append((k, off, n_pad))
        off += n_pad
    return out, order, off


def pack_weights(inp):
    wp = WPack()
    for l in range(2):
        for f in range(2):
            w_in = inp["w_ffn_in"][l, f]
            w1 = w_in.reshape(KC, 128, 2, FC, 128).transpose(3, 1, 0, 2, 4)
            wp.put(("w1", l, f), w1)
            w_out = inp["w_ffn_out"][l, f]
            w2 = w_out.reshape(FC, 128, KC, 128).transpose(2, 1, 0, 3)
            wp.put(("w2", l, f), w2)
        wg = inp["w_ple_gate"][l]
        wp.put(("pg", l), wg.reshape(KC, 128, KC, 128).transpose(2, 1, 0, 3))
        wq = inp["w_ple_proj"][l]
        wp.put(("pp", l), wq.reshape(2, 128, KC, 128).transpose(2, 1, 0, 3))
        fm = lambda w, n: w.reshape(KC, 128, n, 128).transpose(2, 1, 0, 3)
        tm = lambda w, n: w.reshape(KC, 128, n, 512).transpose(2, 1, 0, 3)
        if l == 0:
            wi = inp["mlstm_w_in"][0]
            wp.put(("mq",), fm(wi[:, 0:512], 4))
            wp.put(("mk",), fm(wi[:, 512:1024], 4))
            wp.put(("mt",), tm(wi[:, 512:3072], 5))
            wp.put(("mg",), wi[:, 3072:3080].reshape(KC, 128, 8).transpose(1, 0, 2))
            wp.put(("mo",), fm(inp["mlstm_w_out"][0], KC))
            wp.put(("kk",), fm(inp["w_kv"][:, 0:1024], KC))
            wp.put(("kv",), tm(inp["w_kv"][:, 1024:2048], 2))
        else:
            wp.put(("dq",), fm(inp["diff_w_q"][0], KC))
            wp.put(("do",), fm(inp["diff_w_out"][0], KC))
    offs, order, total = w_layout_sizes()
    assert total == wp.n, (total, wp.n)
    for k in offs:
        assert offs[k] == wp.off[k], (k, offs[k], wp.off[k])
    return wp.flat()


def gcols_host(inp):
    cols = []
    ng = inp["norm_g"]
    cols.append(ng.reshape(2, 8, KC, 128).transpose(3, 0, 1, 2).reshape(128, 2 * 8 * KC))
    cols.append(inp["kv_norm"].reshape(KC, 128).T)
    return np.ascontiguousarray(np.concatenate(cols, axis=1), dtype=np.float32)


NG = 2 * 8 * KC + KC


def gcol(l, i, kc):
    return (l * 8 + i) * KC + kc


NKV = 2048 * 512
NST = 4 * 257 + 4
DKS = 128 ** -0.5
NEG = -30000.0
NOCC = False
C_ID, C_U, C_ONE, C_MC4, C_MT4, C_MD = 0, 128, 256, 384, 896, 1408
NCST = 1408 + 4 * 512
P_BG, P_HN, P_LAM, P_SUBLN, P_SEL, P_BIASA = 0, 8, 1032, 1288, 1289, 1290
NPRM = 1291


def consts_host():
    p = np.arange(128)[:, None]
    j = np.arange(128)[None, :]
    ident = (p == j).astype(np.float32)
    U = (p <= j).astype(np.float32)
    ones = np.ones((128, 128), np.float32)
    maskC = np.where(j <= p, 0.0, -1e30).astype(np.float32)
    maskT = np.where(j >= p, 0.0, 1e30).astype(np.float32)
    i = np.arange(512)[None, :]
    md = [np.where(i >= 128 * j4 + p, 0.0, NEG).astype(np.float32) for j4 in range(4)]
    return np.ascontiguousarray(np.concatenate(
        [ident, U, ones, np.tile(maskC, (1, 4)), np.tile(maskT, (1, 4))] + md, axis=1))


def prm_host(inp, core):
    a = np.zeros((128, NPRM), np.float32)
    a[:, P_BG:P_BG + 8] = inp["mlstm_b_gates"].reshape(1, 8)
    a[:, P_HN:P_HN + 1024] = inp["mlstm_head_norm"].reshape(1, 1024)
    a[:, P_LAM:P_LAM + 256] = inp["diff_lambda"].reshape(1, 256)
    a[:, P_SUBLN] = inp["diff_subln"].reshape(128)
    a[:, P_SEL] = float(core % 2)
    a[:, P_BIASA] = 0.0 if core % 2 == 1 else NEG
    return a


class Ctx:
    pass


def build(stages, NT=TOK // TT, phases=None):
    nc = bass.Bass("TRN2", target_bir_lowering=False)
    P = Prog(nc)
    c = Ctx()
    offs, order, wtotal = w_layout_sizes()
    import math
    LAM_INIT = 0.8 - 0.6 * math.exp(-0.3 * 1)

    xT_d = nc.dram_tensor("xT", [128, KC, TOK], F32, kind="ExternalInput").ap()
    pT_d = nc.dram_tensor("pT", [2, 128, 2, TOK], F32, kind="ExternalInput").ap()
    gc_d = nc.dram_tensor("gcols", [128, NG], F32, kind="ExternalInput").ap()
    cst_d = nc.dram_tensor("cst", [128, NCST], F32, kind="ExternalInput").ap()
    prm_d = nc.dram_tensor("prm", [128, NPRM], F32, kind="ExternalInput").ap()
    wall_d = nc.dram_tensor("wall", [wtotal // ROW, ROW], F32, kind="ExternalInput").ap()
    out_d = nc.dram_tensor("outT", [128, KC, TOK], F32, kind="ExternalOutput").ap()
    wbf_d = nc.dram_tensor("wbf", [wtotal // ROW, ROW], BF16).ap()
    wbf_flat = wbf_d.rearrange("a b -> (a b)")
    xs_d = nc.dram_tensor("xs", [128, KC, TOK], F32).ap()
    st_own = nc.dram_tensor("st_own", [128, NST], F32)
    st_all = nc.dram_tensor("st_all", [256, NST], F32)
    kv_own = [nc.dram_tensor("kv_own%d" % t, [2048, 512], BF16) for t in range(8)]
    kv_all = [nc.dram_tensor("kv_all%d" % t, [4096, 512], BF16) for t in range(8)]
    PAIRS = [[0, 1], [2, 3], [4, 5], [6, 7]]

    def rwb(key):
        return r_wparts[key] if key in r_wparts else [r_wbf[key]]

    def wview(key, pattern, **kw):
        off, n = offs[key]
        return wbf_flat[off:off + n].rearrange(pattern, **kw)

    c.xTs = [P.sbuf("xT_sb%d" % i, [128, KC, TT], F32) for i in range(2)]
    c.xT = c.xTs[0]
    c.sq = P.sbuf("sq", [128, KC, TT], BF16)
    c.rstd = P.sbuf("rstd", [128, TT], F32)
    c.xn = P.sbuf("xn", [128, KC, TT], BF16)
    NW1, NW2, NWT = 5, 3, 2
    c.w1 = [P.sbuf("w1_%d" % i, [128, KC, 256], BF16) for i in range(NW1)]
    c.w2 = [P.sbuf("w2_%d" % i, [128, FC, 128], BF16) for i in range(NW2)]
    c.wt = [P.sbuf("wt_%d" % i, [128, KC, 512], BF16) for i in range(NWT)]
    c.sg = [P.sbuf("sg_%d" % i, [128, TT], BF16) for i in range(2)]
    c.aT = P.sbuf("aT", [128, FC, TT], BF16)
    c.y = P.sbuf("y", [128, KC, TT], F32)
    c.tmp = [P.sbuf("tmp_%d" % i, [128, TT], F32) for i in range(2)]
    c.gc = P.sbuf("gc", [128, NG], F32)
    c.hgc = P.sbuf("hgc", [128, NG], F32)
    c.ones = P.sbuf("ones", [128, 128], BF16)
    c.one1 = P.sbuf("one1", [128, 128], BF16)
    c.o128 = P.sbuf("o128", [128, 128], BF16)
    c.identb = P.sbuf("identb", [128, 128], BF16)
    c.epsc = P.sbuf("epsc", [128, 1], F32)
    c.pT = P.sbuf("pT_sb", [128, 2, TT], F32)
    c.pTb = P.sbuf("pTb", [128, 2, TT], BF16)
    c.cst = P.sbuf("cst_sb", [128, C_MD], F32)
    c.maskb = P.sbuf("maskb", [128, 4 * 512], BF16)
    c.prm = P.sbuf("prm_sb", [128, NPRM], F32)
    c.mg = P.sbuf("mg", [128, KC, 8], BF16)
    A = P.sbuf("arena", [128, 12288], BF16)
    c.gt = P.sbuf("gt", [128, 4, 8], F32)
    c.sm = P.sbuf("sm", [128, 32, 4], F32)
    c.dg = P.sbuf("dg", [128, 4, 128], F32)
    c.e1 = P.sbuf("e1", [128, 4, 128], F32)
    c.wT = P.sbuf("wT", [128, 4, 128], F32)
    c.sib = P.sbuf("sib", [128, 4, 128], F32)
    c.junk = P.sbuf("junk", [128, 256], F32)
    c.v1 = A[:, 0:4112].rearrange("p (c h v) -> p c h v", c=4, h=4, v=257)
    c.Cnb = A[:, 4112:5140].rearrange("p (h v) -> p h v", h=4, v=257)
    c.qs = A[:, 5140:5652].rearrange("p (h t) -> p h t", h=4, t=128)
    c.qkw = A[:, 5652:6164].rearrange("p (h t) -> p h t", h=4, t=128)
    c.ksc = A[:, 6164:6676].rearrange("p (h t) -> p h t", h=4, t=128)
    c.og = A[:, 6676:7700]
    c.Cn = A[:, 7700:7700 + 2 * NST].bitcast(F32)
    c.hh = A[:, 9764:11812].bitcast(F32).rearrange("p (h v) -> p h v", h=4, v=256)
    c.kb = [A[:, 2048 * i:2048 * (i + 1)] for i in range(2)]
    c.vb = [A[:, 4096 + 2048 * i:4096 + 2048 * (i + 1)].rearrange("p (k d) -> p k d", k=16, d=128) for i in range(2)]
    c.ptpair = [A[:, 8192 + 1024 * i:8192 + 1024 * (i + 1)] for i in range(2)]
    c.pt = [[c.ptpair[i][:, 512 * m:512 * (m + 1)] for i in range(2)] for m in range(2)]
    c.msall = A[:, 10240:12288].bitcast(F32)
    c.ms = [c.msall[:, 512 * m:512 * (m + 1)] for m in range(2)]
    c.lamc = P.sbuf("lamc", [128, 4], F32)
    c.pairs = [P.psum("bankpair%d" % i, (128, 1024)) for i in range(4)]
    c.bank = []
    for i in range(4):
        c.bank += [c.pairs[i][:, 0:512], c.pairs[i][:, 512:1024]]
    c.pg, c.pu, c.py, c.pss = c.bank[0:2], c.bank[2:4], c.bank[4:6], c.bank[6]

    R = lambda n: Res(n)
    c.r_xs2 = [[R("x%d_%d" % (i, m)) for m in range(KC)] for i in range(2)]
    c.r_x = c.r_xs2[0]
    c.r_sq = [R("sq%d" % k) for k in range(KC)]
    c.r_rstd = R("rstd")
    c.r_xn = [R("xn%d" % k) for k in range(KC)]
    c.r_w1 = [R("w1") for _ in range(NW1)]
    c.r_w2 = [R("w2") for _ in range(NW2)]
    c.r_wt = [R("wt") for _ in range(NWT)]
    c.r_sg = [R("sg") for _ in range(2)]
    c.r_a = [R("a") for _ in range(FC)]
    c.r_y = [R("y") for _ in range(KC)]
    c.r_tmp = [R("tmp") for _ in range(2)]
    c.r_gc, c.r_ones = R("gc"), R("ones")
    c.r_pT, c.r_pTb = R("pT"), R("pTb")
    c.r_bank = [R("bank%d" % i) for i in range(8)]
    c.r_pg, c.r_pu, c.r_py, c.r_pss = c.r_bank[0:2], c.r_bank[2:4], c.r_bank[4:6], c.r_bank[6]
    c.r_out, c.r_xs = R("out"), R("xs")
    c.r_cst, c.r_prm = R("cst"), R("prm")
    c.r_mg, c.r_v1, c.r_gt, c.r_Cn, c.r_Cnb = R("mg"), R("v1"), [R("gt%d" % i) for i in range(4)], R("Cn"), R("Cnb")
    c.r_sm = [R("sm%d" % i) for i in range(32)]
    c.r_dg, c.r_e1, c.r_wT, c.r_sib = R("dg"), R("e1"), R("wT"), R("sib")
    c.r_qs, c.r_qkw, c.r_ksc, c.r_og, c.r_junk = R("qs"), R("qkw"), R("ksc"), R("og"), R("junk")
    c.r_hh = [R("hh%d" % i) for i in range(4)]
    c.r_kb = [R("kb") for _ in range(2)]
    c.r_vb = [R("vb") for _ in range(2)]
    c.r_pt = [[R("pt") for _ in range(2)] for _ in range(2)]
    c.r_ms = [R("ms") for _ in range(2)]
    c.r_lam = R("lam")
    c.r_st_own, c.r_st_all = R("st_own"), R("st_all")
    c.r_kv_own = [R("kvo") for _ in range(8)]
    c.r_kv_all = [R("kva") for _ in range(8)]
    r_wbf = {k: R("wbf") for k in offs}
    ALLK = [("mq",), ("mk",), ("mt",), ("mg",), ("mo",), ("kk",), ("kv",), ("dq",), ("do",)]
    for l in range(2):
        ALLK += [("w1", l, 0), ("w2", l, 0), ("w1", l, 1), ("w2", l, 1), ("pg", l), ("pp", l)]
    for kk in ALLK:
        if kk not in offs:
            offs[kk] = (0, 1 << 30)
            r_wbf[kk] = R("wbf_missing")
    c.w1_i = 0
    c.w2_i = 0
    c.wt_i = 0
    c.kb_i = 0

    def cs(off, n=128):
        return c.cst[:, off:off + n]

    ident, Umat, onesf = cs(C_ID), cs(C_U), cs(C_ONE)

    P.add("sp", lambda e: e.dma_start(out=c.gc[:], in_=gc_d[:, :]), w=[c.r_gc], dma="gc")
    P.add("sp", lambda e: e.dma_start(out=c.cst[:], in_=cst_d[:, 0:C_MD]), w=[c.r_cst], dma="cst")
    ystage = c.y[:, 0:4, :].rearrange("p a t -> p (a t)")
    P.add("sp", lambda e: e.dma_start(out=ystage, in_=cst_d[:, C_MD:C_MD + 2048]), w=c.r_y[0:4], dma="cst")
    P.add("dve", lambda e: e.tensor_copy(c.maskb[:], ystage), r=c.r_y[0:4], w=[c.r_cst])
    P.add("sp", lambda e: e.dma_start(out=c.prm[:], in_=prm_d[:, :]), w=[c.r_prm], dma="prm")
    P.add("dve", lambda e: e.memset(c.ones[:], 1.0 / D), w=[c.r_ones])
    P.add("dve", lambda e: e.memset(c.one1[:], 1.0), w=[c.r_ones])
    P.add("dve", lambda e: e.memset(c.o128[:], 1.0 / 128), w=[c.r_ones])
    P.add("dve", lambda e: e.memset(c.epsc[:], EPS), w=[c.r_ones])
    P.add("dve", lambda e: e.memset(c.v1[:], 1.0), w=[c.r_v1])
    P.add("dve", lambda e: e.tensor_copy(c.identb[:], ident), r=[c.r_cst], w=[c.r_ones])
    P.add("dve", lambda e: e.tensor_scalar(c.hgc[:], c.gc[:], 0.5, None, ALU.mult),
          r=[c.r_gc], w=[c.r_gc])
    r_cast = R("castchain")
    c.w1first = None
    use_order = [("w1", 0, 0), ("w2", 0, 0), ("mg",), ("mt",), ("mq",), ("mk",), ("mo",),
                 ("w1", 0, 1), ("w2", 0, 1), ("pg", 0), ("pp", 0), ("kk",), ("kv",),
                 ("w1", 1, 0), ("w2", 1, 0), ("dq",), ("do",), ("w1", 1, 1), ("w2", 1, 1), ("pg", 1), ("pp", 1)]
    omap = {k: (off, n_pad) for k, off, n_pad in order}
    order2 = [(k, omap[k][0], omap[k][1]) for k in use_order if k in omap]
    assert len(order2) == len(order)
    CROWS = 704
    cast_chunks = []
    for k, off, n_pad in order2:
        r0, r1 = off // ROW, (off + n_pad) // ROW
        if k == ("w1", 0, 0):
            c.w1first = []
        for ra in range(r0, r1, CROWS):
            rb = min(ra + CROWS, r1)
            rp = R("wbf_part")
            if k == ("w1", 0, 0):
                c.w1first.append((ra - r0, rb - r0, rp))
            cast_chunks.append((k, ra, rb, rp))
    c.cast_i = 0
    c.cast_n = 0
    c.drip_n = 0
    c.drip_every = 4
    r_wparts = {}
    for k, ra, rb, rp in cast_chunks:
        r_wparts.setdefault(k, []).append(rp)

    def drip(n=None, upto=None, after=()):
        while c.cast_i < len(cast_chunks):
            k, ra, rb, rp = cast_chunks[c.cast_i]
            if upto is not None:
                if k not in upto:
                    break
            elif n is not None:
                if n <= 0:
                    break
                n -= 1
            q = c.cast_n % 4
            c.cast_n += 1
            c.cast_i += 1
            P.add("pool", lambda e, ra=ra, rb=rb: e.dma_start(out=wbf_d[ra:rb, :], in_=wall_d[ra:rb, :]),
                  r=list(after), w=[rp], dma="wcast%d" % q)

    PH_A = {("w1", 0, 0), ("w2", 0, 0), ("mg",), ("mt",)}
    PH_B = PH_A | {("mq",), ("mk",), ("mo",), ("w1", 0, 1), ("w2", 0, 1), ("pg", 0), ("pp", 0), ("kk",), ("kv",)}
    drip(upto=PH_A)

    def rsqrt_bank(bi, dst, r_dst):
        P.add("act", lambda e: e.activation(dst, c.bank[bi][:], AF.Ln, bias=c.epsc[:, 0:1]),
              r=[c.r_ones], w=[c.r_bank[bi], r_dst])
        P.add("act", lambda e: e.activation(dst, dst, AF.Exp, scale=-0.5), r=[r_dst], w=[r_dst])

    def load_w1slot(dmafn, rws, ndma=1):
        wi = c.w1_i % NW1
        c.w1_i += 1
        wtile = c.w1[wi]
        P.add("sp", lambda e: dmafn(e, wtile), r=rws, w=[c.r_w1[wi]], dma="w1_%d" % wi, ndma=ndma)
        return wtile, c.r_w1[wi]

    def load_wtslot(src, rws):
        wi = c.wt_i % NWT
        c.wt_i += 1
        wtile = c.wt[wi]
        P.add("sp", lambda e: e.dma_start(out=wtile[:], in_=src), r=rws, w=[c.r_wt[wi]], dma="wt_%d" % wi)
        return wtile, c.r_wt[wi]

    def prenorm(gbase):
        xT, r_x = c.xT, c.r_x
        for kc in range(KC):
            P.add("act", lambda e, kc=kc: e.activation(c.sq[:, kc, :], xT[:, kc, :], AF.Square),
                  r=[r_x[kc]], w=[c.r_sq[kc]])
        for kc in range(KC):
            P.add("pe", lambda e, kc=kc: e.matmul(c.pss[:], c.ones[:], c.sq[:, kc, :],
                                                  start=(kc == 0), stop=(kc == KC - 1)),
                  r=[c.r_sq[kc], c.r_ones], w=[c.r_pss])
        rsqrt_bank(6, c.rstd[:], c.r_rstd)
        for kc in range(KC):
            col = gbase + kc
            P.add("dve", lambda e, kc=kc, col=col: e.scalar_tensor_tensor(
                c.xn[:, kc, :], xT[:, kc, :], c.gc[:, col:col + 1], c.rstd[:], ALU.mult, ALU.mult),
                r=[r_x[kc], c.r_rstd, c.r_gc], w=[c.r_xn[kc]])

    def postnorm_add(gl, gi, half):
        xT, r_x = c.xT, c.r_x
        for m in range(KC):
            P.add("pe", lambda e, m=m: e.matmul(c.pss[:], c.ones[:], c.sq[:, m, :],
                                                start=(m == 0), stop=(m == KC - 1)),
                  r=[c.r_sq[m], c.r_ones], w=[c.r_pss])
        rsqrt_bank(6, c.rstd[:], c.r_rstd)
        gsrc = c.hgc if half else c.gc
        for m in range(KC):
            col = gcol(gl, gi, m)
            P.add("dve", lambda e, m=m, col=col: e.scalar_tensor_tensor(
                c.y[:, m, :], c.y[:, m, :], gsrc[:, col:col + 1], c.rstd[:], ALU.mult, ALU.mult),
                r=[c.r_rstd, c.r_gc], w=[c.r_y[m]])
            eng = "pool" if m % 2 == 0 else "dve"
            P.add(eng, lambda e, m=m: e.tensor_tensor(
                xT[:, m, :], xT[:, m, :], c.y[:, m, :], ALU.add),
                r=[c.r_y[m]], w=[r_x[m]])

    def fm_proj(key, n, src, r_src, evac):
        wv = wview(key, "(m p k c) -> p m k c", m=n, p=128, k=KC, c=128)
        for i in range(n):
            wtile, rw = load_w1slot(lambda e, wtile, i=i: e.dma_start(out=wtile[:, :, 0:128], in_=wv[:, i]),
                                    rwb(key))
            b = i % 2
            for kc in range(KC):
                P.add("pe", lambda e, kc=kc, wtile=wtile, b=b: e.matmul(
                    c.pg[b][:], wtile[:, kc, 0:128], src[:, kc, :],
                    start=(kc == 0), stop=(kc == KC - 1)),
                    r=[rw, r_src[kc]], w=[c.r_pg[b]])
            evac(i, b)

    def out_proj(key, src, r_src, gl, gi):
        def evac(m, b):
            P.add("dve", lambda e: e.tensor_copy(c.y[:, m, :], c.pg[b][:]), w=[c.r_pg[b], c.r_y[m]])
            P.add("act", lambda e: e.activation(c.sq[:, m, :], c.y[:, m, :], AF.Square),
                  r=[c.r_y[m]], w=[c.r_sq[m]])
        fm_proj(key, KC, src, r_src, evac)
        postnorm_add(gl, gi, False)

    def ffn(l, f):
        prenorm(gcol(l, 0 if f == 0 else 4, 0))
        w1v = wview(("w1", l, f), "(j p k c) -> j p k c", j=FC, p=128, k=KC, c=256)
        w2v = wview(("w2", l, f), "(m p k c) -> m p k c", m=KC, p=128, k=FC, c=128)
        rw = rwb(("w1", l, f))
        rw2 = rwb(("w2", l, f))
        for j in range(FC):
            rwj = rw
            if (l, f) == (0, 0) and c.w1first is not None:
                rwj = [rp for a, b_, rp in c.w1first if a < 128 * (j + 1) and b_ > 128 * j]
            wtile, rws = load_w1slot(lambda e, wtile, j=j: e.dma_start(out=wtile[:], in_=w1v[j]), rwj)
            b = j % 2
            for kc in range(KC):
                P.add("pe", lambda e, kc=kc, wtile=wtile, b=b: e.matmul(
                    c.pg[b][:], wtile[:, kc, 0:128], c.xn[:, kc, :],
                    start=(kc == 0), stop=(kc == KC - 1)),
                    r=[rws, c.r_xn[kc]], w=[c.r_pg[b]])
            for kc in range(KC):
                P.add("pe", lambda e, kc=kc, wtile=wtile, b=b: e.matmul(
                    c.pu[b][:], wtile[:, kc, 128:256], c.xn[:, kc, :],
                    start=(kc == 0), stop=(kc == KC - 1)),
                    r=[rws, c.r_xn[kc]], w=[c.r_pu[b]])
            P.add("act", lambda e, b=b: e.activation(c.sg[b][:], c.pg[b][:], AF.Silu),
                  w=[c.r_pg[b], c.r_sg[b]])
            P.add("dve", lambda e, j=j, b=b: e.tensor_tensor(
                c.aT[:, j, :], c.sg[b][:], c.pu[b][:], ALU.mult),
                r=[c.r_sg[b]], w=[c.r_pu[b], c.r_a[j]])
        for m in range(KC):
            wi = c.w2_i % NW2
            c.w2_i += 1
            P.add("sp", lambda e, m=m, wi=wi: e.dma_start(out=c.w2[wi][:], in_=w2v[m]),
                  r=rw2, w=[c.r_w2[wi]], dma="w2_%d" % wi)
            b = m % 2
            for j in range(FC):
                P.add("pe", lambda e, j=j, m=m, wi=wi, b=b: e.matmul(
                    c.py[b][:], c.w2[wi][:, j, :], c.aT[:, j, :],
                    start=(j == 0), stop=(j == FC - 1)),
                    r=[c.r_w2[wi], c.r_a[j]], w=[c.r_py[b]])
            P.add("dve", lambda e, m=m, b=b: e.tensor_copy(c.y[:, m, :], c.py[b][:]),
                  w=[c.r_py[b], c.r_y[m]])
            P.add("act", lambda e, m=m: e.activation(c.sq[:, m, :], c.y[:, m, :], AF.Square),
                  r=[c.r_y[m]], w=[c.r_sq[m]])
        postnorm_add(l, 1 if f == 0 else 5, True)

    def ple(l, t):
        prenorm(gcol(l, 6, 0))
        P.add("sp", lambda e: e.dma_start(out=c.pT[:], in_=pT_d[l, :, :, t * TT:(t + 1) * TT]),
              w=[c.r_pT], dma="pT")
        P.add("act", lambda e: e.activation(c.pTb[:], c.pT[:], AF.Copy), r=[c.r_pT], w=[c.r_pTb])
        wgv = wview(("pg", l), "(m p k c) -> p m k c", m=KC, p=128, k=KC, c=128)
        wpv = wview(("pp", l), "(m p k c) -> p m k c", m=KC, p=128, k=2, c=128)
        for m in range(KC):
            wtile, rws = load_w1slot(lambda e, wtile, m=m: [
                e.dma_start(out=wtile[:, :, 0:128], in_=wgv[:, m]),
                e.dma_start(out=wtile[:, 0:2, 128:256], in_=wpv[:, m])],
                rwb(("pg", l)) + rwb(("pp", l)), ndma=2)
            b = m % 2
            for kc in range(KC):
                P.add("pe", lambda e, kc=kc, wtile=wtile, b=b: e.matmul(
                    c.pg[b][:], wtile[:, kc, 0:128], c.xn[:, kc, :],
                    start=(kc == 0), stop=(kc == KC - 1)),
                    r=[rws, c.r_xn[kc]], w=[c.r_pg[b]])
            for pc in range(2):
                P.add("pe", lambda e, pc=pc, wtile=wtile, b=b: e.matmul(
                    c.pu[b][:], wtile[:, pc, 128:256], c.pTb[:, pc, :],
                    start=(pc == 0), stop=(pc == 1)),
                    r=[rws, c.r_pTb], w=[c.r_pu[b]])
            P.add("act", lambda e, b=b: e.activation(c.tmp[b][:], c.pg[b][:], AF.Sigmoid),
                  w=[c.r_pg[b], c.r_tmp[b]])
            P.add("dve", lambda e, m=m, b=b: e.tensor_tensor(
                c.y[:, m, :], c.tmp[b][:], c.pu[b][:], ALU.mult),
                r=[c.r_tmp[b]], w=[c.r_pu[b], c.r_y[m]])
            P.add("act", lambda e, m=m: e.activation(c.sq[:, m, :], c.y[:, m, :], AF.Square),
                  r=[c.r_y[m]], w=[c.r_sq[m]])
        postnorm_add(l, 7, False)

    ogT, r_ogT = c.aT[:, 0:8, :], c.r_a[0:8]
    qT, r_qT = c.aT[:, 8:12, :], c.r_a[8:12]
    kT, r_kT = c.aT[:, 12:16, :], c.r_a[12:16]
    ktm, r_ktm = c.aT[:, 16:20, :], c.r_a[16:20]
    so = c.y[:].rearrange("p (c a) t -> p c (a t)", c=4, a=2)
    mst = c.Cn[:, 4 * 257:4 * 257 + 4]
    Cn4 = c.Cn[:, 0:4 * 257].rearrange("p (h v) -> p h v", h=4, v=257)

    def smv(i):
        return c.sm[:, i, :], c.r_sm[i]

    def mlstm_proj(full):
        for ch in range(4):
            b = ch % 2
            for kc in range(KC):
                P.add("pe", lambda e, kc=kc, ch=ch, b=b: e.matmul(
                    c.pg[b][:, 0:8], c.xn[:, kc, ch * 128:(ch + 1) * 128], c.mg[:, kc, :],
                    start=(kc == 0), stop=(kc == KC - 1)),
                    r=[c.r_mg, c.r_xn[kc]], w=[c.r_pg[b]])
            P.add("dve", lambda e, ch=ch, b=b: e.tensor_tensor(
                c.gt[:, ch, :], c.pg[b][:, 0:8], c.prm[:, P_BG:P_BG + 8], ALU.add),
                r=[c.r_prm], w=[c.r_pg[b], c.r_gt[ch]])

        for ch in range(4):
            mlstm_pre(ch, full)
        if full:
            def evq(h, b):
                P.add("act", lambda e: e.activation(qT[:, h, :], c.pg[b][:], AF.Copy),
                      w=[c.r_pg[b], r_qT[h]])
            fm_proj(("mq",), 4, c.xn, c.r_xn, evq)

            def evk(h, b):
                P.add("act", lambda e: e.activation(kT[:, h, :], c.pg[b][:], AF.Copy, scale=DKS),
                      w=[c.r_pg[b], r_kT[h]])
            fm_proj(("mk",), 4, c.xn, c.r_xn, evk)
        mtv = wview(("mt",), "(g p k c) -> g p k c", g=5, p=128, k=KC, c=512)
        groups = [0, 1, 2, 3, 4] if full else [0, 1, 2]
        nb = 0
        for g in groups:
            wtile, rw = load_wtslot(mtv[g], rwb(("mt",)))
            for ch in range(4):
                b = nb % 2
                nb += 1
                for kc in range(KC):
                    P.add("pe", lambda e, kc=kc, ch=ch, wtile=wtile, b=b: e.matmul(
                        c.pu[b][:], c.xn[:, kc, ch * 128:(ch + 1) * 128], wtile[:, kc, :],
                        start=(kc == 0), stop=(kc == KC - 1)),
                        r=[rw, c.r_xn[kc]], w=[c.r_pu[b]])
                if g == 0:
                    P.add("act", lambda e, ch=ch, b=b: e.activation(ktm[:, ch, :], c.pu[b][:], AF.Copy),
                          w=[c.r_pu[b], r_ktm[ch]])
                elif g in (1, 2):
                    h0 = 2 * (g - 1)
                    P.add("dve", lambda e, ch=ch, b=b, h0=h0: e.tensor_copy(
                        c.v1[:, ch, h0:h0 + 2, 0:256],
                        c.pu[b][:].rearrange("p (h v) -> p h v", h=2, v=256)),
                        w=[c.r_pu[b], c.r_v1])
                else:
                    o0 = (g - 3) * 512
                    P.add("act", lambda e, ch=ch, b=b, o0=o0: e.activation(
                        so[:, ch, o0:o0 + 512], c.pu[b][:], AF.Sigmoid),
                        w=[c.r_pu[b], c.r_y[2 * ch], c.r_y[2 * ch + 1]])
    def mlstm_pre(ch, full):
        li = c.gt[:, ch, 0:4]
        fz = c.gt[:, ch, 4:8]
        pb = 16 + 4 * ch
        (sp_, r_sp) = smv(0)
        (a_, r_a_), (nb_, r_nb), (nbL, r_nbL), (cm, r_cm) = smv(pb), smv(pb + 1), smv(pb + 2), smv(pb + 3)
        (M_, r_M), (Ml, r_Ml), (t4, r_t4), (sc_, r_sc) = smv(4), smv(5), smv(6), smv(7)
        (dcy, r_dcy), (dn, r_dn), (lim, r_lim), (ssq, r_ssq) = smv(8), smv(9), smv(10), smv(11)
        (rr, r_rr) = smv(12)
        B0, B1, B2 = c.bank[0], c.bank[1], c.bank[2]
        rB0, rB1, rB2 = c.r_bank[0], c.r_bank[1], c.r_bank[2]
        B4, B5, rB4, rB5 = c.bank[4], c.bank[5], c.r_bank[4], c.r_bank[5]
        P.add("act", lambda e: e.activation(sp_, fz, AF.Exp, scale=-1.0), r=[c.r_gt[ch]], w=[r_sp])
        P.add("act", lambda e: e.activation(sp_, sp_, AF.Ln, bias=1.0), r=[r_sp], w=[r_sp])
        P.add("pe", lambda e: e.matmul(B4[:, 0:4], Umat, sp_, start=True, stop=True),
              r=[r_sp, c.r_cst], w=[rB4])
        P.add("pe", lambda e: e.matmul(B4[:, 4:8], onesf, sp_, start=True, stop=True),
              r=[r_sp, c.r_cst], w=[rB4])
        P.add("dve", lambda e: e.tensor_tensor(a_, B4[:, 0:4], li, ALU.add), r=[c.r_gt[ch]], w=[rB4, r_a_])
        P.add("dve", lambda e: e.tensor_copy(nb_, B4[:, 0:4]), w=[rB4, r_nb])
        P.add("dve", lambda e: e.tensor_copy(nbL, B4[:, 4:8]), w=[rB4, r_nbL])
        for h in range(4):
            P.add("dve", lambda e, h=h: e.tensor_scalar(c.dg[:, h, :], ident, a_[:, h:h + 1], None, ALU.mult),
                  r=[c.r_cst, r_a_], w=[c.r_dg])
        P.add("pe", lambda e: e.matmul(B5[:], onesf, c.dg[:].rearrange("p h s -> p (h s)"),
                                       start=True, stop=True), r=[c.r_dg, c.r_cst], w=[rB5])
        if full:
            P.add("dve", lambda e: e.tensor_tensor(c.e1[:].rearrange("p h s -> p (h s)"), B5[:],
                                                   cs(C_MC4, 512), ALU.add), r=[c.r_cst], w=[rB5, c.r_e1])
            P.add("dve", lambda e: e.tensor_reduce(cm, c.e1[:], AX.X, ALU.max), r=[c.r_e1], w=[r_cm])
        else:
            P.add("dve", lambda e: e.tensor_reduce(cm, B5[:].rearrange("p (h s) -> p h s", h=4, s=128),
                                                   AX.X, ALU.max), w=[rB5, r_cm])

    def mlstm_mid(ch):
        li = c.gt[:, ch, 0:4]
        fz = c.gt[:, ch, 4:8]
        pb = 16 + 4 * ch
        (sp_, r_sp) = smv(0)
        (a_, r_a_), (nb_, r_nb), (nbL, r_nbL), (cm, r_cm) = smv(pb), smv(pb + 1), smv(pb + 2), smv(pb + 3)
        (M_, r_M), (Ml, r_Ml), (t4, r_t4), (sc_, r_sc) = smv(4), smv(5), smv(6), smv(7)
        (dcy, r_dcy), (dn, r_dn), (lim, r_lim), (ssq, r_ssq) = smv(8), smv(9), smv(10), smv(11)
        (rr, r_rr) = smv(12)
        B0, B1, B2 = c.bank[0], c.bank[1], c.bank[2]
        rB0, rB1, rB2 = c.r_bank[0], c.r_bank[1], c.r_bank[2]
        P.add("dve", lambda e: e.tensor_tensor(M_, cm, mst, ALU.max), r=[r_cm, c.r_Cn], w=[r_M])
        for h in range(4):
            P.add("dve", lambda e, h=h: e.tensor_scalar(c.dg[:, h, :], ident, M_[:, h:h + 1], None, ALU.mult),
                  r=[c.r_cst, r_M], w=[c.r_dg])
        P.add("pe", lambda e: e.matmul(B2[:], onesf, c.dg[:].rearrange("p h s -> p (h s)"),
                                       start=True, stop=True), r=[c.r_dg, c.r_cst], w=[rB2])
        B2v = B2[:].rearrange("p (h t) -> p h t", h=4, t=128)
        P.add("dve", lambda e: e.tensor_tensor(c.e1[:].rearrange("p h s -> p (h s)"), B2[:],
                                               cs(C_MT4, 512), ALU.add), r=[c.r_cst], w=[rB2, c.r_e1])
        for h in range(4):
            P.add("act", lambda e, h=h: e.activation(c.wT[:, h, :], c.e1[:, h, :], AF.Exp,
                                                     bias=a_[:, h:h + 1], scale=-1.0),
                  r=[c.r_e1, r_a_], w=[c.r_wT])
        for h in range(4):
            P.add("act", lambda e, h=h: e.activation(c.sib[:, h, :], B2v[:, h, :], AF.Exp,
                                                     bias=mst[:, h:h + 1], scale=-1.0),
                  r=[c.r_Cn], w=[rB2, c.r_sib])
        P.add("dve", lambda e: e.tensor_copy(Ml, B2v[:, :, 127]), w=[rB2, r_Ml])
        P.add("dve", lambda e: e.tensor_tensor(c.qs[:], qT[:, :, ch * 128:(ch + 1) * 128], c.sib[:], ALU.mult),
              r=r_qT + [c.r_sib], w=[c.r_qs])
        for h in range(4):
            P.add("pe", lambda e, h=h: e.matmul(B0[:, h * 128:(h + 1) * 128],
                                                kT[:, h, ch * 128:(ch + 1) * 128],
                                                qT[:, h, ch * 128:(ch + 1) * 128], start=True, stop=True),
                  r=r_kT + r_qT, w=[rB0])
        P.add("dve", lambda e: e.tensor_tensor(c.qkw[:].rearrange("p h t -> p (h t)"), B0[:],
                                               c.wT[:].rearrange("p h t -> p (h t)"), ALU.mult),
              r=[c.r_wT], w=[rB0, c.r_qkw])
        for h in range(4):
            bk, rbk = c.bank[3 + h], c.r_bank[3 + h]
            P.add("pe", lambda e, h=h, bk=bk: e.matmul(bk[:, 0:257], c.qkw[:, h, :], c.v1[:, ch, h, :],
                                                       start=True, stop=False),
                  r=[c.r_qkw, c.r_v1], w=[rbk])
            P.add("pe", lambda e, h=h, bk=bk: e.matmul(bk[:, 0:257], c.qs[:, h, :], c.Cnb[:, h, :],
                                                       start=False, stop=True),
                  r=[c.r_qs, c.r_Cnb], w=[rbk])

    def mlstm_mid_b(ch):
        li = c.gt[:, ch, 0:4]
        fz = c.gt[:, ch, 4:8]
        pb = 16 + 4 * ch
        (sp_, r_sp) = smv(0)
        (a_, r_a_), (nb_, r_nb), (nbL, r_nbL), (cm, r_cm) = smv(pb), smv(pb + 1), smv(pb + 2), smv(pb + 3)
        (M_, r_M), (Ml, r_Ml), (t4, r_t4), (sc_, r_sc) = smv(4), smv(5), smv(6), smv(7)
        (dcy, r_dcy), (dn, r_dn), (lim, r_lim), (ssq, r_ssq) = smv(8), smv(9), smv(10), smv(11)
        (rr, r_rr) = smv(12)
        B0, B1, B2 = c.bank[0], c.bank[1], c.bank[2]
        rB0, rB1, rB2 = c.r_bank[0], c.r_bank[1], c.r_bank[2]
        P.add("dve", lambda e: e.tensor_tensor(t4, nb_, M_, ALU.subtract), r=[r_nb, r_M], w=[r_t4])
        P.add("act", lambda e: e.activation(lim, t4, AF.Exp), r=[r_t4], w=[r_lim])
        for h in range(4):
            bk, rbk = c.bank[3 + h], c.r_bank[3 + h]
            P.add("dve", lambda e, h=h, bk=bk: e.tensor_copy(dn[:, h:h + 1], bk[:, 256:257]),
                  w=[rbk, r_dn])
        P.add("dve", lambda e: e.scalar_tensor_tensor(t4, dn, -1.0, dn, ALU.mult, ALU.max),
              r=[r_dn], w=[r_t4])
        P.add("dve", lambda e: e.tensor_tensor(t4, t4, lim, ALU.max), r=[r_t4, r_lim], w=[r_t4])
        P.add("dve", lambda e: e.reciprocal(dn, t4), r=[r_t4], w=[r_dn])
        for h in range(4):
            bk, rbk = c.bank[3 + h], c.r_bank[3 + h]
            P.add("dve", lambda e, h=h, bk=bk: e.tensor_scalar(c.hh[:, h, :], bk[:, 0:256],
                                                               dn[:, h:h + 1], None, ALU.mult),
                  r=[r_dn], w=[rbk, c.r_hh[h]])
            P.add("act", lambda e, h=h: e.activation(c.junk[:], c.hh[:, h, :], AF.Square,
                                                     accum_out=ssq[:, h:h + 1]),
                  r=[c.r_hh[h]], w=[c.r_junk, r_ssq])

    def mlstm_fin(ch):
        li = c.gt[:, ch, 0:4]
        fz = c.gt[:, ch, 4:8]
        pb = 16 + 4 * ch
        (sp_, r_sp) = smv(0)
        (a_, r_a_), (nb_, r_nb), (nbL, r_nbL), (cm, r_cm) = smv(pb), smv(pb + 1), smv(pb + 2), smv(pb + 3)
        (M_, r_M), (Ml, r_Ml), (t4, r_t4), (sc_, r_sc) = smv(4), smv(5), smv(6), smv(7)
        (dcy, r_dcy), (dn, r_dn), (lim, r_lim), (ssq, r_ssq) = smv(8), smv(9), smv(10), smv(11)
        (rr, r_rr) = smv(12)
        B0, B1, B2 = c.bank[0], c.bank[1], c.bank[2]
        rB0, rB1, rB2 = c.r_bank[0], c.r_bank[1], c.r_bank[2]
        P.add("act", lambda e: e.activation(rr, ssq, AF.Ln, bias=c.epsc[:, 0:1], scale=1.0 / 256),
              r=[r_ssq, c.r_ones], w=[r_rr])
        P.add("act", lambda e: e.activation(rr, rr, AF.Exp, scale=-0.5), r=[r_rr], w=[r_rr])
        for h in range(4):
            P.add("dve", lambda e, h=h: e.scalar_tensor_tensor(
                c.hh[:, h, :], c.hh[:, h, :], rr[:, h:h + 1],
                c.prm[:, P_HN + h * 256:P_HN + (h + 1) * 256], ALU.mult, ALU.mult),
                r=[c.r_hh[h], r_rr, c.r_prm], w=[c.r_hh[h]])
        P.add("dve", lambda e: e.tensor_tensor(c.og[:], c.hh[:].rearrange("p h v -> p (h v)"),
                                               so[:, ch, :], ALU.mult),
              r=c.r_hh + [c.r_y[2 * ch], c.r_y[2 * ch + 1]], w=[c.r_og])
        B7b = c.bank[7][:].bitcast(BF16)
        for kc in range(KC):
            P.add("pe", lambda e, kc=kc: e.transpose(B7b[:, kc * 128:(kc + 1) * 128],
                                                     c.og[:, kc * 128:(kc + 1) * 128], c.identb[:]),
                  r=[c.r_og, c.r_ones], w=[c.r_bank[7]])
        P.add("act", lambda e: e.activation(ogT[:, :, ch * 128:(ch + 1) * 128],
                                            B7b[:, 0:1024].rearrange("p (k t) -> p k t", k=8, t=128), AF.Copy),
              w=[c.r_bank[7]] + r_ogT)

    def mlstm_upd(ch, full):
        li = c.gt[:, ch, 0:4]
        fz = c.gt[:, ch, 4:8]
        pb = 16 + 4 * ch
        (sp_, r_sp) = smv(0)
        (a_, r_a_), (nb_, r_nb), (nbL, r_nbL), (cm, r_cm) = smv(pb), smv(pb + 1), smv(pb + 2), smv(pb + 3)
        (M_, r_M), (Ml, r_Ml), (t4, r_t4), (sc_, r_sc) = smv(4), smv(5), smv(6), smv(7)
        (dcy, r_dcy), (dn, r_dn), (lim, r_lim), (ssq, r_ssq) = smv(8), smv(9), smv(10), smv(11)
        (rr, r_rr) = smv(12)
        B0, B1, B2 = c.bank[0], c.bank[1], c.bank[2]
        rB0, rB1, rB2 = c.r_bank[0], c.r_bank[1], c.r_bank[2]
        if not full:
            P.add("dve", lambda e: e.tensor_tensor(Ml, cm, mst, ALU.max), r=[r_cm, c.r_Cn], w=[r_Ml])
        P.add("dve", lambda e: e.tensor_tensor(t4, a_, Ml, ALU.subtract), r=[r_a_, r_Ml], w=[r_t4])
        P.add("act", lambda e: e.activation(sc_, t4, AF.Exp), r=[r_t4], w=[r_sc])
        P.add("dve", lambda e: e.tensor_tensor(dcy, mst, Ml, ALU.subtract), r=[c.r_Cn, r_Ml], w=[r_dcy])
        P.add("act", lambda e: e.activation(dcy, dcy, AF.Exp), r=[r_dcy], w=[r_dcy])
        for h in range(4):
            P.add("dve", lambda e, h=h: e.tensor_scalar(c.ksc[:, h, :], ktm[:, ch, h * 128:(h + 1) * 128],
                                                        sc_[:, h:h + 1], DKS, ALU.mult, ALU.mult),
                  r=r_ktm + [r_sc], w=[c.r_ksc])
        for h in range(4):
            bk, rbk = c.bank[3 + h], c.r_bank[3 + h]
            P.add("pe", lambda e, h=h, bk=bk: e.matmul(bk[:, 0:257], c.ksc[:, h, :], c.v1[:, ch, h, :],
                                                       start=True, stop=True),
                  r=[c.r_ksc, c.r_v1], w=[rbk])
            P.add("dve", lambda e, h=h, bk=bk: e.scalar_tensor_tensor(
                Cn4[:, h, :], Cn4[:, h, :], dcy[:, h:h + 1], bk[:, 0:257], ALU.mult, ALU.add),
                r=[r_dcy, c.r_Cn], w=[rbk, c.r_Cn])
        P.add("dve", lambda e: e.tensor_tensor(mst, Ml, nbL, ALU.subtract), r=[r_Ml, r_nbL, c.r_Cn], w=[c.r_Cn])
        if full:
            P.add("act", lambda e: e.activation(c.Cnb[:], Cn4, AF.Copy), r=[c.r_Cn], w=[c.r_Cnb])


    def mlstm(full):
        prenorm(gcol(0, 2, 0))
        mlstm_proj(full)
        for ch in range(4):
            if full:
                mlstm_mid(ch)
                if ch > 0:
                    mlstm_fin(ch - 1)
                mlstm_mid_b(ch)
            mlstm_upd(ch, full)
            if c.drip_n and ch in (1, 3):
                drip(n=1, after=[c.r_ksc])
        if full:
            mlstm_fin(3)
            out_proj(("mo",), ogT, r_ogT, 0, 3)

    def mlstm_state_init_zero():
        P.add("dve", lambda e: e.memset(c.Cn[:], 0.0), w=[c.r_Cn])
        P.add("dve", lambda e: e.memset(mst, -1e30), w=[c.r_Cn])
        P.add("sp", lambda e: e.dma_start(out=c.mg[:], in_=wview(("mg",), "(p k c) -> p k c", p=128, k=KC, c=8)),
              r=rwb(("mg",)), w=[c.r_mg], dma="mg")

    def mlstm_exchange():
        P.add("sp", lambda e: e.dma_start(out=st_own.ap(), in_=c.Cn[:]), r=[c.r_Cn], w=[c.r_st_own], dma="st")
        P.add("pool", lambda e: e.collective_compute("AllGather", ALU.bypass, replica_groups=PAIRS,
                                                     ins=[st_own.ap().opt()], outs=[st_all.ap().opt()]),
              r=[c.r_st_own], w=[c.r_st_all], name="cc", dma="cc", inc=1)
        P.add("sp", lambda e: e.dma_start(out=c.Cn[:], in_=st_all.ap()[0:128, :]), r=[c.r_st_all], w=[c.r_Cn], dma="st")
        sel = c.prm[:, P_SEL:P_SEL + 1]
        P.add("dve", lambda e: e.tensor_scalar(c.Cn[:], c.Cn[:], sel, None, ALU.mult), r=[c.r_prm, c.r_Cn], w=[c.r_Cn])
        t4, r_t4 = smv(6)
        P.add("dve", lambda e: e.tensor_scalar(t4[:, 0:1], sel, -1.0, 1e30, ALU.add, ALU.mult), r=[c.r_prm], w=[r_t4])
        P.add("dve", lambda e: e.tensor_scalar(mst, mst, t4[:, 0:1], None, ALU.add), r=[r_t4, c.r_Cn], w=[c.r_Cn])
        P.add("act", lambda e: e.activation(c.Cnb[:], Cn4, AF.Copy), r=[c.r_Cn], w=[c.r_Cnb])

    kst, r_kst = c.aT[:, 0:8, :], c.r_a[0:8]
    vst, r_vst = c.aT[:, 8:16, :], c.r_a[8:16]
    vst4 = c.aT[:, 8:16, :].rearrange("p (c a) t -> p c (a t)", c=4, a=2)

    def kvproj(t):
        prenorm(2 * 8 * KC)
        def evk(h, b):
            P.add("act", lambda e: e.activation(kst[:, h, :], c.pg[b][:], AF.Copy), w=[c.r_pg[b], r_kst[h]])
        fm_proj(("kk",), KC, c.xn, c.r_xn, evk)
        kvv = wview(("kv",), "(g p k c) -> g p k c", g=2, p=128, k=KC, c=512)
        nb = 0
        for g in range(2):
            wtile, rw = load_wtslot(kvv[g], rwb(("kv",)))
            for ch in range(4):
                b = nb % 2
                nb += 1
                for kc in range(KC):
                    P.add("pe", lambda e, kc=kc, ch=ch, wtile=wtile, b=b: e.matmul(
                        c.pu[b][:], c.xn[:, kc, ch * 128:(ch + 1) * 128], wtile[:, kc, :],
                        start=(kc == 0), stop=(kc == KC - 1)),
                        r=[rw, c.r_xn[kc]], w=[c.r_pu[b]])
                P.add("act", lambda e, ch=ch, b=b, g=g: e.activation(
                    vst4[:, ch, g * 512:(g + 1) * 512], c.pu[b][:], AF.Copy),
                    w=[c.r_pu[b], r_vst[2 * ch], r_vst[2 * ch + 1]])
        ko = kv_own[t].ap()
        P.add("act", lambda e: [
            e.dma_start(out=ko[0:1024, :].rearrange("(h p) t -> p h t", h=8, p=128), in_=kst),
            e.dma_start(out=ko[1024:2048, :].rearrange("(c p a) t -> p c (a t)", c=4, p=128, a=2), in_=vst4)],
            r=r_kst + r_vst, w=[c.r_kv_own[t]], dma="kvst", ndma=2)
        if NOCC:
            P.add("sp", lambda e: e.dma_start(out=kv_all[t].ap()[0:2048, :], in_=kv_own[t].ap()),
                  r=[c.r_kv_own[t]], w=[c.r_kv_all[t]], dma="kvcp")
        else:
            P.add("pool", lambda e: e.collective_compute("AllGather", ALU.bypass, replica_groups=PAIRS,
                                                         ins=[kv_own[t].ap().opt()], outs=[kv_all[t].ap().opt()]),
                  r=[c.r_kv_own[t]], w=[c.r_kv_all[t]], name="cc", dma="cc", inc=1)

    aqT, r_aqT = c.aT[:, 0:8, :], c.r_a[0:8]
    aoT, r_aoT = c.aT[:, 8:16, :], c.r_a[8:16]

    def attn_setup():
        lam = c.prm[:, P_LAM:P_LAM + 256]
        P.add("dve", lambda e: e.tensor_tensor(c.junk[:, 0:64], lam[:, 0:64], lam[:, 64:128], ALU.mult),
              r=[c.r_prm], w=[c.r_junk])
        P.add("dve", lambda e: e.tensor_tensor(c.junk[:, 64:128], lam[:, 128:192], lam[:, 192:256], ALU.mult),
              r=[c.r_prm], w=[c.r_junk])
        P.add("dve", lambda e: e.tensor_reduce(c.lamc[:, 0:2], c.junk[:, 0:128].rearrange("p (a d) -> p a d", a=2, d=64),
                                               AX.X, ALU.add), r=[c.r_junk], w=[c.r_lam])
        P.add("act", lambda e: e.activation(c.lamc[:, 0:2], c.lamc[:, 0:2], AF.Exp), r=[c.r_lam], w=[c.r_lam])
        P.add("dve", lambda e: e.scalar_tensor_tensor(c.lamc[:, 2:3], c.lamc[:, 0:1], LAM_INIT, c.lamc[:, 1:2],
                                                      ALU.add, ALU.subtract), r=[c.r_lam], w=[c.r_lam])
        P.add("dve", lambda e: e.tensor_scalar(c.lamc[:, 3:4], c.prm[:, P_SUBLN:P_SUBLN + 1], 1.0 - LAM_INIT, None,
                                               ALU.mult), r=[c.r_prm, c.r_lam], w=[c.r_lam])

    def attn(t):
        prenorm(gcol(1, 2, 0))

        def evq(h, b):
            P.add("act", lambda e: e.activation(aqT[:, h, :], c.pg[b][:], AF.Copy, scale=0.125),
                  w=[c.r_pg[b], r_aqT[h]])
        fm_proj(("dq",), KC, c.xn, c.r_xn, evq)
        chunks = []
        na = min(NT, 8)
        for ci in range(0, na, 4):
            chunks.append(("A", [(kv_all[tt], tt) for tt in range(ci, min(ci + 4, na))]))
        own = list(range(t + 1))
        for ci in range(0, len(own), 4):
            chunks.append(("B", [(kv_own[tt], tt) for tt in own[ci:ci + 4]]))
        nkb_total = sum(4 * len(srcs) for _, srcs in chunks)
        SA, SB = [c.bank[0], c.bank[2]], [c.bank[1], c.bank[3]]
        rSA, rSB = [c.r_bank[0], c.r_bank[2]], [c.r_bank[1], c.r_bank[3]]
        O = [c.bank[4], c.bank[5]]
        rO = [c.r_bank[4], c.r_bank[5]]
        SM = [c.bank[6], c.bank[7]]
        rSM = [c.r_bank[6], c.r_bank[7]]
        loads = []
        iters = []
        for h in range(8):
            kbi = 0
            for reg, srcs in chunks:
                li_ = len(loads)
                loads.append((h, reg, srcs))
                for n, (dt_, tt) in enumerate(srcs):
                    for j4 in range(4):
                        iters.append(dict(h=h, load=li_, kblk=n * 4 + j4, j4=j4, reg=reg,
                                          diag=(reg == "B" and tt == t),
                                          first=(kbi == 0), last=(kbi == nkb_total - 1)))
                        kbi += 1
        slot_of = {}

        def emit_load(li_):
            h, reg, srcs = loads[li_]
            si = c.kb_i % 2
            c.kb_i += 1
            slot_of[li_] = si
            kbuf, vbuf = c.kb[si], c.vb[si]

            def dmas(e):
                ins = []
                for n, (dt_, tt) in enumerate(srcs):
                    a = dt_.ap()
                    ins.append(e.dma_start(out=kbuf[:, n * 512:(n + 1) * 512], in_=a[h * 128:(h + 1) * 128, :]))
                    vsrc = a[1024:2048, :].rearrange("(c p a) t -> p c (a t)", c=4, p=128, a=2)
                    ins.append(e.dma_start(out=vbuf[:, 4 * n:4 * n + 4, :], in_=vsrc[:, :, h * 128:(h + 1) * 128]))
                return ins
            rsrc = [c.r_kv_all[tt] if reg == "A" else c.r_kv_own[tt] for _, tt in srcs]
            P.add("sp", dmas, r=rsrc, w=[c.r_kb[si], c.r_vb[si]], dma="kvld_%d" % si, ndma=2 * len(srcs))

        def emit_scores(i, d):
            si = slot_of[d["load"]]
            kbuf = c.kb[si]
            b = i % 2
            h, kblk, j4 = d["h"], d["kblk"], d["j4"]
            dg_ = d["diag"]
            for m, (S_, rS) in enumerate(((SA, rSA), (SB, rSB))):
                p0 = 64 * m
                P.add("pe", lambda e, S_=S_, p0=p0: e.matmul(
                    S_[b][:], kbuf[p0:p0 + 64, kblk * 128:(kblk + 1) * 128], aqT[p0:p0 + 64, h, :],
                    start=True, stop=not dg_),
                    r=[c.r_kb[si], r_aqT[h]], w=[rS[b]])
            if dg_:
                for m, (S_, rS) in enumerate(((SA, rSA), (SB, rSB))):
                    P.add("pe", lambda e, S_=S_: e.matmul(
                        S_[b][:], c.identb[:], c.maskb[:, 512 * j4:512 * (j4 + 1)], start=False, stop=True),
                        r=[c.r_cst, c.r_ones], w=[rS[b]])
            spair = c.pairs[b][:, 0:1024]
            ppair = c.ptpair[b]
            wres = [rSA[b], rSB[b], c.r_pt[0][b], c.r_pt[1][b]]
            if d["reg"] == "A":
                P.add("act", lambda e: e.activation(ppair[:], spair, AF.Exp, bias=c.prm[:, P_BIASA:P_BIASA + 1]),
                      r=[c.r_prm], w=wres)
            else:
                P.add("act", lambda e: e.activation(ppair[:], spair, AF.Exp), w=wres)

        def emit_av(i, d):
            si = slot_of[d["load"]]
            vbuf = c.vb[si]
            b = i % 2
            kblk, first, last = d["kblk"], d["first"], d["last"]
            for m in range(2):
                ptile, rpt = c.pt[m][b], c.r_pt[m][b]
                P.add("pe", lambda e, m=m, ptile=ptile: e.matmul(
                    O[m][:], vbuf[:, kblk, :], ptile[:], start=first, stop=last),
                    r=[c.r_vb[si], rpt], w=[rO[m]])
                if m == 0:
                    P.add("pe", lambda e, m=m, ptile=ptile: e.matmul(
                        SM[m][:], c.one1[:], ptile[:], start=first, stop=last),
                        r=[c.r_ones, rpt], w=[rSM[m]])
                elif first:
                    P.add("dve", lambda e, ptile=ptile: e.tensor_copy(c.rstd[:], ptile[:]),
                          r=[rpt], w=[c.r_rstd])
                else:
                    P.add("dve", lambda e, ptile=ptile: e.tensor_tensor(c.rstd[:], c.rstd[:], ptile[:], ALU.add),
                          r=[rpt], w=[c.r_rstd])

        def emit_combine(h):
            P.add("pe", lambda e: e.matmul(SM[1][:], onesf, c.rstd[:], start=True, stop=True),
                  r=[c.r_rstd, c.r_cst], w=[rSM[1]])
            P.add("dve", lambda e: e.tensor_copy(c.ms[0][:], SM[0][:]), w=[rSM[0], c.r_ms[0]])
            P.add("dve", lambda e: e.tensor_copy(c.ms[1][:], SM[1][:]), w=[rSM[1], c.r_ms[1]])
            P.add("dve", lambda e: e.tensor_copy(c.tmp[0][:], O[0][:]), w=[rO[0], c.r_tmp[0]])
            P.add("dve", lambda e: e.tensor_copy(c.tmp[1][:], O[1][:]), w=[rO[1], c.r_tmp[1]])
            P.add("act", lambda e: e.activation(c.msall[:], c.msall[:], AF.Ln),
                  r=[c.r_ms[0], c.r_ms[1]], w=[c.r_ms[0], c.r_ms[1]])
            P.add("act", lambda e: e.activation(c.msall[:], c.msall[:], AF.Exp, scale=-1.0),
                  r=[c.r_ms[0], c.r_ms[1]], w=[c.r_ms[0], c.r_ms[1]])
            P.add("dve", lambda e: e.tensor_tensor(c.tmp[0][:], c.tmp[0][:], c.ms[0][:], ALU.mult),
                  r=[c.r_ms[0], c.r_tmp[0]], w=[c.r_tmp[0]])
            P.add("dve", lambda e: e.scalar_tensor_tensor(c.tmp[1][:], c.tmp[1][:], c.lamc[:, 2:3], c.ms[1][:],
                                                          ALU.mult, ALU.mult),
                  r=[c.r_ms[1], c.r_tmp[1], c.r_lam], w=[c.r_tmp[1]])
            P.add("dve", lambda e: e.tensor_tensor(c.y[:, h, :], c.tmp[0][:], c.tmp[1][:], ALU.subtract),
                  r=[c.r_tmp[0], c.r_tmp[1]], w=[c.r_y[h]])

        def emit_subln():
            for h in range(8):
                b = h % 2
                P.add("act", lambda e, h=h: e.activation(c.sq[:, h, :], c.y[:, h, :], AF.Square),
                      r=[c.r_y[h]], w=[c.r_sq[h]])
                P.add("pe", lambda e, h=h, b=b: e.matmul(c.bank[b][:], c.o128[:], c.sq[:, h, :], start=True, stop=True),
                      r=[c.r_sq[h], c.r_ones], w=[c.r_bank[b]])
                rsqrt_bank(b, c.tmp[b][:], c.r_tmp[b])
                P.add("dve", lambda e, h=h, b=b: e.scalar_tensor_tensor(aoT[:, h, :], c.y[:, h, :], c.lamc[:, 3:4],
                                                                        c.tmp[b][:], ALU.mult, ALU.mult),
                      r=[c.r_y[h], c.r_tmp[b], c.r_lam], w=[r_aoT[h]])

        nI = len(iters)
        emit_load(0)
        for i in range(nI + 1):
            if i < nI:
                d = iters[i]
                emit_scores(i, d)
            if i >= 1:
                dp = iters[i - 1]
                emit_av(i - 1, dp)
                if dp["last"]:
                    emit_combine(dp["h"])
                if i < nI and iters[i]["load"] != dp["load"] and iters[i]["load"] + 1 < len(loads):
                    emit_load(iters[i]["load"] + 1)
            elif len(loads) > 1:
                emit_load(1)
        emit_subln()
        out_proj(("do",), aoT, r_aoT, 1, 3)

    def load_x(src, ts, rs=(), buf=0):
        xT, r_x = c.xTs[buf], c.r_xs2[buf]
        P.add("sp", lambda e: e.dma_start(out=xT[:], in_=src[:, :, ts]), r=list(rs), w=r_x, dma="xin%d" % buf)

    def use_x(buf):
        c.xT, c.r_x = c.xTs[buf], c.r_xs2[buf]

    def store_x(dst, ts, rdst):
        xT, r_x = c.xT, c.r_x
        P.add("act", lambda e: e.dma_start(out=dst[:, :, ts], in_=xT[:]), r=r_x, w=[rdst], dma="xout")

    def tile_loop(src, rs, body):
        load_x(src, slice(0, TT), rs, 0)
        for t in range(NT):
            ts = slice(t * TT, (t + 1) * TT)
            if t + 1 < NT:
                load_x(src, slice((t + 1) * TT, (t + 2) * TT), rs, (t + 1) % 2)
            use_x(t % 2)
            body(t, ts)

    def run_stage(st, t):
        if st[0] == "ffn":
            ffn(st[1], st[2])
        elif st[0] == "ple":
            ple(st[1], t)
        elif st[0] == "mlstm":
            mlstm(st[1])
        elif st[0] == "kvproj":
            kvproj(t)
        elif st[0] == "attn":
            attn(t)

    if phases is None:
        drip(upto={k_ for k_, _, _, _ in cast_chunks})
        flat = [s_ for p_ in stages for s_ in (p_ if isinstance(p_, list) else [p_])]
        names = [s_[0] for s_ in flat]
        if "mlstm" in names:
            mlstm_state_init_zero()
            if any(s_[0] == "mlstm" and s_[1] for s_ in flat):
                P.add("act", lambda e: e.activation(c.Cnb[:], Cn4, AF.Copy), r=[c.r_Cn], w=[c.r_Cnb])
        if "attn" in names:
            attn_setup()
        passes = stages if (stages and isinstance(stages[0], list)) else [stages]
        for pi, pst in enumerate(passes):
            def body(t, ts, pst=pst, pi=pi):
                for st in pst:
                    run_stage(st, t)
                if pi == len(passes) - 1:
                    store_x(out_d, ts, c.r_out)
            tile_loop(xT_d, [], body)
            P.barrier()
    else:
        mlstm_state_init_zero()
        attn_setup()

        def body_a(t, ts):
            ffn(0, 0)
            store_x(xs_d, ts, c.r_xs)
            mlstm(False)
        c.drip_n, c.drip_every = 1, 8
        tile_loop(xT_d, [], body_a)
        drip(upto=PH_B)
        mlstm_exchange()

        def body_b(t, ts):
            mlstm(True)
            ffn(0, 1)
            ple(0, t)
            store_x(xs_d, ts, c.r_xs)
            kvproj(t)
        c.drip_n, c.drip_every = 1, 8
        tile_loop(xs_d, [c.r_xs], body_b)
        drip(upto={k_ for k_, _, _, _ in cast_chunks})
        c.drip_n = 0
        P.barrier()

        def body_c(t, ts):
            ffn(1, 0)
            attn(t)
            ffn(1, 1)
            ple(1, t)
            store_x(out_d, ts, c.r_out)
        tile_loop(xs_d, [c.r_xs], body_c)
    P.add("act", None, r=[c.r_out])
    P.emit()
    P.close()
    return nc


def to_fm(a):
    t, ch = a.shape
    return np.ascontiguousarray(a.reshape(t, ch // 128, 128).transpose(2, 1, 0))


def from_fm(a):
    p, k, t = a.shape
    return a.transpose(2, 1, 0).reshape(t, k * p)


def make_in_maps(inp, x_override=None, ncores=NCORES):
    x = inp["x"] if x_override is None else x_override
    wall = pack_weights(inp).reshape(-1, ROW)
    gc = gcols_host(inp)
    cst = consts_host()
    maps = []
    for core in range(ncores):
        b, h = core // 2, core % 2
        sl = slice(h * TOK, (h + 1) * TOK)
        maps.append({
            "xT": to_fm(np.asarray(x[b, sl])),
            "pT": np.stack([to_fm(np.asarray(inp["p"][l, b, sl])) for l in range(2)]),
            "gcols": gc,
            "cst": cst,
            "prm": prm_host(inp, core),
            "wall": wall,
        })
    return maps


def gather_out(results):
    out = np.zeros((4, 8192, D), np.float32)
    for core in range(len(results)):
        b, h = core // 2, core % 2
        out[b, h * TOK:(h + 1) * TOK] = from_fm(np.asarray(results[core]["outT"]).reshape(128, KC, TOK))
    return out


def kernel(**inputs):
    inp = {k: np.asarray(v) for k, v in inputs.items()}
    nc = build(None, phases=True)
    res = run_bass_kernel_spmd(nc, make_in_maps(inp), core_ids=list(range(NCORES)))
    return gather_out(res.results)
```

```python
import numpy as np
from contextlib import ExitStack
import concourse.bass as bass
import concourse.mybir as mybir
from concourse.bass_utils import run_bass_kernel_spmd

F32, BF16 = mybir.dt.float32, mybir.dt.bfloat16
AF = mybir.ActivationFunctionType
ALU = mybir.AluOpType
AX = mybir.AxisListType

NCORES = 8
D = 1024
KC = 8
DFF = 2816
FC = 22
TOK = 4096
TT = 512
EPS = 1e-6

ENGS = ("pe", "act", "dve", "pool", "sp")
SEM_LIMIT = 30000


class Res:
    __slots__ = ("name", "w", "rs")

    def __init__(self, name):
        self.name = name
        self.w = None
        self.rs = []


class Op:
    __slots__ = ("eng", "fn", "deps", "sig", "sem", "val", "dma", "ndma", "name", "inc")


class Prog:
    def __init__(self, nc):
        self.nc = nc
        self.q = {e: [] for e in ENGS}
        self.dma_cnt = {}
        self.stack = ExitStack()
        self.sems = {}
        self.nsem = 0
        self.last_dma = {}

    def sbuf(self, name, shape, dt):
        return self.stack.enter_context(self.nc.sbuf_tensor(name, list(shape), dt))

    def psum(self, name, shape=(128, 512), dt=F32):
        return self.stack.enter_context(self.nc.psum_tensor(name, list(shape), dt))

    def sem(self, key):
        if key not in self.sems:
            self.sems[key] = self.stack.enter_context(self.nc.semaphore("s_%d" % self.nsem))
            self.nsem += 1
        return self.sems[key]

    def add(self, eng, fn, r=(), w=(), dma=None, ndma=1, name="", inc=16):
        op = Op()
        op.eng, op.fn, op.dma, op.ndma, op.name = eng, fn, dma, ndma, name
        op.inc = inc
        op.sig = False
        op.sem = None
        op.val = 0
        raw, oth = set(), set()
        for x in r:
            if x.w is not None:
                raw.add(x.w)
        for x in w:
            if x.w is not None:
                oth.add(x.w)
            oth.update(x.rs)
        deps = []
        for d in raw | oth:
            if d is op:
                continue
            if d.dma is None and dma is None and d.eng == eng:
                if eng == "pe":
                    continue
            deps.append(d)
        if dma is not None:
            prev = self.last_dma.get(dma)
            if prev is not None and prev not in deps:
                deps.append(prev)
            self.last_dma[dma] = op
        op.deps = deps
        for d in deps:
            d.sig = True
        for x in r:
            x.rs.append(op)
        for x in w:
            x.w = op
            x.rs = []
        self.q[eng].append(op)
        return op

    def barrier(self):
        lasts = []
        for e in ENGS:
            for op in reversed(self.q[e]):
                if op.fn is not None:
                    lasts.append(op)
                    break
        for d in self.last_dma.values():
            if d not in lasts:
                lasts.append(d)
        for e in ENGS:
            op = Op()
            op.eng, op.fn, op.dma, op.ndma, op.name = e, None, None, 1, "barrier"
            op.inc = 16
            op.sig, op.sem, op.val = False, None, 0
            op.deps = list(lasts)
            for d in op.deps:
                d.sig = True
            self.q[e].append(op)

    def finalize(self):
        for e in ENGS:
            cnt = 0
            epoch = 0
            for op in self.q[e]:
                if op.dma is not None:
                    c = self.dma_cnt.get(op.dma, 0) + op.ndma
                    self.dma_cnt[op.dma] = c
                    op.sem = self.sem(("dma", op.dma))
                    op.val = op.inc * c
                elif op.sig:
                    if cnt >= SEM_LIMIT:
                        epoch += 1
                        cnt = 0
                    cnt += 1
                    op.sem = self.sem((e, epoch))
                    op.val = cnt

    def emit(self):
        self.finalize()
        nc = self.nc
        prog = self

        def run(ename, eng):
            waited = {}
            for op in prog.q[ename]:
                for d in op.deps:
                    k = id(d.sem)
                    if waited.get(k, 0) >= d.val:
                        continue
                    eng.wait_ge(d.sem, d.val)
                    waited[k] = d.val
                if op.fn is None:
                    continue
                ins = op.fn(eng)
                if op.dma is not None:
                    if not isinstance(ins, (list, tuple)):
                        ins = [ins]
                    assert len(ins) == op.ndma, (op.name, len(ins), op.ndma)
                    for i in ins:
                        i.then_inc(op.sem, op.inc)
                elif op.sem is not None:
                    ins.then_inc(op.sem, 1)

        with nc.Block() as block:
            @block.tensor
            def _(e):
                run("pe", e)

            @block.scalar
            def _(e):
                run("act", e)

            @block.vector
            def _(e):
                run("dve", e)

            @block.gpsimd
            def _(e):
                run("pool", e)

            @block.sync
            def _(e):
                run("sp", e)

    def close(self):
        self.stack.close()


ROW = 2048


class WPack:
    def __init__(self):
        self.parts = []
        self.off = {}
        self.n = 0

    def put(self, key, arr):
        if WKEYS is not None and key not in WKEYS:
            return
        a = np.ascontiguousarray(arr, dtype=np.float32).reshape(-1)
        pad = (-a.size) % ROW
        self.off[key] = (self.n, a.size)
        self.parts.append(a)
        if pad:
            self.parts.append(np.zeros(pad, np.float32))
        self.n += a.size + pad

    def flat(self):
        return np.concatenate(self.parts)


LVL = 99
WKEYS = None


def w_layout_sizes():
    sizes = []
    for l in range(2):
        for f in range(2):
            sizes.append((("w1", l, f), FC * 128 * KC * 256))
            sizes.append((("w2", l, f), KC * 128 * FC * 128))
        sizes.append((("pg", l), KC * 128 * KC * 128))
        sizes.append((("pp", l), KC * 128 * 2 * 128))
        if l == 0:
            sizes.append((("mq",), 4 * 128 * KC * 128))
            sizes.append((("mk",), 4 * 128 * KC * 128))
            sizes.append((("mt",), 5 * 128 * KC * 512))
            sizes.append((("mg",), 128 * KC * 8))
            sizes.append((("mo",), KC * 128 * KC * 128))
            sizes.append((("kk",), KC * 128 * KC * 128))
            sizes.append((("kv",), 2 * 128 * KC * 512))
        else:
            sizes.append((("dq",), KC * 128 * KC * 128))
            sizes.append((("do",), KC * 128 * KC * 128))
    out = {}
    off = 0
    order = []
    if WKEYS is not None:
        sizes = [(k, n) for k, n in sizes if k in WKEYS]
    for k, n in sizes:
        n_pad = n + ((-n) % ROW)
        out[k] = (off, n)
        order.append((k, off, n_pad))
        off += n_pad
    return out, order, off


def pack_weights(inp):
    wp = WPack()
    for l in range(2):
        for f in range(2):
            w_in = inp["w_ffn_in"][l, f]
            w1 = w_in.reshape(KC, 128, 2, FC, 128).transpose(3, 1, 0, 2, 4)
            wp.put(("w1", l, f), w1)
            w_out = inp["w_ffn_out"][l, f]
            w2 = w_out.reshape(FC, 128, KC, 128).transpose(2, 1, 0, 3)
            wp.put(("w2", l, f), w2)
        wg = inp["w_ple_gate"][l]
        wp.put(("pg", l), wg.reshape(KC, 128, KC, 128).transpose(2, 1, 0, 3))
        wq = inp["w_ple_proj"][l]
        wp.put(("pp", l), wq.reshape(2, 128, KC, 128).transpose(2, 1, 0, 3))
        fm = lambda w, n: w.reshape(KC, 128, n, 128).transpose(2, 1, 0, 3)
        tm = lambda w, n: w.reshape(KC, 128, n, 512).transpose(2, 1, 0, 3)
        if l == 0:
            wi = inp["mlstm_w_in"][0]
            wp.put(("mq",), fm(wi[:, 0:512], 4))
            wp.put(("mk",), fm(wi[:, 512:1024], 4))
            wp.put(("mt",), tm(wi[:, 512:3072], 5))
            wp.put(("mg",), wi[:, 3072:3080].reshape(KC, 128, 8).transpose(1, 0, 2))
            wp.put(("mo",), fm(inp["mlstm_w_out"][0], KC))
            wp.put(("kk",), fm(inp["w_kv"][:, 0:1024], KC))
            wp.put(("kv",), tm(inp["w_kv"][:, 1024:2048], 2))
        else:
            wp.put(("dq",), fm(inp["diff_w_q"][0], KC))
            wp.put(("do",), fm(inp["diff_w_out"][0], KC))
    offs, order, total = w_layout_sizes()
    assert total == wp.n, (total, wp.n)
    for k in offs:
        assert offs[k] == wp.off[k], (k, offs[k], wp.off[k])
    return wp.flat()


def gcols_host(inp):
    cols = []
    ng = inp["norm_g"]
    cols.append(ng.reshape(2, 8, KC, 128).transpose(3, 0, 1, 2).reshape(128, 2 * 8 * KC))
    cols.append(inp["kv_norm"].reshape(KC, 128).T)
    return np.ascontiguousarray(np.concatenate(cols, axis=1), dtype=np.float32)


NG = 2 * 8 * KC + KC


def gcol(l, i, kc):
    return (l * 8 + i) * KC + kc


NKV = 2048 * 512
NST = 4 * 257 + 4
DKS = 128 ** -0.5
NEG = -30000.0
NOCC = False
C_ID, C_U, C_ONE, C_MC4, C_MT4, C_MD = 0, 128, 256, 384, 896, 1408
NCST = 1408 + 4 * 512
P_BG, P_HN, P_LAM, P_SUBLN, P_SEL, P_BIASA = 0, 8, 1032, 1288, 1289, 1290
NPRM = 1291


def consts_host():
    p = np.arange(128)[:, None]
    j = np.arange(128)[None, :]
    ident = (p == j).astype(np.float32)
    U = (p <= j).astype(np.float32)
    ones = np.ones((128, 128), np.float32)
    maskC = np.where(j <= p, 0.0, -1e30).astype(np.float32)
    maskT = np.where(j >= p, 0.0, 1e30).astype(np.float32)
    i = np.arange(512)[None, :]
    md = [np.where(i >= 128 * j4 + p, 0.0, NEG).astype(np.float32) for j4 in range(4)]
    return np.ascontiguousarray(np.concatenate(
        [ident, U, ones, np.tile(maskC, (1, 4)), np.tile(maskT, (1, 4))] + md, axis=1))


def prm_host(inp, core):
    a = np.zeros((128, NPRM), np.float32)
    a[:, P_BG:P_BG + 8] = inp["mlstm_b_gates"].reshape(1, 8)
    a[:, P_HN:P_HN + 1024] = inp["mlstm_head_norm"].reshape(1, 1024)
    a[:, P_LAM:P_LAM + 256] = inp["diff_lambda"].reshape(1, 256)
    a[:, P_SUBLN] = inp["diff_subln"].reshape(128)
    a[:, P_SEL] = float(core % 2)
    a[:, P_BIASA] = 0.0 if core % 2 == 1 else NEG
    return a


class Ctx:
    pass


def build(stages, NT=TOK // TT, phases=None):
    nc = bass.Bass("TRN2", target_bir_lowering=False)
    P = Prog(nc)
    c = Ctx()
    offs, order, wtotal = w_layout_sizes()
    import math
    LAM_INIT = 0.8 - 0.6 * math.exp(-0.3 * 1)

    xT_d = nc.dram_tensor("xT", [128, KC, TOK], F32, kind="ExternalInput").ap()
    pT_d = nc.dram_tensor("pT", [2, 128, 2, TOK], F32, kind="ExternalInput").ap()
    gc_d = nc.dram_tensor("gcols", [128, NG], F32, kind="ExternalInput").ap()
    cst_d = nc.dram_tensor("cst", [128, NCST], F32, kind="ExternalInput").ap()
    prm_d = nc.dram_tensor("prm", [128, NPRM], F32, kind="ExternalInput").ap()
    wall_d = nc.dram_tensor("wall", [wtotal // ROW, ROW], F32, kind="ExternalInput").ap()
    out_d = nc.dram_tensor("outT", [128, KC, TOK], F32, kind="ExternalOutput").ap()
    wbf_d = nc.dram_tensor("wbf", [wtotal // ROW, ROW], BF16).ap()
    wbf_flat = wbf_d.rearrange("a b -> (a b)")
    xs_d = nc.dram_tensor("xs", [128, KC, TOK], F32).ap()
    st_own = nc.dram_tensor("st_own", [128, NST], F32)
    st_all = nc.dram_tensor("st_all", [256, NST], F32)
    kv_own = [nc.dram_tensor("kv_own%d" % t, [2048, 512], BF16) for t in range(8)]
    kv_all = [nc.dram_tensor("kv_all%d" % t, [4096, 512], BF16) for t in range(8)]
    PAIRS = [[0, 1], [2, 3], [4, 5], [6, 7]]

    def rwb(key):
        return r_wparts[key] if key in r_wparts else [r_wbf[key]]

    def wview(key, pattern, **kw):
        off, n = offs[key]
        return wbf_flat[off:off + n].rearrange(pattern, **kw)

    c.xTs = [P.sbuf("xT_sb%d" % i, [128, KC, TT], F32) for i in range(2)]
    c.xT = c.xTs[0]
    c.sq = P.sbuf("sq", [128, KC, TT], BF16)
    c.rstd = P.sbuf("rstd", [128, TT], F32)
    c.xn = P.sbuf("xn", [128, KC, TT], BF16)
    NW1, NW2, NWT = 5, 3, 2
    c.w1 = [P.sbuf("w1_%d" % i, [128, KC, 256], BF16) for i in range(NW1)]
    c.w2 = [P.sbuf("w2_%d" % i, [128, FC, 128], BF16) for i in range(NW2)]
    c.wt = [P.sbuf("wt_%d" % i, [128, KC, 512], BF16) for i in range(NWT)]
    c.sg = [P.sbuf("sg_%d" % i, [128, TT], BF16) for i in range(2)]
    c.aT = P.sbuf("aT", [128, FC, TT], BF16)
    c.y = P.sbuf("y", [128, KC, TT], F32)
    c.tmp = [P.sbuf("tmp_%d" % i, [128, TT], F32) for i in range(2)]
    c.gc = P.sbuf("gc", [128, NG], F32)
    c.hgc = P.sbuf("hgc", [128, NG], F32)
    c.ones = P.sbuf("ones", [128, 128], BF16)
    c.one1 = P.sbuf("one1", [128, 128], BF16)
    c.o128 = P.sbuf("o128", [128, 128], BF16)
    c.identb = P.sbuf("identb", [128, 128], BF16)
    c.epsc = P.sbuf("epsc", [128, 1], F32)
    c.pT = P.sbuf("pT_sb", [128, 2, TT], F32)
    c.pTb = P.sbuf("pTb", [128, 2, TT], BF16)
    c.cst = P.sbuf("cst_sb", [128, C_MD], F32)
    c.maskb = P.sbuf("maskb", [128, 4 * 512], BF16)
    c.prm = P.sbuf("prm_sb", [128, NPRM], F32)
    c.mg = P.sbuf("mg", [128, KC, 8], BF16)
    A = P.sbuf("arena", [128, 12288], BF16)
    c.gt = P.sbuf("gt", [128, 4, 8], F32)
    c.sm = P.sbuf("sm", [128, 32, 4], F32)
    c.dg = P.sbuf("dg", [128, 4, 128], F32)
    c.e1 = P.sbuf("e1", [128, 4, 128], F32)
    c.wT = P.sbuf("wT", [128, 4, 128], F32)
    c.sib = P.sbuf("sib", [128, 4, 128], F32)
    c.junk = P.sbuf("junk", [128, 256], F32)
    c.v1 = A[:, 0:4112].rearrange("p (c h v) -> p c h v", c=4, h=4, v=257)
    c.Cnb = A[:, 4112:5140].rearrange("p (h v) -> p h v", h=4, v=257)
    c.qs = A[:, 5140:5652].rearrange("p (h t) -> p h t", h=4, t=128)
    c.qkw = A[:, 5652:6164].rearrange("p (h t) -> p h t", h=4, t=128)
    c.ksc = A[:, 6164:6676].rearrange("p (h t) -> p h t", h=4, t=128)
    c.og = A[:, 6676:7700]
    c.Cn = A[:, 7700:7700 + 2 * NST].bitcast(F32)
    c.hh = A[:, 9764:11812].bitcast(F32).rearrange("p (h v) -> p h v", h=4, v=256)
    c.kb = [A[:, 2048 * i:2048 * (i + 1)] for i in range(2)]
    c.vb = [A[:, 4096 + 2048 * i:4096 + 2048 * (i + 1)].rearrange("p (k d) -> p k d", k=16, d=128) for i in range(2)]
    c.ptpair = [A[:, 8192 + 1024 * i:8192 + 1024 * (i + 1)] for i in range(2)]
    c.pt = [[c.ptpair[i][:, 512 * m:512 * (m + 1)] for i in range(2)] for m in range(2)]
    c.msall = A[:, 10240:12288].bitcast(F32)
    c.ms = [c.msall[:, 512 * m:512 * (m + 1)] for m in range(2)]
    c.lamc = P.sbuf("lamc", [128, 4], F32)
    c.pairs = [P.psum("bankpair%d" % i, (128, 1024)) for i in range(4)]
    c.bank = []
    for i in range(4):
        c.bank += [c.pairs[i][:, 0:512], c.pairs[i][:, 512:1024]]
    c.pg, c.pu, c.py, c.pss = c.bank[0:2], c.bank[2:4], c.bank[4:6], c.bank[6]

    R = lambda n: Res(n)
    c.r_xs2 = [[R("x%d_%d" % (i, m)) for m in range(KC)] for i in range(2)]
    c.r_x = c.r_xs2[0]
    c.r_sq = [R("sq%d" % k) for k in range(KC)]
    c.r_rstd = R("rstd")
    c.r_xn = [R("xn%d" % k) for k in range(KC)]
    c.r_w1 = [R("w1") for _ in range(NW1)]
    c.r_w2 = [R("w2") for _ in range(NW2)]
    c.r_wt = [R("wt") for _ in range(NWT)]
    c.r_sg = [R("sg") for _ in range(2)]
    c.r_a = [R("a") for _ in range(FC)]
    c.r_y = [R("y") for _ in range(KC)]
    c.r_tmp = [R("tmp") for _ in range(2)]
    c.r_gc, c.r_ones = R("gc"), R("ones")
    c.r_pT, c.r_pTb = R("pT"), R("pTb")
    c.r_bank = [R("bank%d" % i) for i in range(8)]
    c.r_pg, c.r_pu, c.r_py, c.r_pss = c.r_bank[0:2], c.r_bank[2:4], c.r_bank[4:6], c.r_bank[6]
    c.r_out, c.r_xs = R("out"), R("xs")
    c.r_cst, c.r_prm = R("cst"), R("prm")
    c.r_mg, c.r_v1, c.r_gt, c.r_Cn, c.r_Cnb = R("mg"), R("v1"), [R("gt%d" % i) for i in range(4)], R("Cn"), R("Cnb")
    c.r_sm = [R("sm%d" % i) for i in range(32)]
    c.r_dg, c.r_e1, c.r_wT, c.r_sib = R("dg"), R("e1"), R("wT"), R("sib")
    c.r_qs, c.r_qkw, c.r_ksc, c.r_og, c.r_junk = R("qs"), R("qkw"), R("ksc"), R("og"), R("junk")
    c.r_hh = [R("hh%d" % i) for i in range(4)]
    c.r_kb = [R("kb") for _ in range(2)]
    c.r_vb = [R("vb") for _ in range(2)]
    c.r_pt = [[R("pt") for _ in range(2)] for _ in range(2)]
    c.r_ms = [R("ms") for _ in range(2)]
    c.r_lam = R("lam")
    c.r_st_own, c.r_st_all = R("st_own"), R("st_all")
    c.r_kv_own = [R("kvo") for _ in range(8)]
    c.r_kv_all = [R("kva") for _ in range(8)]
    r_wbf = {k: R("wbf") for k in offs}
    ALLK = [("mq",), ("mk",), ("mt",), ("mg",), ("mo",), ("kk",), ("kv",), ("dq",), ("do",)]
    for l in range(2):
        ALLK += [("w1", l, 0), ("w2", l, 0), ("w1", l, 1), ("w2", l, 1), ("pg", l), ("pp", l)]
    for kk in ALLK:
        if kk not in offs:
            offs[kk] = (0, 1 << 30)
            r_wbf[kk] = R("wbf_missing")
    c.w1_i = 0
    c.w2_i = 0
    c.wt_i = 0
    c.kb_i = 0

    def cs(off, n=128):
        return c.cst[:, off:off + n]

    ident, Umat, onesf = cs(C_ID), cs(C_U), cs(C_ONE)

    P.add("sp", lambda e: e.dma_start(out=c.gc[:], in_=gc_d[:, :]), w=[c.r_gc], dma="gc")
    P.add("sp", lambda e: e.dma_start(out=c.cst[:], in_=cst_d[:, 0:C_MD]), w=[c.r_cst], dma="cst")
    ystage = c.y[:, 0:4, :].rearrange("p a t -> p (a t)")
    P.add("sp", lambda e: e.dma_start(out=ystage, in_=cst_d[:, C_MD:C_MD + 2048]), w=c.r_y[0:4], dma="cst")
    P.add("dve", lambda e: e.tensor_copy(c.maskb[:], ystage), r=c.r_y[0:4], w=[c.r_cst])
    P.add("sp", lambda e: e.dma_start(out=c.prm[:], in_=prm_d[:, :]), w=[c.r_prm], dma="prm")
    P.add("dve", lambda e: e.memset(c.ones[:], 1.0 / D), w=[c.r_ones])
    P.add("dve", lambda e: e.memset(c.one1[:], 1.0), w=[c.r_ones])
    P.add("dve", lambda e: e.memset(c.o128[:], 1.0 / 128), w=[c.r_ones])
    P.add("dve", lambda e: e.memset(c.epsc[:], EPS), w=[c.r_ones])
    P.add("dve", lambda e: e.memset(c.v1[:], 1.0), w=[c.r_v1])
    P.add("dve", lambda e: e.tensor_copy(c.identb[:], ident), r=[c.r_cst], w=[c.r_ones])
    P.add("dve", lambda e: e.tensor_scalar(c.hgc[:], c.gc[:], 0.5, None, ALU.mult),
          r=[c.r_gc], w=[c.r_gc])
    r_cast = R("castchain")
    c.w1first = None
    use_order = [("w1", 0, 0), ("w2", 0, 0), ("mg",), ("mt",), ("mq",), ("mk",), ("mo",),
                 ("w1", 0, 1), ("w2", 0, 1), ("pg", 0), ("pp", 0), ("kk",), ("kv",),
                 ("w1", 1, 0), ("w2", 1, 0), ("dq",), ("do",), ("w1", 1, 1), ("w2", 1, 1), ("pg", 1), ("pp", 1)]
    omap = {k: (off, n_pad) for k, off, n_pad in order}
    order2 = [(k, omap[k][0], omap[k][1]) for k in use_order if k in omap]
    assert len(order2) == len(order)
    CROWS = 704
    cast_chunks = []
    for k, off, n_pad in order2:
        r0, r1 = off // ROW, (off + n_pad) // ROW
        if k == ("w1", 0, 0):
            c.w1first = []
        for ra in range(r0, r1, CROWS):
            rb = min(ra + CROWS, r1)
            rp = R("wbf_part")
            if k == ("w1", 0, 0):
                c.w1first.append((ra - r0, rb - r0, rp))
            cast_chunks.append((k, ra, rb, rp))
    c.cast_i = 0
    c.cast_n = 0
    c.drip_n = 0
    c.drip_every = 4
    r_wparts = {}
    for k, ra, rb, rp in cast_chunks:
        r_wparts.setdefault(k, []).append(rp)

    def drip(n=None, upto=None, after=()):
        while c.cast_i < len(cast_chunks):
            k, ra, rb, rp = cast_chunks[c.cast_i]
            if upto is not None:
                if k not in upto:
                    break
            elif n is not None:
                if n <= 0:
                    break
                n -= 1
            q = c.cast_n % 4
            c.cast_n += 1
            c.cast_i += 1
            P.add("pool", lambda e, ra=ra, rb=rb: e.dma_start(out=wbf_d[ra:rb, :], in_=wall_d[ra:rb, :]),
                  r=list(after), w=[rp], dma="wcast%d" % q)

    PH_A = {("w1", 0, 0), ("w2", 0, 0), ("mg",), ("mt",)}
    PH_B = PH_A | {("mq",), ("mk",), ("mo",), ("w1", 0, 1), ("w2", 0, 1), ("pg", 0), ("pp", 0), ("kk",), ("kv",)}
    drip(upto=PH_A)

    def rsqrt_bank(bi, dst, r_dst):
        P.add("act", lambda e: e.activation(dst, c.bank[bi][:], AF.Ln, bias=c.epsc[:, 0:1]),
              r=[c.r_ones], w=[c.r_bank[bi], r_dst])
        P.add("act", lambda e: e.activation(dst, dst, AF.Exp, scale=-0.5), r=[r_dst], w=[r_dst])

    def load_w1slot(dmafn, rws, ndma=1):
        wi = c.w1_i % NW1
        c.w1_i += 1
        wtile = c.w1[wi]
        P.add("sp", lambda e: dmafn(e, wtile), r=rws, w=[c.r_w1[wi]], dma="w1_%d" % wi, ndma=ndma)
        return wtile, c.r_w1[wi]

    def load_wtslot(src, rws):
        wi = c.wt_i % NWT
        c.wt_i += 1
        wtile = c.wt[wi]
        P.add("sp", lambda e: e.dma_start(out=wtile[:], in_=src), r=rws, w=[c.r_wt[wi]], dma="wt_%d" % wi)
        return wtile, c.r_wt[wi]

    def prenorm(gbase):
        xT, r_x = c.xT, c.r_x
        for kc in range(KC):
            P.add("act", lambda e, kc=kc: e.activation(c.sq[:, kc, :], xT[:, kc, :], AF.Square),
                  r=[r_x[kc]], w=[c.r_sq[kc]])
        for kc in range(KC):
            P.add("pe", lambda e, kc=kc: e.matmul(c.pss[:], c.ones[:], c.sq[:, kc, :],
                                                  start=(kc == 0), stop=(kc == KC - 1)),
                  r=[c.r_sq[kc], c.r_ones], w=[c.r_pss])
        rsqrt_bank(6, c.rstd[:], c.r_rstd)
        for kc in range(KC):
            col = gbase + kc
            P.add("dve", lambda e, kc=kc, col=col: e.scalar_tensor_tensor(
                c.xn[:, kc, :], xT[:, kc, :], c.gc[:, col:col + 1], c.rstd[:], ALU.mult, ALU.mult),
                r=[r_x[kc], c.r_rstd, c.r_gc], w=[c.r_xn[kc]])

    def postnorm_add(gl, gi, half):
        xT, r_x = c.xT, c.r_x
        for m in range(KC):
            P.add("pe", lambda e, m=m: e.matmul(c.pss[:], c.ones[:], c.sq[:, m, :],
                                                start=(m == 0), stop=(m == KC - 1)),
                  r=[c.r_sq[m], c.r_ones], w=[c.r_pss])
        rsqrt_bank(6, c.rstd[:], c.r_rstd)
        gsrc = c.hgc if half else c.gc
        for m in range(KC):
            col = gcol(gl, gi, m)
            P.add("dve", lambda e, m=m, col=col: e.scalar_tensor_tensor(
                c.y[:, m, :], c.y[:, m, :], gsrc[:, col:col + 1], c.rstd[:], ALU.mult, ALU.mult),
                r=[c.r_rstd, c.r_gc], w=[c.r_y[m]])
            eng = "pool" if m % 2 == 0 else "dve"
            P.add(eng, lambda e, m=m: e.tensor_tensor(
                xT[:, m, :], xT[:, m, :], c.y[:, m, :], ALU.add),
                r=[c.r_y[m]], w=[r_x[m]])

    def fm_proj(key, n, src, r_src, evac):
        wv = wview(key, "(m p k c) -> p m k c", m=n, p=128, k=KC, c=128)
        for i in range(n):
            wtile, rw = load_w1slot(lambda e, wtile, i=i: e.dma_start(out=wtile[:, :, 0:128], in_=wv[:, i]),
                                    rwb(key))
            b = i % 2
            for kc in range(KC):
                P.add("pe", lambda e, kc=kc, wtile=wtile, b=b: e.matmul(
                    c.pg[b][:], wtile[:, kc, 0:128], src[:, kc, :],
                    start=(kc == 0), stop=(kc == KC - 1)),
                    r=[rw, r_src[kc]], w=[c.r_pg[b]])
            evac(i, b)

    def out_proj(key, src, r_src, gl, gi):
        def evac(m, b):
            P.add("dve", lambda e: e.tensor_copy(c.y[:, m, :], c.pg[b][:]), w=[c.r_pg[b], c.r_y[m]])
            P.add("act", lambda e: e.activation(c.sq[:, m, :], c.y[:, m, :], AF.Square),
                  r=[c.r_y[m]], w=[c.r_sq[m]])
        fm_proj(key, KC, src, r_src, evac)
        postnorm_add(gl, gi, False)

    def ffn(l, f):
        prenorm(gcol(l, 0 if f == 0 else 4, 0))
        w1v = wview(("w1", l, f), "(j p k c) -> j p k c", j=FC, p=128, k=KC, c=256)
        w2v = wview(("w2", l, f), "(m p k c) -> m p k c", m=KC, p=128, k=FC, c=128)
        rw = rwb(("w1", l, f))
        rw2 = rwb(("w2", l, f))
        for j in range(FC):
            rwj = rw
            if (l, f) == (0, 0) and c.w1first is not None:
                rwj = [rp for a, b_, rp in c.w1first if a < 128 * (j + 1) and b_ > 128 * j]
            wtile, rws = load_w1slot(lambda e, wtile, j=j: e.dma_start(out=wtile[:], in_=w1v[j]), rwj)
            b = j % 2
            for kc in range(KC):
                P.add("pe", lambda e, kc=kc, wtile=wtile, b=b: e.matmul(
                    c.pg[b][:], wtile[:, kc, 0:128], c.xn[:, kc, :],
                    start=(kc == 0), stop=(kc == KC - 1)),
                    r=[rws, c.r_xn[kc]], w=[c.r_pg[b]])
            for kc in range(KC):
                P.add("pe", lambda e, kc=kc, wtile=wtile, b=b: e.matmul(
                    c.pu[b][:], wtile[:, kc, 128:256], c.xn[:, kc, :],
                    start=(kc == 0), stop=(kc == KC - 1)),
                    r=[rws, c.r_xn[kc]], w=[c.r_pu[b]])
            P.add("act", lambda e, b=b: e.activation(c.sg[b][:], c.pg[b][:], AF.Silu),
                  w=[c.r_pg[b], c.r_sg[b]])
            P.add("dve", lambda e, j=j, b=b: e.tensor_tensor(
                c.aT[:, j, :], c.sg[b][:], c.pu[b][:], ALU.mult),
                r=[c.r_sg[b]], w=[c.r_pu[b], c.r_a[j]])
        for m in range(KC):
            wi = c.w2_i % NW2
            c.w2_i += 1
            P.add("sp", lambda e, m=m, wi=wi: e.dma_start(out=c.w2[wi][:], in_=w2v[m]),
                  r=rw2, w=[c.r_w2[wi]], dma="w2_%d" % wi)
            b = m % 2
            for j in range(FC):
                P.add("pe", lambda e, j=j, m=m, wi=wi, b=b: e.matmul(
                    c.py[b][:], c.w2[wi][:, j, :], c.aT[:, j, :],
                    start=(j == 0), stop=(j == FC - 1)),
                    r=[c.r_w2[wi], c.r_a[j]], w=[c.r_py[b]])
            P.add("dve", lambda e, m=m, b=b: e.tensor_copy(c.y[:, m, :], c.py[b][:]),
                  w=[c.r_py[b], c.r_y[m]])
            P.add("act", lambda e, m=m: e.activation(c.sq[:, m, :], c.y[:, m, :], AF.Square),
                  r=[c.r_y[m]], w=[c.r_sq[m]])
        postnorm_add(l, 1 if f == 0 else 5, True)

    def ple(l, t):
        prenorm(gcol(l, 6, 0))
        P.add("sp", lambda e: e.dma_start(out=c.pT[:], in_=pT_d[l, :, :, t * TT:(t + 1) * TT]),
              w=[c.r_pT], dma="pT")
        P.add("act", lambda e: e.activation(c.pTb[:], c.pT[:], AF.Copy), r=[c.r_pT], w=[c.r_pTb])
        wgv = wview(("pg", l), "(m p k c) -> p m k c", m=KC, p=128, k=KC, c=128)
        wpv = wview(("pp", l), "(m p k c) -> p m k c", m=KC, p=128, k=2, c=128)
        for m in range(KC):
            wtile, rws = load_w1slot(lambda e, wtile, m=m: [
                e.dma_start(out=wtile[:, :, 0:128], in_=wgv[:, m]),
                e.dma_start(out=wtile[:, 0:2, 128:256], in_=wpv[:, m])],
                rwb(("pg", l)) + rwb(("pp", l)), ndma=2)
            b = m % 2
            for kc in range(KC):
                P.add("pe", lambda e, kc=kc, wtile=wtile, b=b: e.matmul(
                    c.pg[b][:], wtile[:, kc, 0:128], c.xn[:, kc, :],
                    start=(kc == 0), stop=(kc == KC - 1)),
                    r=[rws, c.r_xn[kc]], w=[c.r_pg[b]])
            for pc in range(2):
                P.add("pe", lambda e, pc=pc, wtile=wtile, b=b: e.matmul(
                    c.pu[b][:], wtile[:, pc, 128:256], c.pTb[:, pc, :],
                    start=(pc == 0), stop=(pc == 1)),
                    r=[rws, c.r_pTb], w=[c.r_pu[b]])
            P.add("act", lambda e, b=b: e.activation(c.tmp[b][:], c.pg[b][:], AF.Sigmoid),
                  w=[c.r_pg[b], c.r_tmp[b]])
            P.add("dve", lambda e, m=m, b=b: e.tensor_tensor(
                c.y[:, m, :], c.tmp[b][:], c.pu[b][:], ALU.mult),
                r=[c.r_tmp[b]], w=[c.r_pu[b], c.r_y[m]])
            P.add("act", lambda e, m=m: e.activation(c.sq[:, m, :], c.y[:, m, :], AF.Square),
                  r=[c.r_y[m]], w=[c.r_sq[m]])
        postnorm_add(l, 7, False)

    ogT, r_ogT = c.aT[:, 0:8, :], c.r_a[0:8]
    qT, r_qT = c.aT[:, 8:12, :], c.r_a[8:12]
    kT, r_kT = c.aT[:, 12:16, :], c.r_a[12:16]
    ktm, r_ktm = c.aT[:, 16:20, :], c.r_a[16:20]
    so = c.y[:].rearrange("p (c a) t -> p c (a t)", c=4, a=2)
    mst = c.Cn[:, 4 * 257:4 * 257 + 4]
    Cn4 = c.Cn[:, 0:4 * 257].rearrange("p (h v) -> p h v", h=4, v=257)

    def smv(i):
        return c.sm[:, i, :], c.r_sm[i]

    def mlstm_proj(full):
        for ch in range(4):
            b = ch % 2
            for kc in range(KC):
                P.add("pe", lambda e, kc=kc, ch=ch, b=b: e.matmul(
                    c.pg[b][:, 0:8], c.xn[:, kc, ch * 128:(ch + 1) * 128], c.mg[:, kc, :],
                    start=(kc == 0), stop=(kc == KC - 1)),
                    r=[c.r_mg, c.r_xn[kc]], w=[c.r_pg[b]])
            P.add("dve", lambda e, ch=ch, b=b: e.tensor_tensor(
                c.gt[:, ch, :], c.pg[b][:, 0:8], c.prm[:, P_BG:P_BG + 8], ALU.add),
                r=[c.r_prm], w=[c.r_pg[b], c.r_gt[ch]])

        for ch in range(4):
            mlstm_pre(ch, full)
        if full:
            def evq(h, b):
                P.add("act", lambda e: e.activation(qT[:, h, :], c.pg[b][:], AF.Copy),
                      w=[c.r_pg[b], r_qT[h]])
            fm_proj(("mq",), 4, c.xn, c.r_xn, evq)

            def evk(h, b):
                P.add("act", lambda e: e.activation(kT[:, h, :], c.pg[b][:], AF.Copy, scale=DKS),
                      w=[c.r_pg[b], r_kT[h]])
            fm_proj(("mk",), 4, c.xn, c.r_xn, evk)
        mtv = wview(("mt",), "(g p k c) -> g p k c", g=5, p=128, k=KC, c=512)
        groups = [0, 1, 2, 3, 4] if full else [0, 1, 2]
        nb = 0
        for g in groups:
            wtile, rw = load_wtslot(mtv[g], rwb(("mt",)))
            for ch in range(4):
                b = nb % 2
                nb += 1
                for kc in range(KC):
                    P.add("pe", lambda e, kc=kc, ch=ch, wtile=wtile, b=b: e.matmul(
                        c.pu[b][:], c.xn[:, kc, ch * 128:(ch + 1) * 128], wtile[:, kc, :],
                        start=(kc == 0), stop=(kc == KC - 1)),
                        r=[rw, c.r_xn[kc]], w=[c.r_pu[b]])
                if g == 0:
                    P.add("act", lambda e, ch=ch, b=b: e.activation(ktm[:, ch, :], c.pu[b][:], AF.Copy),
                          w=[c.r_pu[b], r_ktm[ch]])
                elif g in (1, 2):
                    h0 = 2 * (g - 1)
                    P.add("dve", lambda e, ch=ch, b=b, h0=h0: e.tensor_copy(
                        c.v1[:, ch, h0:h0 + 2, 0:256],
                        c.pu[b][:].rearrange("p (h v) -> p h v", h=2, v=256)),
                        w=[c.r_pu[b], c.r_v1])
                else:
                    o0 = (g - 3) * 512
                    P.add("act", lambda e, ch=ch, b=b, o0=o0: e.activation(
                        so[:, ch, o0:o0 + 512], c.pu[b][:], AF.Sigmoid),
                        w=[c.r_pu[b], c.r_y[2 * ch], c.r_y[2 * ch + 1]])
    def mlstm_pre(ch, full):
        li = c.gt[:, ch, 0:4]
        fz = c.gt[:, ch, 4:8]
        pb = 16 + 4 * ch
        (sp_, r_sp) = smv(0)
        (a_, r_a_), (nb_, r_nb), (nbL, r_nbL), (cm, r_cm) = smv(pb), smv(pb + 1), smv(pb + 2), smv(pb + 3)
        (M_, r_M), (Ml, r_Ml), (t4, r_t4), (sc_, r_sc) = smv(4), smv(5), smv(6), smv(7)
        (dcy, r_dcy), (dn, r_dn), (lim, r_lim), (ssq, r_ssq) = smv(8), smv(9), smv(10), smv(11)
        (rr, r_rr) = smv(12)
        B0, B1, B2 = c.bank[0], c.bank[1], c.bank[2]
        rB0, rB1, rB2 = c.r_bank[0], c.r_bank[1], c.r_bank[2]
        B4, B5, rB4, rB5 = c.bank[4], c.bank[5], c.r_bank[4], c.r_bank[5]
        P.add("act", lambda e: e.activation(sp_, fz, AF.Exp, scale=-1.0), r=[c.r_gt[ch]], w=[r_sp])
        P.add("act", lambda e: e.activation(sp_, sp_, AF.Ln, bias=1.0), r=[r_sp], w=[r_sp])
        P.add("pe", lambda e: e.matmul(B4[:, 0:4], Umat, sp_, start=True, stop=True),
              r=[r_sp, c.r_cst], w=[rB4])
        P.add("pe", lambda e: e.matmul(B4[:, 4:8], onesf, sp_, start=True, stop=True),
              r=[r_sp, c.r_cst], w=[rB4])
        P.add("dve", lambda e: e.tensor_tensor(a_, B4[:, 0:4], li, ALU.add), r=[c.r_gt[ch]], w=[rB4, r_a_])
        P.add("dve", lambda e: e.tensor_copy(nb_, B4[:, 0:4]), w=[rB4, r_nb])
        P.add("dve", lambda e: e.tensor_copy(nbL, B4[:, 4:8]), w=[rB4, r_nbL])
        for h in range(4):
            P.add("dve", lambda e, h=h: e.tensor_scalar(c.dg[:, h, :], ident, a_[:, h:h + 1], None, ALU.mult),
                  r=[c.r_cst, r_a_], w=[c.r_dg])
        P.add("pe", lambda e: e.matmul(B5[:], onesf, c.dg[:].rearrange("p h s -> p (h s)"),
                                       start=True, stop=True), r=[c.r_dg, c.r_cst], w=[rB5])
        if full:
            P.add("dve", lambda e: e.tensor_tensor(c.e1[:].rearrange("p h s -> p (h s)"), B5[:],
                                                   cs(C_MC4, 512), ALU.add), r=[c.r_cst], w=[rB5, c.r_e1])
            P.add("dve", lambda e: e.tensor_reduce(cm, c.e1[:], AX.X, ALU.max), r=[c.r_e1], w=[r_cm])
        else:
            P.add("dve", lambda e: e.tensor_reduce(cm, B5[:].rearrange("p (h s) -> p h s", h=4, s=128),
                                                   AX.X, ALU.max), w=[rB5, r_cm])

    def mlstm_mid(ch):
        li = c.gt[:, ch, 0:4]
        fz = c.gt[:, ch, 4:8]
        pb = 16 + 4 * ch
        (sp_, r_sp) = smv(0)
        (a_, r_a_), (nb_, r_nb), (nbL, r_nbL), (cm, r_cm) = smv(pb), smv(pb + 1), smv(pb + 2), smv(pb + 3)
        (M_, r_M), (Ml, r_Ml), (t4, r_t4), (sc_, r_sc) = smv(4), smv(5), smv(6), smv(7)
        (dcy, r_dcy), (dn, r_dn), (lim, r_lim), (ssq, r_ssq) = smv(8), smv(9), smv(10), smv(11)
        (rr, r_rr) = smv(12)
        B0, B1, B2 = c.bank[0], c.bank[1], c.bank[2]
        rB0, rB1, rB2 = c.r_bank[0], c.r_bank[1], c.r_bank[2]
        P.add("dve", lambda e: e.tensor_tensor(M_, cm, mst, ALU.max), r=[r_cm, c.r_Cn], w=[r_M])
        for h in range(4):
            P.add("dve", lambda e, h=h: e.tensor_scalar(c.dg[:, h, :], ident, M_[:, h:h + 1], None, ALU.mult),
                  r=[c.r_cst, r_M], w=[c.r_dg])
        P.add("pe", lambda e: e.matmul(B2[:], onesf, c.dg[:].rearrange("p h s -> p (h s)"),
                                       start=True, stop=True), r=[c.r_dg, c.r_cst], w=[rB2])
        B2v = B2[:].rearrange("p (h t) -> p h t", h=4, t=128)
        P.add("dve", lambda e: e.tensor_tensor(c.e1[:].rearrange("p h s -> p (h s)"), B2[:],
                                               cs(C_MT4, 512), ALU.add), r=[c.r_cst], w=[rB2, c.r_e1])
        for h in range(4):
            P.add("act", lambda e, h=h: e.activation(c.wT[:, h, :], c.e1[:, h, :], AF.Exp,
                                                     bias=a_[:, h:h + 1], scale=-1.0),
                  r=[c.r_e1, r_a_], w=[c.r_wT])
        for h in range(4):
            P.add("act", lambda e, h=h: e.activation(c.sib[:, h, :], B2v[:, h, :], AF.Exp,
                                                     bias=mst[:, h:h + 1], scale=-1.0),
                  r=[c.r_Cn], w=[rB2, c.r_sib])
        P.add("dve", lambda e: e.tensor_copy(Ml, B2v[:, :, 127]), w=[rB2, r_Ml])
        P.add("dve", lambda e: e.tensor_tensor(c.qs[:], qT[:, :, ch * 128:(ch + 1) * 128], c.sib[:], ALU.mult),
              r=r_qT + [c.r_sib], w=[c.r_qs])
        for h in range(4):
            P.add("pe", lambda e, h=h: e.matmul(B0[:, h * 128:(h + 1) * 128],
                                                kT[:, h, ch * 128:(ch + 1) * 128],
                                                qT[:, h, ch * 128:(ch + 1) * 128], start=True, stop=True),
                  r=r_kT + r_qT, w=[rB0])
        P.add("dve", lambda e: e.tensor_tensor(c.qkw[:].rearrange("p h t -> p (h t)"), B0[:],
                                               c.wT[:].rearrange("p h t -> p (h t)"), ALU.mult),
              r=[c.r_wT], w=[rB0, c.r_qkw])
        for h in range(4):
            bk, rbk = c.bank[3 + h], c.r_bank[3 + h]
            P.add("pe", lambda e, h=h, bk=bk: e.matmul(bk[:, 0:257], c.qkw[:, h, :], c.v1[:, ch, h, :],
                                                       start=True, stop=False),
                  r=[c.r_qkw, c.r_v1], w=[rbk])
            P.add("pe", lambda e, h=h, bk=bk: e.matmul(bk[:, 0:257], c.qs[:, h, :], c.Cnb[:, h, :],
                                                       start=False, stop=True),
                  r=[c.r_qs, c.r_Cnb], w=[rbk])

    def mlstm_mid_b(ch):
        li = c.gt[:, ch, 0:4]
        fz = c.gt[:, ch, 4:8]
        pb = 16 + 4 * ch
        (sp_, r_sp) = smv(0)
        (a_, r_a_), (nb_, r_nb), (nbL, r_nbL), (cm, r_cm) = smv(pb), smv(pb + 1), smv(pb + 2), smv(pb + 3)
        (M_, r_M), (Ml, r_Ml), (t4, r_t4), (sc_, r_sc) = smv(4), smv(5), smv(6), smv(7)
        (dcy, r_dcy), (dn, r_dn), (lim, r_lim), (ssq, r_ssq) = smv(8), smv(9), smv(10), smv(11)
        (rr, r_rr) = smv(12)
        B0, B1, B2 = c.bank[0], c.bank[1], c.bank[2]
        rB0, rB1, rB2 = c.r_bank[0], c.r_bank[1], c.r_bank[2]
        P.add("dve", lambda e: e.tensor_tensor(t4, nb_, M_, ALU.subtract), r=[r_nb, r_M], w=[r_t4])
        P.add("act", lambda e: e.activation(lim, t4, AF.Exp), r=[r_t4], w=[r_lim])
        for h in range(4):
            bk, rbk = c.bank[3 + h], c.r_bank[3 + h]
            P.add("dve", lambda e, h=h, bk=bk: e.tensor_copy(dn[:, h:h + 1], bk[:, 256:257]),
                  w=[rbk, r_dn])
        P.add("dve", lambda e: e.scalar_tensor_tensor(t4, dn, -1.0, dn, ALU.mult, ALU.max),
              r=[r_dn], w=[r_t4])
        P.add("dve", lambda e: e.tensor_tensor(t4, t4, lim, ALU.max), r=[r_t4, r_lim], w=[r_t4])
        P.add("dve", lambda e: e.reciprocal(dn, t4), r=[r_t4], w=[r_dn])
        for h in range(4):
            bk, rbk = c.bank[3 + h], c.r_bank[3 + h]
            P.add("dve", lambda e, h=h, bk=bk: e.tensor_scalar(c.hh[:, h, :], bk[:, 0:256],
                                                               dn[:, h:h + 1], None, ALU.mult),
                  r=[r_dn], w=[rbk, c.r_hh[h]])
            P.add("act", lambda e, h=h: e.activation(c.junk[:], c.hh[:, h, :], AF.Square,
                                                     accum_out=ssq[:, h:h + 1]),
                  r=[c.r_hh[h]], w=[c.r_junk, r_ssq])

    def mlstm_fin(ch):
        li = c.gt[:, ch, 0:4]
        fz = c.gt[:, ch, 4:8]
        pb = 16 + 4 * ch
        (sp_, r_sp) = smv(0)
        (a_, r_a_), (nb_, r_nb), (nbL, r_nbL), (cm, r_cm) = smv(pb), smv(pb + 1), smv(pb + 2), smv(pb + 3)
        (M_, r_M), (Ml, r_Ml), (t4, r_t4), (sc_, r_sc) = smv(4), smv(5), smv(6), smv(7)
        (dcy, r_dcy), (dn, r_dn), (lim, r_lim), (ssq, r_ssq) = smv(8), smv(9), smv(10), smv(11)
        (rr, r_rr) = smv(12)
        B0, B1, B2 = c.bank[0], c.bank[1], c.bank[2]
        rB0, rB1, rB2 = c.r_bank[0], c.r_bank[1], c.r_bank[2]
        P.add("act", lambda e: e.activation(rr, ssq, AF.Ln, bias=c.epsc[:, 0:1], scale=1.0 / 256),
              r=[r_ssq, c.r_ones], w=[r_rr])
        P.add("act", lambda e: e.activation(rr, rr, AF.Exp, scale=-0.5), r=[r_rr], w=[r_rr])
        for h in range(4):
            P.add("dve", lambda e, h=h: e.scalar_tensor_tensor(
                c.hh[:, h, :], c.hh[:, h, :], rr[:, h:h + 1],
                c.prm[:, P_HN + h * 256:P_HN + (h + 1) * 256], ALU.mult, ALU.mult),
                r=[c.r_hh[h], r_rr, c.r_prm], w=[c.r_hh[h]])
        P.add("dve", lambda e: e.tensor_tensor(c.og[:], c.hh[:].rearrange("p h v -> p (h v)"),
                                               so[:, ch, :], ALU.mult),
              r=c.r_hh + [c.r_y[2 * ch], c.r_y[2 * ch + 1]], w=[c.r_og])
        B7b = c.bank[7][:].bitcast(BF16)
        for kc in range(KC):
            P.add("pe", lambda e, kc=kc: e.transpose(B7b[:, kc * 128:(kc + 1) * 128],
                                                     c.og[:, kc * 128:(kc + 1) * 128], c.identb[:]),
                  r=[c.r_og, c.r_ones], w=[c.r_bank[7]])
        P.add("act", lambda e: e.activation(ogT[:, :, ch * 128:(ch + 1) * 128],
                                            B7b[:, 0:1024].rearrange("p (k t) -> p k t", k=8, t=128), AF.Copy),
              w=[c.r_bank[7]] + r_ogT)

    def mlstm_upd(ch, full):
        li = c.gt[:, ch, 0:4]
        fz = c.gt[:, ch, 4:8]
        pb = 16 + 4 * ch
        (sp_, r_sp) = smv(0)
        (a_, r_a_), (nb_, r_nb), (nbL, r_nbL), (cm, r_cm) = smv(pb), smv(pb + 1), smv(pb + 2), smv(pb + 3)
        (M_, r_M), (Ml, r_Ml), (t4, r_t4), (sc_, r_sc) = smv(4), smv(5), smv(6), smv(7)
        (dcy, r_dcy), (dn, r_dn), (lim, r_lim), (ssq, r_ssq) = smv(8), smv(9), smv(10), smv(11)
        (rr, r_rr) = smv(12)
        B0, B1, B2 = c.bank[0], c.bank[1], c.bank[2]
        rB0, rB1, rB2 = c.r_bank[0], c.r_bank[1], c.r_bank[2]
        if not full:
            P.add("dve", lambda e: e.tensor_tensor(Ml, cm, mst, ALU.max), r=[r_cm, c.r_Cn], w=[r_Ml])
        P.add("dve", lambda e: e.tensor_tensor(t4, a_, Ml, ALU.subtract), r=[r_a_, r_Ml], w=[r_t4])
        P.add("act", lambda e: e.activation(sc_, t4, AF.Exp), r=[r_t4], w=[r_sc])
        P.add("dve", lambda e: e.tensor_tensor(dcy, mst, Ml, ALU.subtract), r=[c.r_Cn, r_Ml], w=[r_dcy])
        P.add("act", lambda e: e.activation(dcy, dcy, AF.Exp), r=[r_dcy], w=[r_dcy])
        for h in range(4):
            P.add("dve", lambda e, h=h: e.tensor_scalar(c.ksc[:, h, :], ktm[:, ch, h * 128:(h + 1) * 128],
                                                        sc_[:, h:h + 1], DKS, ALU.mult, ALU.mult),
                  r=r_ktm + [r_sc], w=[c.r_ksc])
        for h in range(4):
            bk, rbk = c.bank[3 + h], c.r_bank[3 + h]
            P.add("pe", lambda e, h=h, bk=bk: e.matmul(bk[:, 0:257], c.ksc[:, h, :], c.v1[:, ch, h, :],
                                                       start=True, stop=True),
                  r=[c.r_ksc, c.r_v1], w=[rbk])
            P.add("dve", lambda e, h=h, bk=bk: e.scalar_tensor_tensor(
                Cn4[:, h, :], Cn4[:, h, :], dcy[:, h:h + 1], bk[:, 0:257], ALU.mult, ALU.add),
                r=[r_dcy, c.r_Cn], w=[rbk, c.r_Cn])
        P.add("dve", lambda e: e.tensor_tensor(mst, Ml, nbL, ALU.subtract), r=[r_Ml, r_nbL, c.r_Cn], w=[c.r_Cn])
        if full:
            P.add("act", lambda e: e.activation(c.Cnb[:], Cn4, AF.Copy), r=[c.r_Cn], w=[c.r_Cnb])


    def mlstm(full):
        prenorm(gcol(0, 2, 0))
        mlstm_proj(full)
        for ch in range(4):
            if full:
                mlstm_mid(ch)
                if ch > 0:
                    mlstm_fin(ch - 1)
                mlstm_mid_b(ch)
            mlstm_upd(ch, full)
            if c.drip_n and ch in (1, 3):
                drip(n=1, after=[c.r_ksc])
        if full:
            mlstm_fin(3)
            out_proj(("mo",), ogT, r_ogT, 0, 3)

    def mlstm_state_init_zero():
        P.add("dve", lambda e: e.memset(c.Cn[:], 0.0), w=[c.r_Cn])
        P.add("dve", lambda e: e.memset(mst, -1e30), w=[c.r_Cn])
        P.add("sp", lambda e: e.dma_start(out=c.mg[:], in_=wview(("mg",), "(p k c) -> p k c", p=128, k=KC, c=8)),
              r=rwb(("mg",)), w=[c.r_mg], dma="mg")

    def mlstm_exchange():
        P.add("sp", lambda e: e.dma_start(out=st_own.ap(), in_=c.Cn[:]), r=[c.r_Cn], w=[c.r_st_own], dma="st")
        P.add("pool", lambda e: e.collective_compute("AllGather", ALU.bypass, replica_groups=PAIRS,
                                                     ins=[st_own.ap().opt()], outs=[st_all.ap().opt()]),
              r=[c.r_st_own], w=[c.r_st_all], name="cc", dma="cc", inc=1)
        P.add("sp", lambda e: e.dma_start(out=c.Cn[:], in_=st_all.ap()[0:128, :]), r=[c.r_st_all], w=[c.r_Cn], dma="st")
        sel = c.prm[:, P_SEL:P_SEL + 1]
        P.add("dve", lambda e: e.tensor_scalar(c.Cn[:], c.Cn[:], sel, None, ALU.mult), r=[c.r_prm, c.r_Cn], w=[c.r_Cn])
        t4, r_t4 = smv(6)
        P.add("dve", lambda e: e.tensor_scalar(t4[:, 0:1], sel, -1.0, 1e30, ALU.add, ALU.mult), r=[c.r_prm], w=[r_t4])
        P.add("dve", lambda e: e.tensor_scalar(mst, mst, t4[:, 0:1], None, ALU.add), r=[r_t4, c.r_Cn], w=[c.r_Cn])
        P.add("act", lambda e: e.activation(c.Cnb[:], Cn4, AF.Copy), r=[c.r_Cn], w=[c.r_Cnb])

    kst, r_kst = c.aT[:, 0:8, :], c.r_a[0:8]
    vst, r_vst = c.aT[:, 8:16, :], c.r_a[8:16]
    vst4 = c.aT[:, 8:16, :].rearrange("p (c a) t -> p c (a t)", c=4, a=2)

    def kvproj(t):
        prenorm(2 * 8 * KC)
        def evk(h, b):
            P.add("act", lambda e: e.activation(kst[:, h, :], c.pg[b][:], AF.Copy), w=[c.r_pg[b], r_kst[h]])
        fm_proj(("kk",), KC, c.xn, c.r_xn, evk)
        kvv = wview(("kv",), "(g p k c) -> g p k c", g=2, p=128, k=KC, c=512)
        nb = 0
        for g in range(2):
            wtile, rw = load_wtslot(kvv[g], rwb(("kv",)))
            for ch in range(4):
                b = nb % 2
                nb += 1
                for kc in range(KC):
                    P.add("pe", lambda e, kc=kc, ch=ch, wtile=wtile, b=b: e.matmul(
                        c.pu[b][:], c.xn[:, kc, ch * 128:(ch + 1) * 128], wtile[:, kc, :],
                        start=(kc == 0), stop=(kc == KC - 1)),
                        r=[rw, c.r_xn[kc]], w=[c.r_pu[b]])
                P.add("act", lambda e, ch=ch, b=b, g=g: e.activation(
                    vst4[:, ch, g * 512:(g + 1) * 512], c.pu[b][:], AF.Copy),
                    w=[c.r_pu[b], r_vst[2 * ch], r_vst[2 * ch + 1]])
        ko = kv_own[t].ap()
        P.add("act", lambda e: [
            e.dma_start(out=ko[0:1024, :].rearrange("(h p) t -> p h t", h=8, p=128), in_=kst),
            e.dma_start(out=ko[1024:2048, :].rearrange("(c p a) t -> p c (a t)", c=4, p=128, a=2), in_=vst4)],
            r=r_kst + r_vst, w=[c.r_kv_own[t]], dma="kvst", ndma=2)
        if NOCC:
            P.add("sp", lambda e: e.dma_start(out=kv_all[t].ap()[0:2048, :], in_=kv_own[t].ap()),
                  r=[c.r_kv_own[t]], w=[c.r_kv_all[t]], dma="kvcp")
        else:
            P.add("pool", lambda e: e.collective_compute("AllGather", ALU.bypass, replica_groups=PAIRS,
                                                         ins=[kv_own[t].ap().opt()], outs=[kv_all[t].ap().opt()]),
                  r=[c.r_kv_own[t]], w=[c.r_kv_all[t]], name="cc", dma="cc", inc=1)

    aqT, r_aqT = c.aT[:, 0:8, :], c.r_a[0:8]
    aoT, r_aoT = c.aT[:, 8:16, :], c.r_a[8:16]

    def attn_setup():
        lam = c.prm[:, P_LAM:P_LAM + 256]
        P.add("dve", lambda e: e.tensor_tensor(c.junk[:, 0:64], lam[:, 0:64], lam[:, 64:128], ALU.mult),
              r=[c.r_prm], w=[c.r_junk])
        P.add("dve", lambda e: e.tensor_tensor(c.junk[:, 64:128], lam[:, 128:192], lam[:, 192:256], ALU.mult),
              r=[c.r_prm], w=[c.r_junk])
        P.add("dve", lambda e: e.tensor_reduce(c.lamc[:, 0:2], c.junk[:, 0:128].rearrange("p (a d) -> p a d", a=2, d=64),
                                               AX.X, ALU.add), r=[c.r_junk], w=[c.r_lam])
        P.add("act", lambda e: e.activation(c.lamc[:, 0:2], c.lamc[:, 0:2], AF.Exp), r=[c.r_lam], w=[c.r_lam])
        P.add("dve", lambda e: e.scalar_tensor_tensor(c.lamc[:, 2:3], c.lamc[:, 0:1], LAM_INIT, c.lamc[:, 1:2],
                                                      ALU.add, ALU.subtract), r=[c.r_lam], w=[c.r_lam])
        P.add("dve", lambda e: e.tensor_scalar(c.lamc[:, 3:4], c.prm[:, P_SUBLN:P_SUBLN + 1], 1.0 - LAM_INIT, None,
                                               ALU.mult), r=[c.r_prm, c.r_lam], w=[c.r_lam])

    def attn(t):
        prenorm(gcol(1, 2, 0))

        def evq(h, b):
            P.add("act", lambda e: e.activation(aqT[:, h, :], c.pg[b][:], AF.Copy, scale=0.125),
                  w=[c.r_pg[b], r_aqT[h]])
        fm_proj(("dq",), KC, c.xn, c.r_xn, evq)
        chunks = []
        na = min(NT, 8)
        for ci in range(0, na, 4):
            chunks.append(("A", [(kv_all[tt], tt) for tt in range(ci, min(ci + 4, na))]))
        own = list(range(t + 1))
        for ci in range(0, len(own), 4):
            chunks.append(("B", [(kv_own[tt], tt) for tt in own[ci:ci + 4]]))
        nkb_total = sum(4 * len(srcs) for _, srcs in chunks)
        SA, SB = [c.bank[0], c.bank[2]], [c.bank[1], c.bank[3]]
        rSA, rSB = [c.r_bank[0], c.r_bank[2]], [c.r_bank[1], c.r_bank[3]]
        O = [c.bank[4], c.bank[5]]
        rO = [c.r_bank[4], c.r_bank[5]]
        SM = [c.bank[6], c.bank[7]]
        rSM = [c.r_bank[6], c.r_bank[7]]
        loads = []
        iters = []
        for h in range(8):
            kbi = 0
            for reg, srcs in chunks:
                li_ = len(loads)
                loads.append((h, reg, srcs))
                for n, (dt_, tt) in enumerate(srcs):
                    for j4 in range(4):
                        iters.append(dict(h=h, load=li_, kblk=n * 4 + j4, j4=j4, reg=reg,
                                          diag=(reg == "B" and tt == t),
                                          first=(kbi == 0), last=(kbi == nkb_total - 1)))
                        kbi += 1
        slot_of = {}

        def emit_load(li_):
            h, reg, srcs = loads[li_]
            si = c.kb_i % 2
            c.kb_i += 1
            slot_of[li_] = si
            kbuf, vbuf = c.kb[si], c.vb[si]

            def dmas(e):
                ins = []
                for n, (dt_, tt) in enumerate(srcs):
                    a = dt_.ap()
                    ins.append(e.dma_start(out=kbuf[:, n * 512:(n + 1) * 512], in_=a[h * 128:(h + 1) * 128, :]))
                    vsrc = a[1024:2048, :].rearrange("(c p a) t -> p c (a t)", c=4, p=128, a=2)
                    ins.append(e.dma_start(out=vbuf[:, 4 * n:4 * n + 4, :], in_=vsrc[:, :, h * 128:(h + 1) * 128]))
                return ins
            rsrc = [c.r_kv_all[tt] if reg == "A" else c.r_kv_own[tt] for _, tt in srcs]
            P.add("sp", dmas, r=rsrc, w=[c.r_kb[si], c.r_vb[si]], dma="kvld_%d" % si, ndma=2 * len(srcs))

        def emit_scores(i, d):
            si = slot_of[d["load"]]
            kbuf = c.kb[si]
            b = i % 2
            h, kblk, j4 = d["h"], d["kblk"], d["j4"]
            dg_ = d["diag"]
            for m, (S_, rS) in enumerate(((SA, rSA), (SB, rSB))):
                p0 = 64 * m
                P.add("pe", lambda e, S_=S_, p0=p0: e.matmul(
                    S_[b][:], kbuf[p0:p0 + 64, kblk * 128:(kblk + 1) * 128], aqT[p0:p0 + 64, h, :],
                    start=True, stop=not dg_),
                    r=[c.r_kb[si], r_aqT[h]], w=[rS[b]])
            if dg_:
                for m, (S_, rS) in enumerate(((SA, rSA), (SB, rSB))):
                    P.add("pe", lambda e, S_=S_: e.matmul(
                        S_[b][:], c.identb[:], c.maskb[:, 512 * j4:512 * (j4 + 1)], start=False, stop=True),
                        r=[c.r_cst, c.r_ones], w=[rS[b]])
            spair = c.pairs[b][:, 0:1024]
            ppair = c.ptpair[b]
            wres = [rSA[b], rSB[b], c.r_pt[0][b], c.r_pt[1][b]]
            if d["reg"] == "A":
                P.add("act", lambda e: e.activation(ppair[:], spair, AF.Exp, bias=c.prm[:, P_BIASA:P_BIASA + 1]),
                      r=[c.r_prm], w=wres)
            else:
                P.add("act", lambda e: e.activation(ppair[:], spair, AF.Exp), w=wres)

        def emit_av(i, d):
            si = slot_of[d["load"]]
            vbuf = c.vb[si]
            b = i % 2
            kblk, first, last = d["kblk"], d["first"], d["last"]
            for m in range(2):
                ptile, rpt = c.pt[m][b], c.r_pt[m][b]
                P.add("pe", lambda e, m=m, ptile=ptile: e.matmul(
                    O[m][:], vbuf[:, kblk, :], ptile[:], start=first, stop=last),
                    r=[c.r_vb[si], rpt], w=[rO[m]])
                if m == 0:
                    P.add("pe", lambda e, m=m, ptile=ptile: e.matmul(
                        SM[m][:], c.one1[:], ptile[:], start=first, stop=last),
                        r=[c.r_ones, rpt], w=[rSM[m]])
                elif first:
                    P.add("dve", lambda e, ptile=ptile: e.tensor_copy(c.rstd[:], ptile[:]),
                          r=[rpt], w=[c.r_rstd])
                else:
                    P.add("dve", lambda e, ptile=ptile: e.tensor_tensor(c.rstd[:], c.rstd[:], ptile[:], ALU.add),
                          r=[rpt], w=[c.r_rstd])

        def emit_combine(h):
            P.add("pe", lambda e: e.matmul(SM[1][:], onesf, c.rstd[:], start=True, stop=True),
                  r=[c.r_rstd, c.r_cst], w=[rSM[1]])
            P.add("dve", lambda e: e.tensor_copy(c.ms[0][:], SM[0][:]), w=[rSM[0], c.r_ms[0]])
            P.add("dve", lambda e: e.tensor_copy(c.ms[1][:], SM[1][:]), w=[rSM[1], c.r_ms[1]])
            P.add("dve", lambda e: e.tensor_copy(c.tmp[0][:], O[0][:]), w=[rO[0], c.r_tmp[0]])
            P.add("dve", lambda e: e.tensor_copy(c.tmp[1][:], O[1][:]), w=[rO[1], c.r_tmp[1]])
            P.add("act", lambda e: e.activation(c.msall[:], c.msall[:], AF.Ln),
                  r=[c.r_ms[0], c.r_ms[1]], w=[c.r_ms[0], c.r_ms[1]])
            P.add("act", lambda e: e.activation(c.msall[:], c.msall[:], AF.Exp, scale=-1.0),
                  r=[c.r_ms[0], c.r_ms[1]], w=[c.r_ms[0], c.r_ms[1]])
            P.add("dve", lambda e: e.tensor_tensor(c.tmp[0][:], c.tmp[0][:], c.ms[0][:], ALU.mult),
                  r=[c.r_ms[0], c.r_tmp[0]], w=[c.r_tmp[0]])
            P.add("dve", lambda e: e.scalar_tensor_tensor(c.tmp[1][:], c.tmp[1][:], c.lamc[:, 2:3], c.ms[1][:],
                                                          ALU.mult, ALU.mult),
                  r=[c.r_ms[1], c.r_tmp[1], c.r_lam], w=[c.r_tmp[1]])
            P.add("dve", lambda e: e.tensor_tensor(c.y[:, h, :], c.tmp[0][:], c.tmp[1][:], ALU.subtract),
                  r=[c.r_tmp[0], c.r_tmp[1]], w=[c.r_y[h]])

        def emit_subln():
            for h in range(8):
                b = h % 2
                P.add("act", lambda e, h=h: e.activation(c.sq[:, h, :], c.y[:, h, :], AF.Square),
                      r=[c.r_y[h]], w=[c.r_sq[h]])
                P.add("pe", lambda e, h=h, b=b: e.matmul(c.bank[b][:], c.o128[:], c.sq[:, h, :], start=True, stop=True),
                      r=[c.r_sq[h], c.r_ones], w=[c.r_bank[b]])
                rsqrt_bank(b, c.tmp[b][:], c.r_tmp[b])
                P.add("dve", lambda e, h=h, b=b: e.scalar_tensor_tensor(aoT[:, h, :], c.y[:, h, :], c.lamc[:, 3:4],
                                                                        c.tmp[b][:], ALU.mult, ALU.mult),
                      r=[c.r_y[h], c.r_tmp[b], c.r_lam], w=[r_aoT[h]])

        nI = len(iters)
        emit_load(0)
        for i in range(nI + 1):
            if i < nI:
                d = iters[i]
                emit_scores(i, d)
            if i >= 1:
                dp = iters[i - 1]
                emit_av(i - 1, dp)
                if dp["last"]:
                    emit_combine(dp["h"])
                if i < nI and iters[i]["load"] != dp["load"] and iters[i]["load"] + 1 < len(loads):
                    emit_load(iters[i]["load"] + 1)
            elif len(loads) > 1:
                emit_load(1)
        emit_subln()
        out_proj(("do",), aoT, r_aoT, 1, 3)

    def load_x(src, ts, rs=(), buf=0):
        xT, r_x = c.xTs[buf], c.r_xs2[buf]
        P.add("sp", lambda e: e.dma_start(out=xT[:], in_=src[:, :, ts]), r=list(rs), w=r_x, dma="xin%d" % buf)

    def use_x(buf):
        c.xT, c.r_x = c.xTs[buf], c.r_xs2[buf]

    def store_x(dst, ts, rdst):
        xT, r_x = c.xT, c.r_x
        P.add("act", lambda e: e.dma_start(out=dst[:, :, ts], in_=xT[:]), r=r_x, w=[rdst], dma="xout")

    def tile_loop(src, rs, body):
        load_x(src, slice(0, TT), rs, 0)
        for t in range(NT):
            ts = slice(t * TT, (t + 1) * TT)

            def pf(t=t):
                if t + 1 < NT:
                    load_x(src, slice((t + 1) * TT, (t + 2) * TT), rs, (t + 1) % 2)
            use_x(t % 2)
            body(t, ts, pf)

    def run_stage(st, t):
        if st[0] == "ffn":
            ffn(st[1], st[2])
        elif st[0] == "ple":
            ple(st[1], t)
        elif st[0] == "mlstm":
            mlstm(st[1])
        elif st[0] == "kvproj":
            kvproj(t)
        elif st[0] == "attn":
            attn(t)

    if phases is None:
        drip(upto={k_ for k_, _, _, _ in cast_chunks})
        flat = [s_ for p_ in stages for s_ in (p_ if isinstance(p_, list) else [p_])]
        names = [s_[0] for s_ in flat]
        if "mlstm" in names:
            mlstm_state_init_zero()
            if any(s_[0] == "mlstm" and s_[1] for s_ in flat):
                P.add("act", lambda e: e.activation(c.Cnb[:], Cn4, AF.Copy), r=[c.r_Cn], w=[c.r_Cnb])
        if "attn" in names:
            attn_setup()
        passes = stages if (stages and isinstance(stages[0], list)) else [stages]
        for pi, pst in enumerate(passes):
            def body(t, ts, pf, pst=pst, pi=pi):
                for si_, st in enumerate(pst):
                    run_stage(st, t)
                    if si_ == 0:
                        pf()
                if pi == len(passes) - 1:
                    store_x(out_d, ts, c.r_out)
            tile_loop(xT_d, [], body)
            P.barrier()
    else:
        mlstm_state_init_zero()
        attn_setup()

        def body_a(t, ts, pf):
            ffn(0, 0)
            pf()
            store_x(xs_d, ts, c.r_xs)
            mlstm(False)
        c.drip_n, c.drip_every = 1, 8
        tile_loop(xT_d, [], body_a)
        drip(upto=PH_B)
        mlstm_exchange()

        def body_b(t, ts, pf):
            mlstm(True)
            pf()
            ffn(0, 1)
            ple(0, t)
            store_x(xs_d, ts, c.r_xs)
            kvproj(t)
        c.drip_n, c.drip_every = 1, 8
        tile_loop(xs_d, [c.r_xs], body_b)
        drip(upto={k_ for k_, _, _, _ in cast_chunks})
        c.drip_n = 0
        P.barrier()

        def body_c(t, ts, pf):
            ffn(1, 0)
            pf()
            attn(t)
            ffn(1, 1)
            ple(1, t)
            store_x(out_d, ts, c.r_out)
        tile_loop(xs_d, [c.r_xs], body_c)
    P.add("act", None, r=[c.r_out])
    P.emit()
    P.close()
    return nc


def to_fm(a):
    t, ch = a.shape
    return np.ascontiguousarray(a.reshape(t, ch // 128, 128).transpose(2, 1, 0))


def from_fm(a):
    p, k, t = a.shape
    return a.transpose(2, 1, 0).reshape(t, k * p)


def make_in_maps(inp, x_override=None, ncores=NCORES):
    x = inp["x"] if x_override is None else x_override
    wall = pack_weights(inp).reshape(-1, ROW)
    gc = gcols_host(inp)
    cst = consts_host()
    maps = []
    for core in range(ncores):
        b, h = core // 2, core % 2
        sl = slice(h * TOK, (h + 1) * TOK)
        maps.append({
            "xT": to_fm(np.asarray(x[b, sl])),
            "pT": np.stack([to_fm(np.asarray(inp["p"][l, b, sl])) for l in range(2)]),
            "gcols": gc,
            "cst": cst,
            "prm": prm_host(inp, core),
            "wall": wall,
        })
    return maps


def gather_out(results):
    out = np.zeros((4, 8192, D), np.float32)
    for core in range(len(results)):
        b, h = core // 2, core % 2
        out[b, h * TOK:(h + 1) * TOK] = from_fm(np.asarray(results[core]["outT"]).reshape(128, KC, TOK))
    return out


def kernel(**inputs):
    inp = {k: np.asarray(v) for k, v in inputs.items()}
    nc = build(None, phases=True)
    res = run_bass_kernel_spmd(nc, make_in_maps(inp), core_ids=list(range(NCORES)))
    return gather_out(res.results)
```

```python
import numpy as np
from contextlib import ExitStack
import concourse.bass as bass
import concourse.mybir as mybir
from concourse.bass_utils import run_bass_kernel_spmd

F32, BF16 = mybir.dt.float32, mybir.dt.bfloat16
AF = mybir.ActivationFunctionType
ALU = mybir.AluOpType
AX = mybir.AxisListType

NCORES = 8
D = 1024
KC = 8
DFF = 2816
FC = 22
TOK = 4096
TT = 512
EPS = 1e-6

ENGS = ("pe", "act", "dve", "pool", "sp")
SEM_LIMIT = 30000


class Res:
    __slots__ = ("name", "w", "rs")

    def __init__(self, name):
        self.name = name
        self.w = None
        self.rs = []


class Op:
    __slots__ = ("eng", "fn", "deps", "sig", "sem", "val", "dma", "ndma", "name", "inc")


class Prog:
    def __init__(self, nc):
        self.nc = nc
        self.q = {e: [] for e in ENGS}
        self.dma_cnt = {}
        self.stack = ExitStack()
        self.sems = {}
        self.nsem = 0
        self.last_dma = {}

    def sbuf(self, name, shape, dt):
        return self.stack.enter_context(self.nc.sbuf_tensor(name, list(shape), dt))

    def psum(self, name, shape=(128, 512), dt=F32):
        return self.stack.enter_context(self.nc.psum_tensor(name, list(shape), dt))

    def sem(self, key):
        if key not in self.sems:
            self.sems[key] = self.stack.enter_context(self.nc.semaphore("s_%d" % self.nsem))
            self.nsem += 1
        return self.sems[key]

    def add(self, eng, fn, r=(), w=(), dma=None, ndma=1, name="", inc=16):
        op = Op()
        op.eng, op.fn, op.dma, op.ndma, op.name = eng, fn, dma, ndma, name
        op.inc = inc
        op.sig = False
        op.sem = None
        op.val = 0
        raw, oth = set(), set()
        for x in r:
            if x.w is not None:
                raw.add(x.w)
        for x in w:
            if x.w is not None:
                oth.add(x.w)
            oth.update(x.rs)
        deps = []
        for d in raw | oth:
            if d is op:
                continue
            if d.dma is None and dma is None and d.eng == eng:
                if eng == "pe":
                    continue
            deps.append(d)
        if dma is not None:
            prev = self.last_dma.get(dma)
            if prev is not None and prev not in deps:
                deps.append(prev)
            self.last_dma[dma] = op
        op.deps = deps
        for d in deps:
            d.sig = True
        for x in r:
            x.rs.append(op)
        for x in w:
            x.w = op
            x.rs = []
        self.q[eng].append(op)
        return op

    def barrier(self):
        lasts = []
        for e in ENGS:
            for op in reversed(self.q[e]):
                if op.fn is not None:
                    lasts.append(op)
                    break
        for d in self.last_dma.values():
            if d not in lasts:
                lasts.append(d)
        for e in ENGS:
            op = Op()
            op.eng, op.fn, op.dma, op.ndma, op.name = e, None, None, 1, "barrier"
            op.inc = 16
            op.sig, op.sem, op.val = False, None, 0
            op.deps = list(lasts)
            for d in op.deps:
                d.sig = True
            self.q[e].append(op)

    def finalize(self):
        for e in ENGS:
            cnt = 0
            epoch = 0
            for op in self.q[e]:
                if op.dma is not None:
                    c = self.dma_cnt.get(op.dma, 0) + op.ndma
                    self.dma_cnt[op.dma] = c
                    op.sem = self.sem(("dma", op.dma))
                    op.val = op.inc * c
                elif op.sig:
                    if cnt >= SEM_LIMIT:
                        epoch += 1
                        cnt = 0
                    cnt += 1
                    op.sem = self.sem((e, epoch))
                    op.val = cnt

    def emit(self):
        self.finalize()
        nc = self.nc
        prog = self

        def run(ename, eng):
            waited = {}
            for op in prog.q[ename]:
                for d in op.deps:
                    k = id(d.sem)
                    if waited.get(k, 0) >= d.val:
                        continue
                    eng.wait_ge(d.sem, d.val)
                    waited[k] = d.val
                if op.fn is None:
                    continue
                ins = op.fn(eng)
                if op.dma is not None:
                    if not isinstance(ins, (list, tuple)):
                        ins = [ins]
                    assert len(ins) == op.ndma, (op.name, len(ins), op.ndma)
                    for i in ins:
                        i.then_inc(op.sem, op.inc)
                elif op.sem is not None:
                    ins.then_inc(op.sem, 1)

        with nc.Block() as block:
            @block.tensor
            def _(e):
                run("pe", e)

            @block.scalar
            def _(e):
                run("act", e)

            @block.vector
            def _(e):
                run("dve", e)

            @block.gpsimd
            def _(e):
                run("pool", e)

            @block.sync
            def _(e):
                run("sp", e)

    def close(self):
        self.stack.close()


ROW = 2048


class WPack:
    def __init__(self):
        self.parts = []
        self.off = {}
        self.n = 0

    def put(self, key, arr):
        if WKEYS is not None and key not in WKEYS:
            return
        a = np.ascontiguousarray(arr, dtype=np.float32).reshape(-1)
        pad = (-a.size) % ROW
        self.off[key] = (self.n, a.size)
        self.parts.append(a)
        if pad:
            self.parts.append(np.zeros(pad, np.float32))
        self.n += a.size + pad

    def flat(self):
        return np.concatenate(self.parts)


LVL = 99
WKEYS = None


def w_layout_sizes():
    sizes = []
    for l in range(2):
        for f in range(2):
            sizes.append((("w1", l, f), FC * 128 * KC * 256))
            sizes.append((("w2", l, f), KC * 128 * FC * 128))
        sizes.append((("pg", l), KC * 128 * KC * 128))
        sizes.append((("pp", l), KC * 128 * 2 * 128))
        if l == 0:
            sizes.append((("mq",), 4 * 128 * KC * 128))
            sizes.append((("mk",), 4 * 128 * KC * 128))
            sizes.append((("mt",), 5 * 128 * KC * 512))
            sizes.append((("mg",), 128 * KC * 8))
            sizes.append((("mo",), KC * 128 * KC * 128))
            sizes.append((("kk",), KC * 128 * KC * 128))
            sizes.append((("kv",), 2 * 128 * KC * 512))
        else:
            sizes.append((("dq",), KC * 128 * KC * 128))
            sizes.append((("do",), KC * 128 * KC * 128))
    out = {}
    off = 0
    order = []
    if WKEYS is not None:
        sizes = [(k, n) for k, n in sizes if k in WKEYS]
    for k, n in sizes:
        n_pad = n + ((-n) % ROW)
        out[k] = (off, n)
        order.append((k, off, n_pad))
        off += n_pad
    return out, order, off


def pack_weights(inp):
    wp = WPack()
    for l in range(2):
        for f in range(2):
            w_in = inp["w_ffn_in"][l, f]
            w1 = w_in.reshape(KC, 128, 2, FC, 128).transpose(3, 1, 0, 2, 4)
            wp.put(("w1", l, f), w1)
            w_out = inp["w_ffn_out"][l, f]
            w2 = w_out.reshape(FC, 128, KC, 128).transpose(2, 1, 0, 3)
            wp.put(("w2", l, f), w2)
        wg = inp["w_ple_gate"][l]
        wp.put(("pg", l), wg.reshape(KC, 128, KC, 128).transpose(2, 1, 0, 3))
        wq = inp["w_ple_proj"][l]
        wp.put(("pp", l), wq.reshape(2, 128, KC, 128).transpose(2, 1, 0, 3))
        fm = lambda w, n: w.reshape(KC, 128, n, 128).transpose(2, 1, 0, 3)
        tm = lambda w, n: w.reshape(KC, 128, n, 512).transpose(2, 1, 0, 3)
        if l == 0:
            wi = inp["mlstm_w_in"][0]
            wp.put(("mq",), fm(wi[:, 0:512], 4))
            wp.put(("mk",), fm(wi[:, 512:1024], 4))
            wp.put(("mt",), tm(wi[:, 512:3072], 5))
            wp.put(("mg",), wi[:, 3072:3080].reshape(KC, 128, 8).transpose(1, 0, 2))
            wp.put(("mo",), fm(inp["mlstm_w_out"][0], KC))
            wp.put(("kk",), fm(inp["w_kv"][:, 0:1024], KC))
            wp.put(("kv",), tm(inp["w_kv"][:, 1024:2048], 2))
        else:
            wp.put(("dq",), fm(inp["diff_w_q"][0], KC))
            wp.put(("do",), fm(inp["diff_w_out"][0], KC))
    offs, order, total = w_layout_sizes()
    assert total == wp.n, (total, wp.n)
    for k in offs:
        assert offs[k] == wp.off[k], (k, offs[k], wp.off[k])
    return wp.flat()


def gcols_host(inp):
    cols = []
    ng = inp["norm_g"]
    cols.append(ng.reshape(2, 8, KC, 128).transpose(3, 0, 1, 2).reshape(128, 2 * 8 * KC))
    cols.append(inp["kv_norm"].reshape(KC, 128).T)
    return np.ascontiguousarray(np.concatenate(cols, axis=1), dtype=np.float32)


NG = 2 * 8 * KC + KC


def gcol(l, i, kc):
    return (l * 8 + i) * KC + kc


NKV = 2048 * 512
NST = 4 * 257 + 4
DKS = 128 ** -0.5
NEG = -30000.0
NOCC = False
C_ID, C_U, C_ONE, C_MC4, C_MT4, C_MD = 0, 128, 256, 384, 896, 1408
NCST = 1408 + 4 * 512
P_BG, P_HN, P_LAM, P_SUBLN, P_SEL, P_BIASA = 0, 8, 1032, 1288, 1289, 1290
NPRM = 1291


def consts_host():
    p = np.arange(128)[:, None]
    j = np.arange(128)[None, :]
    ident = (p == j).astype(np.float32)
    U = (p <= j).astype(np.float32)
    ones = np.ones((128, 128), np.float32)
    maskC = np.where(j <= p, 0.0, -1e30).astype(np.float32)
    maskT = np.where(j >= p, 0.0, 1e30).astype(np.float32)
    i = np.arange(512)[None, :]
    md = [np.where(i >= 128 * j4 + p, 0.0, NEG).astype(np.float32) for j4 in range(4)]
    return np.ascontiguousarray(np.concatenate(
        [ident, U, ones, np.tile(maskC, (1, 4)), np.tile(maskT, (1, 4))] + md, axis=1))


def prm_host(inp, core):
    a = np.zeros((128, NPRM), np.float32)
    a[:, P_BG:P_BG + 8] = inp["mlstm_b_gates"].reshape(1, 8)
    a[:, P_HN:P_HN + 1024] = inp["mlstm_head_norm"].reshape(1, 1024)
    a[:, P_LAM:P_LAM + 256] = inp["diff_lambda"].reshape(1, 256)
    a[:, P_SUBLN] = inp["diff_subln"].reshape(128)
    a[:, P_SEL] = float(core % 2)
    a[:, P_BIASA] = 0.0 if core % 2 == 1 else NEG
    return a


class Ctx:
    pass


def build(stages, NT=TOK // TT, phases=None):
    nc = bass.Bass("TRN2", target_bir_lowering=False)
    P = Prog(nc)
    c = Ctx()
    offs, order, wtotal = w_layout_sizes()
    import math
    LAM_INIT = 0.8 - 0.6 * math.exp(-0.3 * 1)

    xT_d = nc.dram_tensor("xT", [128, KC, TOK], F32, kind="ExternalInput").ap()
    pT_d = nc.dram_tensor("pT", [2, 128, 2, TOK], F32, kind="ExternalInput").ap()
    gc_d = nc.dram_tensor("gcols", [128, NG], F32, kind="ExternalInput").ap()
    cst_d = nc.dram_tensor("cst", [128, NCST], F32, kind="ExternalInput").ap()
    prm_d = nc.dram_tensor("prm", [128, NPRM], F32, kind="ExternalInput").ap()
    wall_d = nc.dram_tensor("wall", [wtotal // ROW, ROW], F32, kind="ExternalInput").ap()
    out_d = nc.dram_tensor("outT", [128, KC, TOK], F32, kind="ExternalOutput").ap()
    wbf_d = nc.dram_tensor("wbf", [wtotal // ROW, ROW], BF16).ap()
    wbf_flat = wbf_d.rearrange("a b -> (a b)")
    xs_d = nc.dram_tensor("xs", [128, KC, TOK], F32).ap()
    st_own = nc.dram_tensor("st_own", [128, NST], F32)
    st_all = nc.dram_tensor("st_all", [256, NST], F32)
    kv_own = [nc.dram_tensor("kv_own%d" % t, [2048, 512], BF16) for t in range(8)]
    kv_all = [nc.dram_tensor("kv_all%d" % t, [4096, 512], BF16) for t in range(8)]
    PAIRS = [[0, 1], [2, 3], [4, 5], [6, 7]]

    def rwb(key):
        return r_wparts[key] if key in r_wparts else [r_wbf[key]]

    def wview(key, pattern, **kw):
        off, n = offs[key]
        return wbf_flat[off:off + n].rearrange(pattern, **kw)

    c.xTs = [P.sbuf("xT_sb%d" % i, [128, KC, TT], F32) for i in range(2)]
    c.xT = c.xTs[0]
    c.sq = P.sbuf("sq", [128, KC, TT], BF16)
    c.rstd = P.sbuf("rstd", [128, TT], F32)
    c.xn = P.sbuf("xn", [128, KC, TT], BF16)
    NW1, NW2, NWT = 5, 3, 2
    c.w1 = [P.sbuf("w1_%d" % i, [128, KC, 256], BF16) for i in range(NW1)]
    c.w2 = [P.sbuf("w2_%d" % i, [128, FC, 128], BF16) for i in range(NW2)]
    c.wt = [P.sbuf("wt_%d" % i, [128, KC, 512], BF16) for i in range(NWT)]
    c.sg = [P.sbuf("sg_%d" % i, [128, TT], BF16) for i in range(2)]
    c.aT = P.sbuf("aT", [128, FC, TT], BF16)
    c.y = P.sbuf("y", [128, KC, TT], F32)
    c.tmp = [P.sbuf("tmp_%d" % i, [128, TT], F32) for i in range(2)]
    c.gc = P.sbuf("gc", [128, NG], F32)
    c.hgc = P.sbuf("hgc", [128, NG], F32)
    c.ones = P.sbuf("ones", [128, 128], BF16)
    c.one1 = P.sbuf("one1", [128, 128], BF16)
    c.o128 = P.sbuf("o128", [128, 128], BF16)
    c.identb = P.sbuf("identb", [128, 128], BF16)
    c.epsc = P.sbuf("epsc", [128, 1], F32)
    c.pT = P.sbuf("pT_sb", [128, 2, TT], F32)
    c.pTb = P.sbuf("pTb", [128, 2, TT], BF16)
    c.cst = P.sbuf("cst_sb", [128, C_MD], F32)
    c.maskb = P.sbuf("maskb", [128, 4 * 512], BF16)
    c.prm = P.sbuf("prm_sb", [128, NPRM], F32)
    c.mg = P.sbuf("mg", [128, KC, 8], BF16)
    A = P.sbuf("arena", [128, 12288], BF16)
    c.gt = P.sbuf("gt", [128, 4, 8], F32)
    c.sm = P.sbuf("sm", [128, 32, 4], F32)
    c.dg = P.sbuf("dg", [128, 4, 128], F32)
    c.e1 = P.sbuf("e1", [128, 4, 128], F32)
    c.wT = P.sbuf("wT", [128, 4, 128], F32)
    c.sib = P.sbuf("sib", [128, 4, 128], F32)
    c.junk = P.sbuf("junk", [128, 256], F32)
    c.v1 = A[:, 0:4112].rearrange("p (c h v) -> p c h v", c=4, h=4, v=257)
    c.Cnb = A[:, 4112:5140].rearrange("p (h v) -> p h v", h=4, v=257)
    c.qs = A[:, 5140:5652].rearrange("p (h t) -> p h t", h=4, t=128)
    c.qkw = A[:, 5652:6164].rearrange("p (h t) -> p h t", h=4, t=128)
    c.ksc = A[:, 6164:6676].rearrange("p (h t) -> p h t", h=4, t=128)
    c.og = A[:, 6676:7700]
    c.Cn = A[:, 7700:7700 + 2 * NST].bitcast(F32)
    c.hh = A[:, 9764:11812].bitcast(F32).rearrange("p (h v) -> p h v", h=4, v=256)
    c.kb = [A[:, 2048 * i:2048 * (i + 1)] for i in range(2)]
    c.vb = [A[:, 4096 + 2048 * i:4096 + 2048 * (i + 1)].rearrange("p (k d) -> p k d", k=16, d=128) for i in range(2)]
    c.ptpair = [A[:, 8192 + 1024 * i:8192 + 1024 * (i + 1)] for i in range(2)]
    c.pt = [[c.ptpair[i][:, 512 * m:512 * (m + 1)] for i in range(2)] for m in range(2)]
    c.msall = A[:, 10240:12288].bitcast(F32)
    c.ms = [c.msall[:, 512 * m:512 * (m + 1)] for m in range(2)]
    c.lamc = P.sbuf("lamc", [128, 4], F32)
    c.pairs = [P.psum("bankpair%d" % i, (128, 1024)) for i in range(4)]
    c.bank = []
    for i in range(4):
        c.bank += [c.pairs[i][:, 0:512], c.pairs[i][:, 512:1024]]
    c.pg, c.pu, c.py, c.pss = c.bank[0:2], c.bank[2:4], c.bank[4:6], c.bank[6]

    R = lambda n: Res(n)
    c.r_xs2 = [[R("x%d_%d" % (i, m)) for m in range(KC)] for i in range(2)]
    c.r_x = c.r_xs2[0]
    c.r_sq = [R("sq%d" % k) for k in range(KC)]
    c.r_rstd = R("rstd")
    c.r_xn = [R("xn%d" % k) for k in range(KC)]
    c.r_w1 = [R("w1") for _ in range(NW1)]
    c.r_w2 = [R("w2") for _ in range(NW2)]
    c.r_wt = [R("wt") for _ in range(NWT)]
    c.r_sg = [R("sg") for _ in range(2)]
    c.r_a = [R("a") for _ in range(FC)]
    c.r_y = [R("y") for _ in range(KC)]
    c.r_tmp = [R("tmp") for _ in range(2)]
    c.r_gc, c.r_ones = R("gc"), R("ones")
    c.r_pT, c.r_pTb = R("pT"), R("pTb")
    c.r_bank = [R("bank%d" % i) for i in range(8)]
    c.r_pg, c.r_pu, c.r_py, c.r_pss = c.r_bank[0:2], c.r_bank[2:4], c.r_bank[4:6], c.r_bank[6]
    c.r_out, c.r_xs = R("out"), R("xs")
    c.r_cst, c.r_prm = R("cst"), R("prm")
    c.r_mg, c.r_v1, c.r_gt, c.r_Cn, c.r_Cnb = R("mg"), R("v1"), [R("gt%d" % i) for i in range(4)], R("Cn"), R("Cnb")
    c.r_sm = [R("sm%d" % i) for i in range(32)]
    c.r_dg, c.r_e1, c.r_wT, c.r_sib = R("dg"), R("e1"), R("wT"), R("sib")
    c.r_qs, c.r_qkw, c.r_ksc, c.r_og, c.r_junk = R("qs"), R("qkw"), R("ksc"), R("og"), R("junk")
    c.r_hh = [R("hh%d" % i) for i in range(4)]
    c.r_kb = [R("kb") for _ in range(2)]
    c.r_vb = [R("vb") for _ in range(2)]
    c.r_pt = [[R("pt") for _ in range(2)] for _ in range(2)]
    c.r_ms = [R("ms") for _ in range(2)]
    c.r_lam = R("lam")
    c.r_st_own, c.r_st_all = R("st_own"), R("st_all")
    c.r_kv_own = [R("kvo") for _ in range(8)]
    c.r_kv_all = [R("kva") for _ in range(8)]
    r_wbf = {k: R("wbf") for k in offs}
    ALLK = [("mq",), ("mk",), ("mt",), ("mg",), ("mo",), ("kk",), ("kv",), ("dq",), ("do",)]
    for l in range(2):
        ALLK += [("w1", l, 0), ("w2", l, 0), ("w1", l, 1), ("w2", l, 1), ("pg", l), ("pp", l)]
    for kk in ALLK:
        if kk not in offs:
            offs[kk] = (0, 1 << 30)
            r_wbf[kk] = R("wbf_missing")
    c.w1_i = 0
    c.w2_i = 0
    c.wt_i = 0
    c.kb_i = 0

    def cs(off, n=128):
        return c.cst[:, off:off + n]

    ident, Umat, onesf = cs(C_ID), cs(C_U), cs(C_ONE)

    P.add("sp", lambda e: e.dma_start(out=c.gc[:], in_=gc_d[:, :]), w=[c.r_gc], dma="gc")
    P.add("sp", lambda e: e.dma_start(out=c.cst[:], in_=cst_d[:, 0:C_MD]), w=[c.r_cst], dma="cst")
    ystage = c.y[:, 0:4, :].rearrange("p a t -> p (a t)")
    P.add("sp", lambda e: e.dma_start(out=ystage, in_=cst_d[:, C_MD:C_MD + 2048]), w=c.r_y[0:4], dma="cst")
    P.add("dve", lambda e: e.tensor_copy(c.maskb[:], ystage), r=c.r_y[0:4], w=[c.r_cst])
    P.add("sp", lambda e: e.dma_start(out=c.prm[:], in_=prm_d[:, :]), w=[c.r_prm], dma="prm")
    P.add("dve", lambda e: e.memset(c.ones[:], 1.0 / D), w=[c.r_ones])
    P.add("dve", lambda e: e.memset(c.one1[:], 1.0), w=[c.r_ones])
    P.add("dve", lambda e: e.memset(c.o128[:], 1.0 / 128), w=[c.r_ones])
    P.add("dve", lambda e: e.memset(c.epsc[:], EPS), w=[c.r_ones])
    P.add("dve", lambda e: e.memset(c.v1[:], 1.0), w=[c.r_v1])
    P.add("dve", lambda e: e.tensor_copy(c.identb[:], ident), r=[c.r_cst], w=[c.r_ones])
    P.add("dve", lambda e: e.tensor_scalar(c.hgc[:], c.gc[:], 0.5, None, ALU.mult),
          r=[c.r_gc], w=[c.r_gc])
    r_cast = R("castchain")
    c.w1first = None
    use_order = [("w1", 0, 0), ("w2", 0, 0), ("mg",), ("mt",), ("mq",), ("mk",), ("mo",),
                 ("w1", 0, 1), ("w2", 0, 1), ("pg", 0), ("pp", 0), ("kk",), ("kv",),
                 ("w1", 1, 0), ("w2", 1, 0), ("dq",), ("do",), ("w1", 1, 1), ("w2", 1, 1), ("pg", 1), ("pp", 1)]
    omap = {k: (off, n_pad) for k, off, n_pad in order}
    order2 = [(k, omap[k][0], omap[k][1]) for k in use_order if k in omap]
    assert len(order2) == len(order)
    CROWS = 704
    cast_chunks = []
    for k, off, n_pad in order2:
        r0, r1 = off // ROW, (off + n_pad) // ROW
        if k == ("w1", 0, 0):
            c.w1first = []
        for ra in range(r0, r1, CROWS):
            rb = min(ra + CROWS, r1)
            rp = R("wbf_part")
            if k == ("w1", 0, 0):
                c.w1first.append((ra - r0, rb - r0, rp))
            cast_chunks.append((k, ra, rb, rp))
    c.cast_i = 0
    c.cast_n = 0
    c.drip_n = 0
    c.drip_every = 4
    r_wparts = {}
    for k, ra, rb, rp in cast_chunks:
        r_wparts.setdefault(k, []).append(rp)

    def drip(n=None, upto=None, after=()):
        while c.cast_i < len(cast_chunks):
            k, ra, rb, rp = cast_chunks[c.cast_i]
            if upto is not None:
                if k not in upto:
                    break
            elif n is not None:
                if n <= 0:
                    break
                n -= 1
            q = c.cast_n % 4
            c.cast_n += 1
            c.cast_i += 1
            P.add("pool", lambda e, ra=ra, rb=rb: e.dma_start(out=wbf_d[ra:rb, :], in_=wall_d[ra:rb, :]),
                  r=list(after), w=[rp], dma="wcast%d" % q)

    PH_A = {("w1", 0, 0), ("w2", 0, 0), ("mg",), ("mt",)}
    PH_B = PH_A | {("mq",), ("mk",), ("mo",), ("w1", 0, 1), ("w2", 0, 1), ("pg", 0), ("pp", 0), ("kk",), ("kv",)}
    drip(upto=PH_A)

    def rsqrt_bank(bi, dst, r_dst):
        P.add("act", lambda e: e.activation(dst, c.bank[bi][:], AF.Ln, bias=c.epsc[:, 0:1]),
              r=[c.r_ones], w=[c.r_bank[bi], r_dst])
        P.add("act", lambda e: e.activation(dst, dst, AF.Exp, scale=-0.5), r=[r_dst], w=[r_dst])

    def load_w1slot(dmafn, rws, ndma=1):
        wi = c.w1_i % NW1
        c.w1_i += 1
        wtile = c.w1[wi]
        P.add("sp", lambda e: dmafn(e, wtile), r=rws, w=[c.r_w1[wi]], dma="w1_%d" % wi, ndma=ndma)
        return wtile, c.r_w1[wi]

    def load_wtslot(src, rws):
        wi = c.wt_i % NWT
        c.wt_i += 1
        wtile = c.wt[wi]
        P.add("sp", lambda e: e.dma_start(out=wtile[:], in_=src), r=rws, w=[c.r_wt[wi]], dma="wt_%d" % wi)
        return wtile, c.r_wt[wi]

    def prenorm(gbase):
        xT, r_x = c.xT, c.r_x
        for kc in range(KC):
            P.add("act", lambda e, kc=kc: e.activation(c.sq[:, kc, :], xT[:, kc, :], AF.Square),
                  r=[r_x[kc]], w=[c.r_sq[kc]])
        for kc in range(KC):
            P.add("pe", lambda e, kc=kc: e.matmul(c.pss[:], c.ones[:], c.sq[:, kc, :],
                                                  start=(kc == 0), stop=(kc == KC - 1)),
                  r=[c.r_sq[kc], c.r_ones], w=[c.r_pss])
        rsqrt_bank(6, c.rstd[:], c.r_rstd)
        for kc in range(KC):
            col = gbase + kc
            P.add("dve", lambda e, kc=kc, col=col: e.scalar_tensor_tensor(
                c.xn[:, kc, :], xT[:, kc, :], c.gc[:, col:col + 1], c.rstd[:], ALU.mult, ALU.mult),
                r=[r_x[kc], c.r_rstd, c.r_gc], w=[c.r_xn[kc]])

    def postnorm_add(gl, gi, half):
        xT, r_x = c.xT, c.r_x
        for m in range(KC):
            P.add("pe", lambda e, m=m: e.matmul(c.pss[:], c.ones[:], c.sq[:, m, :],
                                                start=(m == 0), stop=(m == KC - 1)),
                  r=[c.r_sq[m], c.r_ones], w=[c.r_pss])
        rsqrt_bank(6, c.rstd[:], c.r_rstd)
        gsrc = c.hgc if half else c.gc
        for m in range(KC):
            col = gcol(gl, gi, m)
            P.add("dve", lambda e, m=m, col=col: e.scalar_tensor_tensor(
                c.y[:, m, :], c.y[:, m, :], gsrc[:, col:col + 1], c.rstd[:], ALU.mult, ALU.mult),
                r=[c.r_rstd, c.r_gc], w=[c.r_y[m]])
            eng = "pool" if m % 2 == 0 else "dve"
            P.add(eng, lambda e, m=m: e.tensor_tensor(
                xT[:, m, :], xT[:, m, :], c.y[:, m, :], ALU.add),
                r=[c.r_y[m]], w=[r_x[m]])

    def fm_proj(key, n, src, r_src, evac):
        wv = wview(key, "(m p k c) -> p m k c", m=n, p=128, k=KC, c=128)
        for i in range(n):
            wtile, rw = load_w1slot(lambda e, wtile, i=i: e.dma_start(out=wtile[:, :, 0:128], in_=wv[:, i]),
                                    rwb(key))
            b = i % 2
            for kc in range(KC):
                P.add("pe", lambda e, kc=kc, wtile=wtile, b=b: e.matmul(
                    c.pg[b][:], wtile[:, kc, 0:128], src[:, kc, :],
                    start=(kc == 0), stop=(kc == KC - 1)),
                    r=[rw, r_src[kc]], w=[c.r_pg[b]])
            evac(i, b)

    def out_proj(key, src, r_src, gl, gi):
        def evac(m, b):
            P.add("dve", lambda e: e.tensor_copy(c.y[:, m, :], c.pg[b][:]), w=[c.r_pg[b], c.r_y[m]])
            P.add("act", lambda e: e.activation(c.sq[:, m, :], c.y[:, m, :], AF.Square),
                  r=[c.r_y[m]], w=[c.r_sq[m]])
        fm_proj(key, KC, src, r_src, evac)
        postnorm_add(gl, gi, False)

    def ffn(l, f):
        prenorm(gcol(l, 0 if f == 0 else 4, 0))
        w1v = wview(("w1", l, f), "(j p k c) -> j p k c", j=FC, p=128, k=KC, c=256)
        w2v = wview(("w2", l, f), "(m p k c) -> m p k c", m=KC, p=128, k=FC, c=128)
        rw = rwb(("w1", l, f))
        rw2 = rwb(("w2", l, f))
        for j in range(FC):
            rwj = rw
            if (l, f) == (0, 0) and c.w1first is not None:
                rwj = [rp for a, b_, rp in c.w1first if a < 128 * (j + 1) and b_ > 128 * j]
            wtile, rws = load_w1slot(lambda e, wtile, j=j: e.dma_start(out=wtile[:], in_=w1v[j]), rwj)
            b = j % 2
            for kc in range(KC):
                P.add("pe", lambda e, kc=kc, wtile=wtile, b=b: e.matmul(
                    c.pg[b][:], wtile[:, kc, 0:128], c.xn[:, kc, :],
                    start=(kc == 0), stop=(kc == KC - 1)),
                    r=[rws, c.r_xn[kc]], w=[c.r_pg[b]])
            for kc in range(KC):
                P.add("pe", lambda e, kc=kc, wtile=wtile, b=b: e.matmul(
                    c.pu[b][:], wtile[:, kc, 128:256], c.xn[:, kc, :],
                    start=(kc == 0), stop=(kc == KC - 1)),
                    r=[rws, c.r_xn[kc]], w=[c.r_pu[b]])
            P.add("act", lambda e, b=b: e.activation(c.sg[b][:], c.pg[b][:], AF.Silu),
                  w=[c.r_pg[b], c.r_sg[b]])
            P.add("dve", lambda e, j=j, b=b: e.tensor_tensor(
                c.aT[:, j, :], c.sg[b][:], c.pu[b][:], ALU.mult),
                r=[c.r_sg[b]], w=[c.r_pu[b], c.r_a[j]])
        for m in range(KC):
            wi = c.w2_i % NW2
            c.w2_i += 1
            P.add("sp", lambda e, m=m, wi=wi: e.dma_start(out=c.w2[wi][:], in_=w2v[m]),
                  r=rw2, w=[c.r_w2[wi]], dma="w2_%d" % wi)
            b = m % 2
            for j in range(FC):
                P.add("pe", lambda e, j=j, m=m, wi=wi, b=b: e.matmul(
                    c.py[b][:], c.w2[wi][:, j, :], c.aT[:, j, :],
                    start=(j == 0), stop=(j == FC - 1)),
                    r=[c.r_w2[wi], c.r_a[j]], w=[c.r_py[b]])
            P.add("dve", lambda e, m=m, b=b: e.tensor_copy(c.y[:, m, :], c.py[b][:]),
                  w=[c.r_py[b], c.r_y[m]])
            P.add("act", lambda e, m=m: e.activation(c.sq[:, m, :], c.y[:, m, :], AF.Square),
                  r=[c.r_y[m]], w=[c.r_sq[m]])
        postnorm_add(l, 1 if f == 0 else 5, True)

    def ple(l, t):
        prenorm(gcol(l, 6, 0))
        P.add("sp", lambda e: e.dma_start(out=c.pT[:], in_=pT_d[l, :, :, t * TT:(t + 1) * TT]),
              w=[c.r_pT], dma="pT")
        P.add("act", lambda e: e.activation(c.pTb[:], c.pT[:], AF.Copy), r=[c.r_pT], w=[c.r_pTb])
        wgv = wview(("pg", l), "(m p k c) -> p m k c", m=KC, p=128, k=KC, c=128)
        wpv = wview(("pp", l), "(m p k c) -> p m k c", m=KC, p=128, k=2, c=128)
        for m in range(KC):
            wtile, rws = load_w1slot(lambda e, wtile, m=m: [
                e.dma_start(out=wtile[:, :, 0:128], in_=wgv[:, m]),
                e.dma_start(out=wtile[:, 0:2, 128:256], in_=wpv[:, m])],
                rwb(("pg", l)) + rwb(("pp", l)), ndma=2)
            b = m % 2
            for kc in range(KC):
                P.add("pe", lambda e, kc=kc, wtile=wtile, b=b: e.matmul(
                    c.pg[b][:], wtile[:, kc, 0:128], c.xn[:, kc, :],
                    start=(kc == 0), stop=(kc == KC - 1)),
                    r=[rws, c.r_xn[kc]], w=[c.r_pg[b]])
            for pc in range(2):
                P.add("pe", lambda e, pc=pc, wtile=wtile, b=b: e.matmul(
                    c.pu[b][:], wtile[:, pc, 128:256], c.pTb[:, pc, :],
                    start=(pc == 0), stop=(pc == 1)),
                    r=[rws, c.r_pTb], w=[c.r_pu[b]])
            P.add("act", lambda e, b=b: e.activation(c.tmp[b][:], c.pg[b][:], AF.Sigmoid),
                  w=[c.r_pg[b], c.r_tmp[b]])
            P.add("dve", lambda e, m=m, b=b: e.tensor_tensor(
                c.y[:, m, :], c.tmp[b][:], c.pu[b][:], ALU.mult),
                r=[c.r_tmp[b]], w=[c.r_pu[b], c.r_y[m]])
            P.add("act", lambda e, m=m: e.activation(c.sq[:, m, :], c.y[:, m, :], AF.Square),
                  r=[c.r_y[m]], w=[c.r_sq[m]])
        postnorm_add(l, 7, False)

    ogT, r_ogT = c.aT[:, 0:8, :], c.r_a[0:8]
    qT, r_qT = c.aT[:, 8:12, :], c.r_a[8:12]
    kT, r_kT = c.aT[:, 12:16, :], c.r_a[12:16]
    ktm, r_ktm = c.aT[:, 16:20, :], c.r_a[16:20]
    so = c.y[:].rearrange("p (c a) t -> p c (a t)", c=4, a=2)
    mst = c.Cn[:, 4 * 257:4 * 257 + 4]
    Cn4 = c.Cn[:, 0:4 * 257].rearrange("p (h v) -> p h v", h=4, v=257)

    def smv(i):
        return c.sm[:, i, :], c.r_sm[i]

    def mlstm_proj(full):
        for ch in range(4):
            b = ch % 2
            for kc in range(KC):
                P.add("pe", lambda e, kc=kc, ch=ch, b=b: e.matmul(
                    c.pg[b][:, 0:8], c.xn[:, kc, ch * 128:(ch + 1) * 128], c.mg[:, kc, :],
                    start=(kc == 0), stop=(kc == KC - 1)),
                    r=[c.r_mg, c.r_xn[kc]], w=[c.r_pg[b]])
            P.add("dve", lambda e, ch=ch, b=b: e.tensor_tensor(
                c.gt[:, ch, :], c.pg[b][:, 0:8], c.prm[:, P_BG:P_BG + 8], ALU.add),
                r=[c.r_prm], w=[c.r_pg[b], c.r_gt[ch]])

        for ch in range(4):
            mlstm_pre(ch, full)
        if full:
            def evq(h, b):
                P.add("act", lambda e: e.activation(qT[:, h, :], c.pg[b][:], AF.Copy),
                      w=[c.r_pg[b], r_qT[h]])
            fm_proj(("mq",), 4, c.xn, c.r_xn, evq)

            def evk(h, b):
                P.add("act", lambda e: e.activation(kT[:, h, :], c.pg[b][:], AF.Copy, scale=DKS),
                      w=[c.r_pg[b], r_kT[h]])
            fm_proj(("mk",), 4, c.xn, c.r_xn, evk)
        mtv = wview(("mt",), "(g p k c) -> g p k c", g=5, p=128, k=KC, c=512)
        groups = [0, 1, 2, 3, 4] if full else [0, 1, 2]
        nb = 0
        for g in groups:
            wtile, rw = load_wtslot(mtv[g], rwb(("mt",)))
            for ch in range(4):
                b = nb % 2
                nb += 1
                for kc in range(KC):
                    P.add("pe", lambda e, kc=kc, ch=ch, wtile=wtile, b=b: e.matmul(
                        c.pu[b][:], c.xn[:, kc, ch * 128:(ch + 1) * 128], wtile[:, kc, :],
                        start=(kc == 0), stop=(kc == KC - 1)),
                        r=[rw, c.r_xn[kc]], w=[c.r_pu[b]])
                if g == 0:
                    P.add("act", lambda e, ch=ch, b=b: e.activation(ktm[:, ch, :], c.pu[b][:], AF.Copy),
                          w=[c.r_pu[b], r_ktm[ch]])
                elif g in (1, 2):
                    h0 = 2 * (g - 1)
                    P.add("dve", lambda e, ch=ch, b=b, h0=h0: e.tensor_copy(
                        c.v1[:, ch, h0:h0 + 2, 0:256],
                        c.pu[b][:].rearrange("p (h v) -> p h v", h=2, v=256)),
                        w=[c.r_pu[b], c.r_v1])
                else:
                    o0 = (g - 3) * 512
                    P.add("act", lambda e, ch=ch, b=b, o0=o0: e.activation(
                        so[:, ch, o0:o0 + 512], c.pu[b][:], AF.Sigmoid),
                        w=[c.r_pu[b], c.r_y[2 * ch], c.r_y[2 * ch + 1]])
    def mlstm_pre(ch, full):
        li = c.gt[:, ch, 0:4]
        fz = c.gt[:, ch, 4:8]
        pb = 16 + 4 * ch
        (sp_, r_sp) = smv(0)
        (a_, r_a_), (nb_, r_nb), (nbL, r_nbL), (cm, r_cm) = smv(pb), smv(pb + 1), smv(pb + 2), smv(pb + 3)
        (M_, r_M), (Ml, r_Ml), (t4, r_t4), (sc_, r_sc) = smv(4), smv(5), smv(6), smv(7)
        (dcy, r_dcy), (dn, r_dn), (lim, r_lim), (ssq, r_ssq) = smv(8), smv(9), smv(10), smv(11)
        (rr, r_rr) = smv(12)
        B0, B1, B2 = c.bank[0], c.bank[1], c.bank[2]
        rB0, rB1, rB2 = c.r_bank[0], c.r_bank[1], c.r_bank[2]
        B4, B5, rB4, rB5 = c.bank[4], c.bank[5], c.r_bank[4], c.r_bank[5]
        P.add("act", lambda e: e.activation(sp_, fz, AF.Exp, scale=-1.0), r=[c.r_gt[ch]], w=[r_sp])
        P.add("act", lambda e: e.activation(sp_, sp_, AF.Ln, bias=1.0), r=[r_sp], w=[r_sp])
        P.add("pe", lambda e: e.matmul(B4[:, 0:4], Umat, sp_, start=True, stop=True),
              r=[r_sp, c.r_cst], w=[rB4])
        P.add("pe", lambda e: e.matmul(B4[:, 4:8], onesf, sp_, start=True, stop=True),
              r=[r_sp, c.r_cst], w=[rB4])
        P.add("dve", lambda e: e.tensor_tensor(a_, B4[:, 0:4], li, ALU.add), r=[c.r_gt[ch]], w=[rB4, r_a_])
        P.add("dve", lambda e: e.tensor_copy(nb_, B4[:, 0:4]), w=[rB4, r_nb])
        P.add("dve", lambda e: e.tensor_copy(nbL, B4[:, 4:8]), w=[rB4, r_nbL])
        for h in range(4):
            P.add("dve", lambda e, h=h: e.tensor_scalar(c.dg[:, h, :], ident, a_[:, h:h + 1], None, ALU.mult),
                  r=[c.r_cst, r_a_], w=[c.r_dg])
        P.add("pe", lambda e: e.matmul(B5[:], onesf, c.dg[:].rearrange("p h s -> p (h s)"),
                                       start=True, stop=True), r=[c.r_dg, c.r_cst], w=[rB5])
        if full:
            P.add("dve", lambda e: e.tensor_tensor(c.e1[:].rearrange("p h s -> p (h s)"), B5[:],
                                                   cs(C_MC4, 512), ALU.add), r=[c.r_cst], w=[rB5, c.r_e1])
            P.add("dve", lambda e: e.tensor_reduce(cm, c.e1[:], AX.X, ALU.max), r=[c.r_e1], w=[r_cm])
        else:
            P.add("dve", lambda e: e.tensor_reduce(cm, B5[:].rearrange("p (h s) -> p h s", h=4, s=128),
                                                   AX.X, ALU.max), w=[rB5, r_cm])

    def mlstm_mid(ch):
        li = c.gt[:, ch, 0:4]
        fz = c.gt[:, ch, 4:8]
        pb = 16 + 4 * ch
        (sp_, r_sp) = smv(0)
        (a_, r_a_), (nb_, r_nb), (nbL, r_nbL), (cm, r_cm) = smv(pb), smv(pb + 1), smv(pb + 2), smv(pb + 3)
        (M_, r_M), (Ml, r_Ml), (t4, r_t4), (sc_, r_sc) = smv(4), smv(5), smv(6), smv(7)
        (dcy, r_dcy), (dn, r_dn), (lim, r_lim), (ssq, r_ssq) = smv(8), smv(9), smv(10), smv(11)
        (rr, r_rr) = smv(12)
        B0, B1, B2 = c.bank[0], c.bank[1], c.bank[2]
        rB0, rB1, rB2 = c.r_bank[0], c.r_bank[1], c.r_bank[2]
        P.add("dve", lambda e: e.tensor_tensor(M_, cm, mst, ALU.max), r=[r_cm, c.r_Cn], w=[r_M])
        for h in range(4):
            P.add("dve", lambda e, h=h: e.tensor_scalar(c.dg[:, h, :], ident, M_[:, h:h + 1], None, ALU.mult),
                  r=[c.r_cst, r_M], w=[c.r_dg])
        P.add("pe", lambda e: e.matmul(B2[:], onesf, c.dg[:].rearrange("p h s -> p (h s)"),
                                       start=True, stop=True), r=[c.r_dg, c.r_cst], w=[rB2])
        B2v = B2[:].rearrange("p (h t) -> p h t", h=4, t=128)
        P.add("dve", lambda e: e.tensor_tensor(c.e1[:].rearrange("p h s -> p (h s)"), B2[:],
                                               cs(C_MT4, 512), ALU.add), r=[c.r_cst], w=[rB2, c.r_e1])
        for h in range(4):
            P.add("act", lambda e, h=h: e.activation(c.wT[:, h, :], c.e1[:, h, :], AF.Exp,
                                                     bias=a_[:, h:h + 1], scale=-1.0),
                  r=[c.r_e1, r_a_], w=[c.r_wT])
        for h in range(4):
            P.add("act", lambda e, h=h: e.activation(c.sib[:, h, :], B2v[:, h, :], AF.Exp,
                                                     bias=mst[:, h:h + 1], scale=-1.0),
                  r=[c.r_Cn], w=[rB2, c.r_sib])
        P.add("dve", lambda e: e.tensor_copy(Ml, B2v[:, :, 127]), w=[rB2, r_Ml])
        P.add("dve", lambda e: e.tensor_tensor(c.qs[:], qT[:, :, ch * 128:(ch + 1) * 128], c.sib[:], ALU.mult),
              r=r_qT + [c.r_sib], w=[c.r_qs])
        for h in range(4):
            P.add("pe", lambda e, h=h: e.matmul(B0[:, h * 128:(h + 1) * 128],
                                                kT[:, h, ch * 128:(ch + 1) * 128],
                                                qT[:, h, ch * 128:(ch + 1) * 128], start=True, stop=True),
                  r=r_kT + r_qT, w=[rB0])
        P.add("dve", lambda e: e.tensor_tensor(c.qkw[:].rearrange("p h t -> p (h t)"), B0[:],
                                               c.wT[:].rearrange("p h t -> p (h t)"), ALU.mult),
              r=[c.r_wT], w=[rB0, c.r_qkw])
        for h in range(4):
            bk, rbk = c.bank[3 + h], c.r_bank[3 + h]
            P.add("pe", lambda e, h=h, bk=bk: e.matmul(bk[:, 0:257], c.qkw[:, h, :], c.v1[:, ch, h, :],
                                                       start=True, stop=False),
                  r=[c.r_qkw, c.r_v1], w=[rbk])
            P.add("pe", lambda e, h=h, bk=bk: e.matmul(bk[:, 0:257], c.qs[:, h, :], c.Cnb[:, h, :],
                                                       start=False, stop=True),
                  r=[c.r_qs, c.r_Cnb], w=[rbk])

    def mlstm_mid_b(ch):
        li = c.gt[:, ch, 0:4]
        fz = c.gt[:, ch, 4:8]
        pb = 16 + 4 * ch
        (sp_, r_sp) = smv(0)
        (a_, r_a_), (nb_, r_nb), (nbL, r_nbL), (cm, r_cm) = smv(pb), smv(pb + 1), smv(pb + 2), smv(pb + 3)
        (M_, r_M), (Ml, r_Ml), (t4, r_t4), (sc_, r_sc) = smv(4), smv(5), smv(6), smv(7)
        (dcy, r_dcy), (dn, r_dn), (lim, r_lim), (ssq, r_ssq) = smv(8), smv(9), smv(10), smv(11)
        (rr, r_rr) = smv(12)
        B0, B1, B2 = c.bank[0], c.bank[1], c.bank[2]
        rB0, rB1, rB2 = c.r_bank[0], c.r_bank[1], c.r_bank[2]
        P.add("dve", lambda e: e.tensor_tensor(t4, nb_, M_, ALU.subtract), r=[r_nb, r_M], w=[r_t4])
        P.add("act", lambda e: e.activation(lim, t4, AF.Exp), r=[r_t4], w=[r_lim])
        for h in range(4):
            bk, rbk = c.bank[3 + h], c.r_bank[3 + h]
            P.add("dve", lambda e, h=h, bk=bk: e.tensor_copy(dn[:, h:h + 1], bk[:, 256:257]),
                  w=[rbk, r_dn])
        P.add("dve", lambda e: e.scalar_tensor_tensor(t4, dn, -1.0, dn, ALU.mult, ALU.max),
              r=[r_dn], w=[r_t4])
        P.add("dve", lambda e: e.tensor_tensor(t4, t4, lim, ALU.max), r=[r_t4, r_lim], w=[r_t4])
        P.add("dve", lambda e: e.reciprocal(dn, t4), r=[r_t4], w=[r_dn])
        for h in range(4):
            bk, rbk = c.bank[3 + h], c.r_bank[3 + h]
            P.add("dve", lambda e, h=h, bk=bk: e.tensor_scalar(c.hh[:, h, :], bk[:, 0:256],
                                                               dn[:, h:h + 1], None, ALU.mult),
                  r=[r_dn], w=[rbk, c.r_hh[h]])
            P.add("act", lambda e, h=h: e.activation(c.junk[:], c.hh[:, h, :], AF.Square,
                                                     accum_out=ssq[:, h:h + 1]),
                  r=[c.r_hh[h]], w=[c.r_junk, r_ssq])

    def mlstm_fin(ch):
        li = c.gt[:, ch, 0:4]
        fz = c.gt[:, ch, 4:8]
        pb = 16 + 4 * ch
        (sp_, r_sp) = smv(0)
        (a_, r_a_), (nb_, r_nb), (nbL, r_nbL), (cm, r_cm) = smv(pb), smv(pb + 1), smv(pb + 2), smv(pb + 3)
        (M_, r_M), (Ml, r_Ml), (t4, r_t4), (sc_, r_sc) = smv(4), smv(5), smv(6), smv(7)
        (dcy, r_dcy), (dn, r_dn), (lim, r_lim), (ssq, r_ssq) = smv(8), smv(9), smv(10), smv(11)
        (rr, r_rr) = smv(12)
        B0, B1, B2 = c.bank[0], c.bank[1], c.bank[2]
        rB0, rB1, rB2 = c.r_bank[0], c.r_bank[1], c.r_bank[2]
        P.add("act", lambda e: e.activation(rr, ssq, AF.Ln, bias=c.epsc[:, 0:1], scale=1.0 / 256),
              r=[r_ssq, c.r_ones], w=[r_rr])
        P.add("act", lambda e: e.activation(rr, rr, AF.Exp, scale=-0.5), r=[r_rr], w=[r_rr])
        for h in range(4):
            P.add("dve", lambda e, h=h: e.scalar_tensor_tensor(
                c.hh[:, h, :], c.hh[:, h, :], rr[:, h:h + 1],
                c.prm[:, P_HN + h * 256:P_HN + (h + 1) * 256], ALU.mult, ALU.mult),
                r=[c.r_hh[h], r_rr, c.r_prm], w=[c.r_hh[h]])
        P.add("dve", lambda e: e.tensor_tensor(c.og[:], c.hh[:].rearrange("p h v -> p (h v)"),
                                               so[:, ch, :], ALU.mult),
              r=c.r_hh + [c.r_y[2 * ch], c.r_y[2 * ch + 1]], w=[c.r_og])
        B7b = c.bank[7][:].bitcast(BF16)
        for kc in range(KC):
            P.add("pe", lambda e, kc=kc: e.transpose(B7b[:, kc * 128:(kc + 1) * 128],
                                                     c.og[:, kc * 128:(kc + 1) * 128], c.identb[:]),
                  r=[c.r_og, c.r_ones], w=[c.r_bank[7]])
        P.add("act", lambda e: e.activation(ogT[:, :, ch * 128:(ch + 1) * 128],
                                            B7b[:, 0:1024].rearrange("p (k t) -> p k t", k=8, t=128), AF.Copy),
              w=[c.r_bank[7]] + r_ogT)

    def mlstm_upd(ch, full):
        li = c.gt[:, ch, 0:4]
        fz = c.gt[:, ch, 4:8]
        pb = 16 + 4 * ch
        (sp_, r_sp) = smv(0)
        (a_, r_a_), (nb_, r_nb), (nbL, r_nbL), (cm, r_cm) = smv(pb), smv(pb + 1), smv(pb + 2), smv(pb + 3)
        (M_, r_M), (Ml, r_Ml), (t4, r_t4), (sc_, r_sc) = smv(4), smv(5), smv(6), smv(7)
        (dcy, r_dcy), (dn, r_dn), (lim, r_lim), (ssq, r_ssq) = smv(8), smv(9), smv(10), smv(11)
        (rr, r_rr) = smv(12)
        B0, B1, B2 = c.bank[0], c.bank[1], c.bank[2]
        rB0, rB1, rB2 = c.r_bank[0], c.r_bank[1], c.r_bank[2]
        if not full:
            P.add("dve", lambda e: e.tensor_tensor(Ml, cm, mst, ALU.max), r=[r_cm, c.r_Cn], w=[r_Ml])
        P.add("dve", lambda e: e.tensor_tensor(t4, a_, Ml, ALU.subtract), r=[r_a_, r_Ml], w=[r_t4])
        P.add("act", lambda e: e.activation(sc_, t4, AF.Exp), r=[r_t4], w=[r_sc])
        P.add("dve", lambda e: e.tensor_tensor(dcy, mst, Ml, ALU.subtract), r=[c.r_Cn, r_Ml], w=[r_dcy])
        P.add("act", lambda e: e.activation(dcy, dcy, AF.Exp), r=[r_dcy], w=[r_dcy])
        for h in range(4):
            P.add("dve", lambda e, h=h: e.tensor_scalar(c.ksc[:, h, :], ktm[:, ch, h * 128:(h + 1) * 128],
                                                        sc_[:, h:h + 1], DKS, ALU.mult, ALU.mult),
                  r=r_ktm + [r_sc], w=[c.r_ksc])
        for h in range(4):
            bk, rbk = c.bank[3 + h], c.r_bank[3 + h]
            P.add("pe", lambda e, h=h, bk=bk: e.matmul(bk[:, 0:257], c.ksc[:, h, :], c.v1[:, ch, h, :],
                                                       start=True, stop=True),
                  r=[c.r_ksc, c.r_v1], w=[rbk])
            P.add("dve", lambda e, h=h, bk=bk: e.scalar_tensor_tensor(
                Cn4[:, h, :], Cn4[:, h, :], dcy[:, h:h + 1], bk[:, 0:257], ALU.mult, ALU.add),
                r=[r_dcy, c.r_Cn], w=[rbk, c.r_Cn])
        P.add("dve", lambda e: e.tensor_tensor(mst, Ml, nbL, ALU.subtract), r=[r_Ml, r_nbL, c.r_Cn], w=[c.r_Cn])
        if full:
            P.add("act", lambda e: e.activation(c.Cnb[:], Cn4, AF.Copy), r=[c.r_Cn], w=[c.r_Cnb])


    def mlstm(full):
        prenorm(gcol(0, 2, 0))
        mlstm_proj(full)
        for ch in range(4):
            if full:
                mlstm_mid(ch)
                if ch > 0:
                    mlstm_fin(ch - 1)
                mlstm_mid_b(ch)
            mlstm_upd(ch, full)
            if c.drip_n and ch in (1, 3):
                drip(n=1, after=[c.r_ksc])
        if full:
            mlstm_fin(3)
            out_proj(("mo",), ogT, r_ogT, 0, 3)

    def mlstm_state_init_zero():
        P.add("dve", lambda e: e.memset(c.Cn[:], 0.0), w=[c.r_Cn])
        P.add("dve", lambda e: e.memset(mst, -1e30), w=[c.r_Cn])
        P.add("sp", lambda e: e.dma_start(out=c.mg[:], in_=wview(("mg",), "(p k c) -> p k c", p=128, k=KC, c=8)),
              r=rwb(("mg",)), w=[c.r_mg], dma="mg")

    def mlstm_exchange():
        P.add("sp", lambda e: e.dma_start(out=st_own.ap(), in_=c.Cn[:]), r=[c.r_Cn], w=[c.r_st_own], dma="st")
        P.add("pool", lambda e: e.collective_compute("AllGather", ALU.bypass, replica_groups=PAIRS,
                                                     ins=[st_own.ap().opt()], outs=[st_all.ap().opt()]),
              r=[c.r_st_own], w=[c.r_st_all], name="cc", dma="cc", inc=1)
        P.add("sp", lambda e: e.dma_start(out=c.Cn[:], in_=st_all.ap()[0:128, :]), r=[c.r_st_all], w=[c.r_Cn], dma="st")
        sel = c.prm[:, P_SEL:P_SEL + 1]
        P.add("dve", lambda e: e.tensor_scalar(c.Cn[:], c.Cn[:], sel, None, ALU.mult), r=[c.r_prm, c.r_Cn], w=[c.r_Cn])
        t4, r_t4 = smv(6)
        P.add("dve", lambda e: e.tensor_scalar(t4[:, 0:1], sel, -1.0, 1e30, ALU.add, ALU.mult), r=[c.r_prm], w=[r_t4])
        P.add("dve", lambda e: e.tensor_scalar(mst, mst, t4[:, 0:1], None, ALU.add), r=[r_t4, c.r_Cn], w=[c.r_Cn])
        P.add("act", lambda e: e.activation(c.Cnb[:], Cn4, AF.Copy), r=[c.r_Cn], w=[c.r_Cnb])

    kst, r_kst = c.aT[:, 0:8, :], c.r_a[0:8]
    vst, r_vst = c.aT[:, 8:16, :], c.r_a[8:16]
    vst4 = c.aT[:, 8:16, :].rearrange("p (c a) t -> p c (a t)", c=4, a=2)

    def kvproj(t):
        prenorm(2 * 8 * KC)
        def evk(h, b):
            P.add("act", lambda e: e.activation(kst[:, h, :], c.pg[b][:], AF.Copy), w=[c.r_pg[b], r_kst[h]])
        fm_proj(("kk",), KC, c.xn, c.r_xn, evk)
        kvv = wview(("kv",), "(g p k c) -> g p k c", g=2, p=128, k=KC, c=512)
        nb = 0
        for g in range(2):
            wtile, rw = load_wtslot(kvv[g], rwb(("kv",)))
            for ch in range(4):
                b = nb % 2
                nb += 1
                for kc in range(KC):
                    P.add("pe", lambda e, kc=kc, ch=ch, wtile=wtile, b=b: e.matmul(
                        c.pu[b][:], c.xn[:, kc, ch * 128:(ch + 1) * 128], wtile[:, kc, :],
                        start=(kc == 0), stop=(kc == KC - 1)),
                        r=[rw, c.r_xn[kc]], w=[c.r_pu[b]])
                P.add("act", lambda e, ch=ch, b=b, g=g: e.activation(
                    vst4[:, ch, g * 512:(g + 1) * 512], c.pu[b][:], AF.Copy),
                    w=[c.r_pu[b], r_vst[2 * ch], r_vst[2 * ch + 1]])
        ko = kv_own[t].ap()
        P.add("act", lambda e: [
            e.dma_start(out=ko[0:1024, :].rearrange("(h p) t -> p h t", h=8, p=128), in_=kst),
            e.dma_start(out=ko[1024:2048, :].rearrange("(c p a) t -> p c (a t)", c=4, p=128, a=2), in_=vst4)],
            r=r_kst + r_vst, w=[c.r_kv_own[t]], dma="kvst", ndma=2)
        if NOCC:
            P.add("sp", lambda e: e.dma_start(out=kv_all[t].ap()[0:2048, :], in_=kv_own[t].ap()),
                  r=[c.r_kv_own[t]], w=[c.r_kv_all[t]], dma="kvcp")
        else:
            P.add("pool", lambda e: e.collective_compute("AllGather", ALU.bypass, replica_groups=PAIRS,
                                                         ins=[kv_own[t].ap().opt()], outs=[kv_all[t].ap().opt()]),
                  r=[c.r_kv_own[t]], w=[c.r_kv_all[t]], name="cc", dma="cc", inc=1)

    aqT, r_aqT = c.aT[:, 0:8, :], c.r_a[0:8]
    aoT, r_aoT = c.aT[:, 8:16, :], c.r_a[8:16]

    def attn_setup():
        lam = c.prm[:, P_LAM:P_LAM + 256]
        P.add("dve", lambda e: e.tensor_tensor(c.junk[:, 0:64], lam[:, 0:64], lam[:, 64:128], ALU.mult),
              r=[c.r_prm], w=[c.r_junk])
        P.add("dve", lambda e: e.tensor_tensor(c.junk[:, 64:128], lam[:, 128:192], lam[:, 192:256], ALU.mult),
              r=[c.r_prm], w=[c.r_junk])
        P.add("dve", lambda e: e.tensor_reduce(c.lamc[:, 0:2], c.junk[:, 0:128].rearrange("p (a d) -> p a d", a=2, d=64),
                                               AX.X, ALU.add), r=[c.r_junk], w=[c.r_lam])
        P.add("act", lambda e: e.activation(c.lamc[:, 0:2], c.lamc[:, 0:2], AF.Exp), r=[c.r_lam], w=[c.r_lam])
        P.add("dve", lambda e: e.scalar_tensor_tensor(c.lamc[:, 2:3], c.lamc[:, 0:1], LAM_INIT, c.lamc[:, 1:2],
                                                      ALU.add, ALU.subtract), r=[c.r_lam], w=[c.r_lam])
        P.add("dve", lambda e: e.tensor_scalar(c.lamc[:, 3:4], c.prm[:, P_SUBLN:P_SUBLN + 1], 1.0 - LAM_INIT, None,
                                               ALU.mult), r=[c.r_prm, c.r_lam], w=[c.r_lam])

    def attn(t):
        prenorm(gcol(1, 2, 0))

        def evq(h, b):
            P.add("act", lambda e: e.activation(aqT[:, h, :], c.pg[b][:], AF.Copy, scale=0.125),
                  w=[c.r_pg[b], r_aqT[h]])
        fm_proj(("dq",), KC, c.xn, c.r_xn, evq)
        chunks = []
        na = min(NT, 8)
        for ci in range(0, na, 4):
            chunks.append(("A", [(kv_all[tt], tt) for tt in range(ci, min(ci + 4, na))]))
        own = list(range(t + 1))
        for ci in range(0, len(own), 4):
            chunks.append(("B", [(kv_own[tt], tt) for tt in own[ci:ci + 4]]))
        nkb_total = sum(4 * len(srcs) for _, srcs in chunks)
        SA, SB = [c.bank[0], c.bank[2]], [c.bank[1], c.bank[3]]
        rSA, rSB = [c.r_bank[0], c.r_bank[2]], [c.r_bank[1], c.r_bank[3]]
        O = [c.bank[4], c.bank[5]]
        rO = [c.r_bank[4], c.r_bank[5]]
        SM = [c.bank[6], c.bank[7]]
        rSM = [c.r_bank[6], c.r_bank[7]]
        loads = []
        iters = []
        for h in range(8):
            kbi = 0
            for reg, srcs in chunks:
                li_ = len(loads)
                loads.append((h, reg, srcs))
                for n, (dt_, tt) in enumerate(srcs):
                    for j4 in range(4):
                        iters.append(dict(h=h, load=li_, kblk=n * 4 + j4, j4=j4, reg=reg,
                                          diag=(reg == "B" and tt == t),
                                          first=(kbi == 0), last=(kbi == nkb_total - 1)))
                        kbi += 1
        slot_of = {}

        def emit_load(li_):
            h, reg, srcs = loads[li_]
            si = c.kb_i % 2
            c.kb_i += 1
            slot_of[li_] = si
            kbuf, vbuf = c.kb[si], c.vb[si]

            def dmas(e):
                ins = []
                for n, (dt_, tt) in enumerate(srcs):
                    a = dt_.ap()
                    ins.append(e.dma_start(out=kbuf[:, n * 512:(n + 1) * 512], in_=a[h * 128:(h + 1) * 128, :]))
                    vsrc = a[1024:2048, :].rearrange("(c p a) t -> p c (a t)", c=4, p=128, a=2)
                    ins.append(e.dma_start(out=vbuf[:, 4 * n:4 * n + 4, :], in_=vsrc[:, :, h * 128:(h + 1) * 128]))
                return ins
            rsrc = [c.r_kv_all[tt] if reg == "A" else c.r_kv_own[tt] for _, tt in srcs]
            P.add("sp", dmas, r=rsrc, w=[c.r_kb[si], c.r_vb[si]], dma="kvld_%d" % si, ndma=2 * len(srcs))

        def emit_scores(i, d):
            si = slot_of[d["load"]]
            kbuf = c.kb[si]
            b = i % 2
            h, kblk, j4 = d["h"], d["kblk"], d["j4"]
            dg_ = d["diag"]
            for m, (S_, rS) in enumerate(((SA, rSA), (SB, rSB))):
                p0 = 64 * m
                P.add("pe", lambda e, S_=S_, p0=p0: e.matmul(
                    S_[b][:], kbuf[p0:p0 + 64, kblk * 128:(kblk + 1) * 128], aqT[p0:p0 + 64, h, :],
                    start=True, stop=not dg_),
                    r=[c.r_kb[si], r_aqT[h]], w=[rS[b]])
            if dg_:
                for m, (S_, rS) in enumerate(((SA, rSA), (SB, rSB))):
                    P.add("pe", lambda e, S_=S_: e.matmul(
                        S_[b][:], c.identb[:], c.maskb[:, 512 * j4:512 * (j4 + 1)], start=False, stop=True),
                        r=[c.r_cst, c.r_ones], w=[rS[b]])
            spair = c.pairs[b][:, 0:1024]
            ppair = c.ptpair[b]
            wres = [rSA[b], rSB[b], c.r_pt[0][b], c.r_pt[1][b]]
            if d["reg"] == "A":
                P.add("act", lambda e: e.activation(ppair[:], spair, AF.Exp, bias=c.prm[:, P_BIASA:P_BIASA + 1]),
                      r=[c.r_prm], w=wres)
            else:
                P.add("act", lambda e: e.activation(ppair[:], spair, AF.Exp), w=wres)

        def emit_av(i, d):
            si = slot_of[d["load"]]
            vbuf = c.vb[si]
            b = i % 2
            kblk, first, last = d["kblk"], d["first"], d["last"]
            for m in range(2):
                ptile, rpt = c.pt[m][b], c.r_pt[m][b]
                P.add("pe", lambda e, m=m, ptile=ptile: e.matmul(
                    O[m][:], vbuf[:, kblk, :], ptile[:], start=first, stop=last),
                    r=[c.r_vb[si], rpt], w=[rO[m]])
                if m == 0:
                    P.add("pe", lambda e, m=m, ptile=ptile: e.matmul(
                        SM[m][:], c.one1[:], ptile[:], start=first, stop=last),
                        r=[c.r_ones, rpt], w=[rSM[m]])
                elif first:
                    P.add("dve", lambda e, ptile=ptile: e.tensor_copy(c.rstd[:], ptile[:]),
                          r=[rpt], w=[c.r_rstd])
                else:
                    P.add("dve", lambda e, ptile=ptile: e.tensor_tensor(c.rstd[:], c.rstd[:], ptile[:], ALU.add),
                          r=[rpt], w=[c.r_rstd])

        def emit_combine(h):
            P.add("pe", lambda e: e.matmul(SM[1][:], onesf, c.rstd[:], start=True, stop=True),
                  r=[c.r_rstd, c.r_cst], w=[rSM[1]])
            P.add("dve", lambda e: e.tensor_copy(c.ms[0][:], SM[0][:]), w=[rSM[0], c.r_ms[0]])
            P.add("dve", lambda e: e.tensor_copy(c.ms[1][:], SM[1][:]), w=[rSM[1], c.r_ms[1]])
            P.add("dve", lambda e: e.tensor_copy(c.tmp[0][:], O[0][:]), w=[rO[0], c.r_tmp[0]])
            P.add("dve", lambda e: e.tensor_copy(c.tmp[1][:], O[1][:]), w=[rO[1], c.r_tmp[1]])
            P.add("act", lambda e: e.activation(c.msall[:], c.msall[:], AF.Ln),
                  r=[c.r_ms[0], c.r_ms[1]], w=[c.r_ms[0], c.r_ms[1]])
            P.add("act", lambda e: e.activation(c.msall[:], c.msall[:], AF.Exp, scale=-1.0),
                  r=[c.r_ms[0], c.r_ms[1]], w=[c.r_ms[0], c.r_ms[1]])
            P.add("dve", lambda e: e.tensor_tensor(c.tmp[0][:], c.tmp[0][:], c.ms[0][:], ALU.mult),
                  r=[c.r_ms[0], c.r_tmp[0]], w=[c.r_tmp[0]])
            P.add("dve", lambda e: e.scalar_tensor_tensor(c.tmp[1][:], c.tmp[1][:], c.lamc[:, 2:3], c.ms[1][:],
                                                          ALU.mult, ALU.mult),
                  r=[c.r_ms[1], c.r_tmp[1], c.r_lam], w=[c.r_tmp[1]])
            P.add("dve", lambda e: e.tensor_tensor(c.y[:, h, :], c.tmp[0][:], c.tmp[1][:], ALU.subtract),
                  r=[c.r_tmp[0], c.r_tmp[1]], w=[c.r_y[h]])

        def emit_subln():
            bufs = [(c.tmp[0], c.r_tmp[0]), (c.tmp[1], c.r_tmp[1]), (c.ms[0], c.r_ms[0]), (c.ms[1], c.r_ms[1])]
            for g in range(2):
                hs = range(4 * g, 4 * g + 4)
                for h in hs:
                    P.add("act", lambda e, h=h: e.activation(c.sq[:, h, :], c.y[:, h, :], AF.Square),
                          r=[c.r_y[h]], w=[c.r_sq[h]])
                for i, h in enumerate(hs):
                    P.add("pe", lambda e, h=h, i=i: e.matmul(c.bank[i][:], c.o128[:], c.sq[:, h, :],
                                                             start=True, stop=True),
                          r=[c.r_sq[h], c.r_ones], w=[c.r_bank[i]])
                for i, h in enumerate(hs):
                    rsqrt_bank(i, bufs[i][0][:], bufs[i][1])
                for i, h in enumerate(hs):
                    P.add("dve", lambda e, h=h, i=i: e.scalar_tensor_tensor(
                        aoT[:, h, :], c.y[:, h, :], c.lamc[:, 3:4], bufs[i][0][:], ALU.mult, ALU.mult),
                        r=[c.r_y[h], bufs[i][1], c.r_lam], w=[r_aoT[h]])

        nI = len(iters)
        emit_load(0)
        for i in range(nI + 1):
            if i < nI:
                d = iters[i]
                emit_scores(i, d)
            if i >= 1:
                dp = iters[i - 1]
                emit_av(i - 1, dp)
                if dp["last"]:
                    emit_combine(dp["h"])
                if i < nI and iters[i]["load"] != dp["load"] and iters[i]["load"] + 1 < len(loads):
                    emit_load(iters[i]["load"] + 1)
            elif len(loads) > 1:
                emit_load(1)
        emit_subln()
        out_proj(("do",), aoT, r_aoT, 1, 3)

    def load_x(src, ts, rs=(), buf=0):
        xT, r_x = c.xTs[buf], c.r_xs2[buf]
        P.add("sp", lambda e: e.dma_start(out=xT[:], in_=src[:, :, ts]), r=list(rs), w=r_x, dma="xin%d" % buf)

    def use_x(buf):
        c.xT, c.r_x = c.xTs[buf], c.r_xs2[buf]

    def store_x(dst, ts, rdst):
        xT, r_x = c.xT, c.r_x
        P.add("act", lambda e: e.dma_start(out=dst[:, :, ts], in_=xT[:]), r=r_x, w=[rdst], dma="xout")

    def tile_loop(src, rs, body, late_prefetch=False):
        load_x(src, slice(0, TT), rs, 0)
        for t in range(NT):
            ts = slice(t * TT, (t + 1) * TT)
            def pf(t=t):
                if t + 1 < NT:
                    load_x(src, slice((t + 1) * TT, (t + 2) * TT), rs, (t + 1) % 2)
            if not late_prefetch:
                pf()
            use_x(t % 2)
            if late_prefetch:
                body(t, ts, pf)
            else:
                body(t, ts)

    def run_stage(st, t):
        if st[0] == "ffn":
            ffn(st[1], st[2])
        elif st[0] == "ple":
            ple(st[1], t)
        elif st[0] == "mlstm":
            mlstm(st[1])
        elif st[0] == "kvproj":
            kvproj(t)
        elif st[0] == "attn":
            attn(t)

    if phases is None:
        drip(upto={k_ for k_, _, _, _ in cast_chunks})
        flat = [s_ for p_ in stages for s_ in (p_ if isinstance(p_, list) else [p_])]
        names = [s_[0] for s_ in flat]
        if "mlstm" in names:
            mlstm_state_init_zero()
            if any(s_[0] == "mlstm" and s_[1] for s_ in flat):
                P.add("act", lambda e: e.activation(c.Cnb[:], Cn4, AF.Copy), r=[c.r_Cn], w=[c.r_Cnb])
        if "attn" in names:
            attn_setup()
        passes = stages if (stages and isinstance(stages[0], list)) else [stages]
        for pi, pst in enumerate(passes):
            def body(t, ts, pst=pst, pi=pi):
                for st in pst:
                    run_stage(st, t)
                if pi == len(passes) - 1:
                    store_x(out_d, ts, c.r_out)
            tile_loop(xT_d, [], body)
            P.barrier()
    else:
        mlstm_state_init_zero()
        attn_setup()

        def body_a(t, ts):
            ffn(0, 0)
            store_x(xs_d, ts, c.r_xs)
            mlstm(False)
        c.drip_n, c.drip_every = 1, 8
        tile_loop(xT_d, [], body_a)
        drip(upto=PH_B)
        mlstm_exchange()

        def body_b(t, ts):
            mlstm(True)
            ffn(0, 1)
            ple(0, t)
            store_x(xs_d, ts, c.r_xs)
            kvproj(t)
        c.drip_n, c.drip_every = 1, 8
        tile_loop(xs_d, [c.r_xs], body_b)
        drip(upto={k_ for k_, _, _, _ in cast_chunks})
        c.drip_n = 0
        P.barrier()

        def body_c(t, ts, pf):
            ffn(1, 0)
            pf()
            attn(t)
            ffn(1, 1)
            ple(1, t)
            store_x(out_d, ts, c.r_out)
        tile_loop(xs_d, [c.r_xs], body_c, late_prefetch=True)
    P.add("act", None, r=[c.r_out])
    P.emit()
    P.close()
    return nc


def to_fm(a):
    t, ch = a.shape
    return np.ascontiguousarray(a.reshape(t, ch // 128, 128).transpose(2, 1, 0))


def from_fm(a):
    p, k, t = a.shape
    return a.transpose(2, 1, 0).reshape(t, k * p)


def make_in_maps(inp, x_override=None, ncores=NCORES):
    x = inp["x"] if x_override is None else x_override
    wall = pack_weights(inp).reshape(-1, ROW)
    gc = gcols_host(inp)
    cst = consts_host()
    maps = []
    for core in range(ncores):
        b, h = core // 2, core % 2
        sl = slice(h * TOK, (h + 1) * TOK)
        maps.append({
            "xT": to_fm(np.asarray(x[b, sl])),
            "pT": np.stack([to_fm(np.asarray(inp["p"][l, b, sl])) for l in range(2)]),
            "gcols": gc,
            "cst": cst,
            "prm": prm_host(inp, core),
            "wall": wall,
        })
    return maps


def gather_out(results):
    out = np.zeros((4, 8192, D), np.float32)
    for core in range(len(results)):
        b, h = core // 2, core % 2
        out[b, h * TOK:(h + 1) * TOK] = from_fm(np.asarray(results[core]["outT"]).reshape(128, KC, TOK))
    return out


def kernel(**inputs):
    inp = {k: np.asarray(v) for k, v in inputs.items()}
    nc = build(None, phases=True)
    res = run_bass_kernel_spmd(nc, make_in_maps(inp), core_ids=list(range(NCORES)))
    return gather_out(res.results)
```
